# Optimizing a Trainium2 kernel written in Bass

```python
import math
import jax, jax.numpy as jnp
from jax import lax
import numpy as np


D_MODEL = 1024
BATCH = 4
SEQ = 4096
DEPTH = 1

GRID_W = 64
CTX_LEN = 256
D_MIX = D_MODEL
D_FOURIER = D_MIX // 4
D_DIFF = D_MIX - D_FOURIER
DIFF_HEAD_DIM = 64
N_DIFF_HEADS = D_DIFF // (2 * DIFF_HEAD_DIM)
N_FOURIER_GROUPS = 4
FOURIER_GROUP_DIM = D_FOURIER // N_FOURIER_GROUPS
D_IN_PROJ = 3 * D_DIFF + D_FOURIER
D_FF = 2816
N_MOD = 9
ROPE_BASE = 10000.0
ROPE_PAIRS = DIFF_HEAD_DIM // 4
Q_BLOCK = 128
RMS_EPS = 1e-6
ATTN_SCALE = DIFF_HEAD_DIM ** -0.5

kernel_name = 'hymba_diff_fnet_macaron_dit_block'


def _rmsnorm(x, g):
    x32 = x.astype(jnp.float32)
    y = x32 * lax.rsqrt(jnp.mean(x32 * x32, axis=-1, keepdims=True) + RMS_EPS)
    return y.astype(x.dtype) * g


def _modulate(h, shift, scale):
    return h * (1 + scale[:, None, :]) + shift[:, None, :]


def _half_ffn(s, shift, scale, gate, g, w_gate, w_up, w_down):
    h = _modulate(_rmsnorm(s, g), shift, scale)
    return s + 0.5 * gate[:, None, :] * ((jax.nn.silu(h @ w_gate) * (h @ w_up)) @ w_down)


def _rope_tables(n_tokens):
    rows = n_tokens // GRID_W
    row = jnp.repeat(jnp.arange(rows, dtype=jnp.float32), GRID_W)
    col = jnp.tile(jnp.arange(GRID_W, dtype=jnp.float32), rows)
    inv_freq = ROPE_BASE ** (-jnp.arange(ROPE_PAIRS, dtype=jnp.float32) / ROPE_PAIRS)
    ang = jnp.stack([row[:, None] * inv_freq, col[:, None] * inv_freq], axis=1)
    return jnp.cos(ang), jnp.sin(ang)


def _apply_rope(x, cos, sin):
    xs = x.astype(jnp.float32).reshape(x.shape[:-1] + (2, 2, ROPE_PAIRS))
    x1, x2 = xs[..., 0, :], xs[..., 1, :]
    out = jnp.stack([x1 * cos - x2 * sin, x2 * cos + x1 * sin], axis=-2)
    return out.reshape(x.shape).astype(x.dtype)


def _split_proj(p):
    b, n, _ = p.shape
    q = p[..., :D_DIFF].reshape(b, n, N_DIFF_HEADS, 2, DIFF_HEAD_DIM).transpose(0, 2, 3, 1, 4)
    k = p[..., D_DIFF:2 * D_DIFF].reshape(b, n, N_DIFF_HEADS, 2, DIFF_HEAD_DIM).transpose(0, 2, 3, 1, 4)
    v = p[..., 2 * D_DIFF:3 * D_DIFF].reshape(b, n, N_DIFF_HEADS, 2 * DIFF_HEAD_DIM).transpose(0, 2, 1, 3)
    f = p[..., 3 * D_DIFF:]
    return q, k, v, f


def _lambda(lq1, lk1, lq2, lk2, lambda_init):
    e1 = jnp.exp(jnp.sum(lq1.astype(jnp.float32) * lk1.astype(jnp.float32)))
    e2 = jnp.exp(jnp.sum(lq2.astype(jnp.float32) * lk2.astype(jnp.float32)))
    return e1 - e2 + lambda_init


def _diff_maps_apply(q, keys, vals, lam):
    s = jnp.einsum('bhcqd,bhckd->bhcqk', q, keys).astype(jnp.float32) * ATTN_SCALE
    p = jax.nn.softmax(s, axis=-1)
    a = p[:, :, 0] - lam * p[:, :, 1]
    return jnp.einsum('bhqk,bhkv->bhqv', a.astype(vals.dtype), vals)


def _diff_attn_latent(q, k, v, k_ctx, v_ctx, lam):
    b, h, _, n, dh = q.shape
    keys = jnp.concatenate([k, k_ctx], axis=3)
    vals = jnp.concatenate([v, v_ctx], axis=2)
    qb = jnp.moveaxis(q.reshape(b, h, 2, n // Q_BLOCK, Q_BLOCK, dh), 3, 0)
    out = lax.map(lambda qi: _diff_maps_apply(qi, keys, vals, lam), qb)
    return jnp.moveaxis(out, 0, 2).reshape(b, h, n, 2 * dh)


def _diff_heads_out(o, subln_g, lambda_init):
    b, h, n, dv = o.shape
    o = _rmsnorm(o, subln_g) * (1.0 - lambda_init)
    return o.transpose(0, 2, 1, 3).reshape(b, n, h * dv)


def _fourier_mix(f, w_fourier):
    b, n, _ = f.shape
    g = f.astype(jnp.float32).reshape(b, n, N_FOURIER_GROUPS, FOURIER_GROUP_DIM)
    z = jnp.fft.fft2(g, axes=(1, 3), norm='ortho').real
    return z.reshape(b, n, D_FOURIER).astype(f.dtype) @ w_fourier


def setup_inputs(seed: int = 0) -> dict:
    key = jax.random.key(seed)
    ks = jax.random.split(key, 24)
    f32 = jnp.float32

    def nrm(k, shape, fan_in, gain=1.0):
        return jax.random.normal(k, shape, f32) * (gain * fan_in ** -0.5)

    def gain(k, shape):
        return 1.0 + 0.05 * jax.random.normal(k, shape, f32)

    return {
        'x': jax.random.normal(ks[0], (BATCH, SEQ, D_MODEL), f32),
        'c': jax.random.normal(ks[1], (BATCH, D_MODEL), f32),
        'ctx': jax.random.normal(ks[2], (BATCH, CTX_LEN, D_MODEL), f32),
        'c_ctx': jax.random.normal(ks[3], (D_MODEL,), f32),
        'w_ada': nrm(ks[4], (DEPTH, D_MODEL, N_MOD * D_MODEL), D_MODEL, 0.5),
        'b_ada': 0.01 * jax.random.normal(ks[5], (DEPTH, N_MOD * D_MODEL), f32),
        'norm1_g': gain(ks[6], (DEPTH, D_MODEL)),
        'ffn1_w_gate': nrm(ks[7], (DEPTH, D_MODEL, D_FF), D_MODEL),
        'ffn1_w_up': nrm(ks[8], (DEPTH, D_MODEL, D_FF), D_MODEL),
        'ffn1_w_down': nrm(ks[9], (DEPTH, D_FF, D_MODEL), D_FF),
        'norm_mix_g': gain(ks[10], (DEPTH, D_MODEL)),
        'w_in': nrm(ks[11], (DEPTH, D_MODEL, D_IN_PROJ), D_MODEL),
        'lambda_q1': 0.1 * jax.random.normal(ks[12], (DEPTH, DIFF_HEAD_DIM), f32),
        'lambda_k1': 0.1 * jax.random.normal(ks[13], (DEPTH, DIFF_HEAD_DIM), f32),
        'lambda_q2': 0.1 * jax.random.normal(ks[14], (DEPTH, DIFF_HEAD_DIM), f32),
        'lambda_k2': 0.1 * jax.random.normal(ks[15], (DEPTH, DIFF_HEAD_DIM), f32),
        'subln_g': gain(ks[16], (DEPTH, 2 * DIFF_HEAD_DIM)),
        'w_fourier': nrm(ks[17], (DEPTH, D_FOURIER, D_FOURIER), D_FOURIER),
        'w_out': nrm(ks[18], (DEPTH, D_MIX, D_MODEL), D_MIX),
        'norm2_g': gain(ks[19], (DEPTH, D_MODEL)),
        'ffn2_w_gate': nrm(ks[20], (DEPTH, D_MODEL, D_FF), D_MODEL),
        'ffn2_w_up': nrm(ks[21], (DEPTH, D_MODEL, D_FF), D_MODEL),
        'ffn2_w_down': nrm(ks[22], (DEPTH, D_FF, D_MODEL), D_FF),
        'final_norm_g': gain(ks[23], (D_MODEL,)),
    }


def reference(x, c, ctx, c_ctx, w_ada, b_ada, norm1_g, ffn1_w_gate, ffn1_w_up, ffn1_w_down,
              norm_mix_g, w_in, lambda_q1, lambda_k1, lambda_q2, lambda_k2, subln_g, w_fourier,
              w_out, norm2_g, ffn2_w_gate, ffn2_w_up, ffn2_w_down, final_norm_g):
    cos, sin = _rope_tables(x.shape[1])
    h_ctx = ctx
    silu_c = jax.nn.silu(c)
    silu_cc = jax.nn.silu(c_ctx)[None, :]
    for l in range(DEPTH):
        lambda_init = 0.8 - 0.6 * math.exp(-0.3 * l)
        update_ctx = l < DEPTH - 1
        mod_x = jnp.split(silu_c @ w_ada[l] + b_ada[l], N_MOD, axis=-1)
        mod_c = jnp.split(silu_cc @ w_ada[l] + b_ada[l], N_MOD, axis=-1)

        x = _half_ffn(x, mod_x[0], mod_x[1], mod_x[2], norm1_g[l], ffn1_w_gate[l], ffn1_w_up[l], ffn1_w_down[l])
        h_ctx = _half_ffn(h_ctx, mod_c[0], mod_c[1], mod_c[2], norm1_g[l], ffn1_w_gate[l], ffn1_w_up[l], ffn1_w_down[l])

        p_x = _modulate(_rmsnorm(x, norm_mix_g[l]), mod_x[3], mod_x[4]) @ w_in[l]
        p_c = _modulate(_rmsnorm(h_ctx, norm_mix_g[l]), mod_c[3], mod_c[4]) @ w_in[l]
        q, k, v, f = _split_proj(p_x)
        qc, kc, vc, fc = _split_proj(p_c)
        q = _apply_rope(q, cos, sin)
        k = _apply_rope(k, cos, sin)
        lam = _lambda(lambda_q1[l], lambda_k1[l], lambda_q2[l], lambda_k2[l], lambda_init)

        att = _diff_heads_out(_diff_attn_latent(q, k, v, kc, vc, lam), subln_g[l], lambda_init)
        mix = jnp.concatenate([att, _fourier_mix(f, w_fourier[l])], axis=-1) @ w_out[l]
        x = x + mod_x[5][:, None, :] * mix

        if update_ctx:
            att_c = _diff_heads_out(_diff_maps_apply(qc, kc, vc, lam), subln_g[l], lambda_init)
            mix_c = jnp.concatenate([att_c, _fourier_mix(fc, w_fourier[l])], axis=-1) @ w_out[l]
            h_ctx = h_ctx + mod_c[5][:, None, :] * mix_c
            h_ctx = _half_ffn(h_ctx, mod_c[6], mod_c[7], mod_c[8], norm2_g[l], ffn2_w_gate[l], ffn2_w_up[l], ffn2_w_down[l])

        x = _half_ffn(x, mod_x[6], mod_x[7], mod_x[8], norm2_g[l], ffn2_w_gate[l], ffn2_w_up[l], ffn2_w_down[l])
    return _rmsnorm(x, final_norm_g)
```

```python
from contextlib import ExitStack

import numpy as np
import ml_dtypes
import concourse.bass as bass
import concourse.mybir as mybir
from concourse.bass_utils import run_bass_kernel_spmd

F32 = mybir.dt.float32
BF16 = mybir.dt.bfloat16
AF = mybir.ActivationFunctionType
ALU = mybir.AluOpType
AX = mybir.AxisListType

D = 1024
NT = 4096
NCTX = 256
NK = NT + NCTX
NQ = 2048
DFF = 2816
NF = DFF // 128
TB = 256
EPS = 1e-6
LAMBDA_INIT = 0.2

DEBUG = False


class Prog:
    ENGS = ("pe", "act", "dve", "pool", "sp")
    GROUP = 2000
    NDMA = 20

    def __init__(self, nc):
        self.nc = nc
        self.ops = {e: [] for e in self.ENGS}
        self.last_w = {}
        self.readers = {}
        self.dma_rr = {e: 0 for e in self.ENGS}
        self.dma_last = {}
        self.dma_cnt = {}

    def _add_reader(self, res, me, is_dma):
        d = self.readers.setdefault(res, {})
        if is_dma:
            d.setdefault("dma", []).append(me)
        else:
            d[me[0]] = me

    def _reader_list(self, res):
        d = self.readers.get(res)
        if not d:
            return []
        out = []
        for k, v in d.items():
            if k == "dma":
                out.extend(v)
            else:
                out.append(v)
        return out

    def op(self, eng, fn, reads=(), writes=(), dma=False):
        idx = len(self.ops[eng])
        me = (eng, idx)
        deps = set()
        for r in reads:
            w = self.last_w.get(r)
            if w is not None:
                deps.add(w)
        for w_ in writes:
            w = self.last_w.get(w_)
            if w is not None:
                deps.add(w)
            deps.update(self._reader_list(w_))
        rec = dict(fn=fn, deps=deps, signal=False, dma=dma, sig=None)
        if dma:
            k = self.dma_rr[eng] % self.NDMA
            self.dma_rr[eng] += 1
            key = ("dma", eng, k)
            prev = self.dma_last.get(key)
            if prev is not None:
                deps.add(prev)
            self.dma_last[key] = me
            n = self.dma_cnt.get(key, 0) + 1
            self.dma_cnt[key] = n
            rec["sig"] = (key, 16 * n)
        deps.discard(me)
        if eng == "pe":
            deps = {d for d in deps if d[0] != "pe"}
        rec["deps"] = deps
        self.ops[eng].append(rec)
        for r in reads:
            self._add_reader(r, me, dma)
        for w_ in writes:
            self.last_w[w_] = me
            self.readers[w_] = {}
        return me

    def inherit(self, new, olds):
        d = self.readers.setdefault(new, {})
        for o in olds:
            w = self.last_w.get(o)
            if w is not None:
                d.setdefault("dma", []).append(w)
            for r in self._reader_list(o):
                d.setdefault("dma", []).append(r)

    def emit(self):
        nc = self.nc
        for e in self.ENGS:
            for rec in self.ops[e]:
                for (e2, i2) in rec["deps"]:
                    t = self.ops[e2][i2]
                    if not t["dma"]:
                        t["signal"] = True
        semkeys = []
        for e in self.ENGS:
            k = 0
            for rec in self.ops[e]:
                if not rec["dma"] and rec["signal"]:
                    rec["sig"] = (("eng", e, k // self.GROUP), k % self.GROUP + 1)
                    k += 1
                if rec["sig"] is not None and rec["sig"][0] not in semkeys:
                    semkeys.append(rec["sig"][0])
        with ExitStack() as es:
            sems = {}
            for k in semkeys:
                sems[k] = es.enter_context(nc.semaphore("s_" + "_".join(str(x) for x in k)))
            block = es.enter_context(nc.Block())
            handles = {"pe": block.tensor, "act": block.scalar, "dve": block.vector,
                       "pool": block.gpsimd, "sp": block.sync}
            for e in self.ENGS:
                if not self.ops[e]:
                    continue

                def body(engh, e=e):
                    waited = {}
                    for rec in self.ops[e]:
                        need = {}
                        for (e2, i2) in rec["deps"]:
                            sk, v = self.ops[e2][i2]["sig"]
                            if waited.get(sk, 0) >= v:
                                continue
                            if need.get(sk, 0) < v:
                                need[sk] = v
                        for sk, v in need.items():
                            engh.wait_ge(sems[sk], v)
                            waited[sk] = v
                        ins = rec["fn"](engh)
                        if ins is not None and rec["sig"] is not None and (rec["dma"] or rec["signal"]):
                            sk, v = rec["sig"]
                            ins.then_inc(sems[sk], 16 if rec["dma"] else 1)
                handles[e](body)


def _prod(s):
    r = 1
    for x in s:
        r *= x
    return r


class Arena:
    def __init__(self, t):
        self.t = t

    def _shape(self, v, shape):
        if len(shape) == 2:
            return v
        if len(shape) == 3:
            return v.rearrange("p (a b) -> p a b", a=shape[1])
        raise ValueError

    def f32(self, off, shape):
        n = _prod(shape[1:])
        assert off % 4 == 0
        v = self.t[:, off // 4: off // 4 + n]
        return self._shape(v, shape)

    def bf(self, off, shape):
        n = _prod(shape[1:])
        assert off % 4 == 0 and n % 2 == 0
        v = self.t[:, off // 4: off // 4 + n // 2].bitcast(BF16)
        return self._shape(v, shape)


ARENA_WORDS = 50688
R_KT = 0
R_V = 52224
R_QT = 104448
R_ZB = 129024
R_PH = 137216


def build_nc():
    nc = bass.Bass("TRN2", target_bir_lowering=False)

    def din(name, shape, dt=F32):
        return nc.dram_tensor(name, shape, dt, kind="ExternalInput").ap()

    xT = din("xT", [D, NT])
    ctxT = din("ctxT", [D, NCTX])
    cc2 = din("cc2", [128, 8, 2])
    w_ada = din("w_ada", [D, 9 * D])
    b_adaT = din("b_adaT", [128, 72])
    gains = din("gains", [128, 4, 8])
    w1g = din("w1g", [D, DFF]); w1u = din("w1u", [D, DFF]); w1d = din("w1d", [DFF, D])
    w2g = din("w2g", [D, DFF]); w2u = din("w2u", [D, DFF]); w2d = din("w2d", [DFF, D])
    w_in = din("w_in", [D, 2560])
    w_in_sw = din("w_in_sw", [D, 1536])
    lamin = din("lamin", [128, 4, 64])
    sublnT = din("sublnT", [128, 1])
    w_fo = din("w_fo", [256, 256])
    bcs = din("bcs", [256, 512], BF16)
    w_out = din("w_out", [D, D])
    ropec = din("ropec", [128, NT])
    ropes = din("ropes", [128, NT])
    dftc = din("dftc", [NT, NQ], BF16)
    dfts = din("dfts", [NT, NQ], BF16)
    outT = nc.dram_tensor("outT", [D, NQ], F32, kind="ExternalOutput").ap()
    scr = nc.dram_tensor("scr", [D, NK], F32, kind="Internal").ap()

    es = ExitStack()
    with es:
        def sb(name, shape, dt):
            return es.enter_context(nc.sbuf_tensor(name, shape, dt))

        arena_t = sb("arena", [128, ARENA_WORDS], F32)
        AR = Arena(arena_t)
        pp = [es.enter_context(nc.psum_tensor("ps%d" % i, [128, 1024], F32)) for i in range(4)]
        ones = sb("ones", [128, 128], BF16)
        s2 = sb("s2", [128, 8, 2], F32)
        s2b = sb("s2b", [128, 8, 2], BF16)
        badaT = sb("badaT", [128, 72], F32)
        gn = sb("gn", [128, 4, 8], F32)
        modT = sb("modT", [128, 72, 2], F32)
        sc = sb("sc", [128, 14, 8], F32)
        lam_t = sb("lam_t", [128, 4, 64], F32)
        lam_w = sb("lam_w", [128, 8], F32)
        sgt = sb("sgt", [128, 2], F32)
        wf_t = sb("wf_t", [128, 2, 256], BF16)
        bcs_t = sb("bcs_t", [128, 2, 512], BF16)

        P = Prog(nc)

        def bank(b, n=512):
            return pp[b // 2][:, (b % 2) * 512: (b % 2) * 512 + n]

        def dma(eng, out, in_, reads=(), writes=()):
            return P.op(eng, lambda e: e.dma_start(out=out, in_=in_), reads=reads, writes=writes, dma=True)

        P.op("dve", lambda e: e.memset(ones[:], 1.0), writes=["ones"])
        dma("sp", s2[:], cc2, writes=["s2"])
        dma("sp", badaT[:], b_adaT, writes=["badaT"])
        dma("sp", gn[:], gains, writes=["gn"])
        dma("sp", lam_t[:], lamin, writes=["lam_t"])
        dma("sp", sgt[:, 0:1], sublnT, writes=["sgt0"])
        dma("pool", wf_t[:], w_fo.rearrange("(c p) n -> p c n", p=128), writes=["wf"])
        dma("sp", bcs_t[:], bcs.rearrange("(c p) n -> p c n", p=128), writes=["bcs"])
        P.op("act", lambda e: e.activation(out=s2[:], in_=s2[:], func=AF.Silu), reads=["s2"], writes=["s2"])

        def load_ffn_weights(tag, wg, wu, wd, olds):
            Wg = AR.bf(0, [128, 8, DFF]); Wu = AR.bf(45056, [128, 8, DFF]); Wd = AR.bf(90112, [128, NF, D])
            wgv = wg.rearrange("(j p) f -> p j f", p=128)
            wuv = wu.rearrange("(j p) f -> p j f", p=128)
            wdv = wd.rearrange("(i p) d -> p i d", p=128)
            for pr in range(11):
                for nm, Wt, src in (("g", Wg, wgv), ("u", Wu, wuv)):
                    res = (tag, nm, pr)
                    P.inherit(res, olds(nm, pr))
                    dma("pool", Wt[:, :, pr * 256:(pr + 1) * 256], src[:, :, pr * 256:(pr + 1) * 256], writes=[res])
            for pr in range(11):
                res = (tag, "d", pr)
                P.inherit(res, olds("d", pr))
                dma("pool", Wd[:, 2 * pr:2 * pr + 2, :], wdv[:, 2 * pr:2 * pr + 2, :], writes=[res])
            return Wg, Wu, Wd

        ST0 = 183296
        stg = [AR.bf(ST0, [128, 8, 512]), AR.bf(ST0 + 8192, [128, 8, 512])]
        wav = w_ada.rearrange("(j p) n -> p j n", p=128)
        P.op("dve", lambda e: e.tensor_copy(out=s2b[:], in_=s2[:]), reads=["s2"], writes=["s2b"])

        def ada_load(nb):
            dma("pool", stg[nb % 2][:, :, :], wav[:, :, nb * 512:(nb + 1) * 512], writes=[("stg", nb % 2)])

        def ada_compute(nb):
            sbuf = stg[nb % 2]
            k = nb // 2
            for q in range(4):
                n = nb * 4 + q
                for j in range(8):
                    P.op("pe", lambda e, sbuf=sbuf, q=q, j=j, n=n: e.matmul(
                        pp[3][:, 512 + 2 * n:512 + 2 * n + 2], lhsT=sbuf[:, j, q * 128:(q + 1) * 128], rhs=s2b[:, j, :],
                        start=(j == 0), stop=(j == 7)), reads=[("stg", nb % 2), "s2b"], writes=["ps_ada"])
            if nb % 2 == 1:
                P.op("dve", lambda e, k=k: e.tensor_tensor(
                    out=modT[:, 8 * k:8 * k + 8, :], in0=pp[3][:, 512 + 16 * k:512 + 16 * k + 16].rearrange("p (n m) -> p n m", m=2),
                    in1=badaT[:, 8 * k:8 * k + 8].unsqueeze(2).to_broadcast([128, 8, 2]), op=ALU.add),
                    reads=["ps_ada", "badaT"], writes=[("modT", k)])

        ada_load(0)
        ada_load(1)
        for nb in range(6):
            ada_compute(nb)
            if nb + 2 < 6:
                ada_load(nb + 2)
        Wg, Wu, Wd = load_ffn_weights("W1", w1g, w1u, w1d, lambda nm, pr: [])

        def mod(k, m):
            return modT[:, k * 8:(k + 1) * 8, m]

        def mk_scale(idx, gi, k, m):
            P.op("dve", lambda e: e.scalar_tensor_tensor(out=sc[:, idx, :], in0=mod(k, m), scalar=1.0, in1=gn[:, gi, :],
                                                         op0=ALU.add, op1=ALU.mult), reads=[("modT", k), "gn"], writes=[("sc", idx)])

        def mk_copy(idx, k, m, f):
            P.op("dve", lambda e: e.tensor_scalar(out=sc[:, idx, :], in0=mod(k, m), scalar1=f, scalar2=None, op0=ALU.mult),
                 reads=[("modT", k)], writes=[("sc", idx)])

        mk_scale(0, 0, 1, 0); mk_copy(1, 0, 0, 1.0); mk_copy(2, 2, 0, 0.5)
        mk_scale(3, 0, 1, 1); mk_copy(4, 0, 1, 1.0); mk_copy(5, 2, 1, 0.5)

        def ada_hook(n):
            if n == 0:
                ada_load(6)
                ada_load(7)
                return
            nb = 5 + n
            ada_compute(nb)
            if nb + 2 < 18:
                ada_load(nb + 2)
            if nb == 17:
                mk_scale(6, 1, 4, 0); mk_copy(7, 3, 0, 1.0)
                mk_scale(8, 1, 4, 1); mk_copy(9, 3, 1, 1.0)
                mk_copy(10, 5, 0, 1.0)
                mk_scale(11, 2, 7, 0); mk_copy(12, 6, 0, 1.0); mk_copy(13, 8, 0, 0.5)
        SCR = [("sc", i) for i in range(14)]

        P.op("dve", lambda e: e.tensor_tensor(out=lam_t[:, 0, :], in0=lam_t[:, 0, :], in1=lam_t[:, 1, :], op=ALU.mult),
             reads=["lam_t"], writes=["lam_t"])
        P.op("dve", lambda e: e.tensor_tensor(out=lam_t[:, 2, :], in0=lam_t[:, 2, :], in1=lam_t[:, 3, :], op=ALU.mult),
             reads=["lam_t"], writes=["lam_t"])
        P.op("dve", lambda e: e.reduce_sum(out=lam_w[:, 0:1], in_=lam_t[:, 0, :], axis=AX.X), reads=["lam_t"], writes=["lam_w"])
        P.op("dve", lambda e: e.reduce_sum(out=lam_w[:, 1:2], in_=lam_t[:, 2, :], axis=AX.X), reads=["lam_w", "lam_t"], writes=["lam_w"])
        P.op("act", lambda e: e.activation(out=lam_w[:, 2:4], in_=lam_w[:, 0:2], func=AF.Exp), reads=["lam_w"], writes=["lam_w"])
        P.op("dve", lambda e: e.tensor_tensor(out=lam_w[:, 4:5], in0=lam_w[:, 3:4], in1=lam_w[:, 2:3], op=ALU.subtract),
             reads=["lam_w"], writes=["lam_w"])
        P.op("dve", lambda e: e.tensor_scalar(out=lam_w[:, 5:6], in0=lam_w[:, 4:5], scalar1=-LAMBDA_INIT, scalar2=None, op0=ALU.add),
             reads=["lam_w"], writes=["neglam"])
        neglam = lam_w[:, 5:6]
        P.op("dve", lambda e: e.tensor_scalar(out=sgt[:, 1:2], in0=sgt[:, 0:1], scalar1=1.0 - LAMBDA_INIT, scalar2=None, op0=ALU.mult),
             reads=["sgt0"], writes=["sg08"])
        sg08 = sgt[:, 1:2]

        def work(base):
            w = {}
            w["xb"] = [AR.f32(base, [128, 8, TB]), AR.f32(base + 8192, [128, 8, TB])]
            w["sq"] = AR.bf(base + 16384, [128, 8, TB])
            w["hT"] = AR.bf(base + 20480, [128, 8, TB])
            w["sd"] = AR.f32(base + 24576, [128, TB])
            w["rstd"] = AR.f32(base + 25600, [128, TB])
            return w

        def norm_stage_pre(tag, w, xbuf, xres, n):
            P.op("act", lambda e: e.activation(out=w["sq"][:], in_=xbuf[:], func=AF.Square), reads=[xres], writes=[(tag, "sq")])

        def norm_stage_pe(tag, w, sbank):
            for j in range(8):
                P.op("pe", lambda e, j=j: e.matmul(bank(sbank, TB), lhsT=ones[:], rhs=w["sq"][:, j, :], start=(j == 0), stop=(j == 7)),
                     reads=[(tag, "sq"), "ones"], writes=[("bank", sbank)])

        def norm_stage_post(tag, w, xbuf, xres, tmp, tmpres, Ai, Bi, sbank, hres, hT=None):
            hT = w["hT"] if hT is None else hT
            P.op("act", lambda e: e.activation(out=w["sd"][:], in_=bank(sbank, TB), func=AF.Sqrt, scale=1.0 / D, bias=EPS),
                 reads=[("bank", sbank)], writes=[(tag, "sd")])
            P.op("dve", lambda e: e.reciprocal(out=w["rstd"][:], in_=w["sd"][:]), reads=[(tag, "sd")], writes=[(tag, "rstd")])
            P.op("dve", lambda e: e.tensor_tensor(out=tmp[:], in0=xbuf[:], in1=w["rstd"][:].unsqueeze(1).to_broadcast([128, 8, TB]),
                                                  op=ALU.mult), reads=[xres, (tag, "rstd")], writes=[tmpres])
            for j in range(8):
                if Bi is None:
                    P.op("act", lambda e, j=j: e.activation(out=hT[:, j, :], in_=tmp[:, j, :], func=AF.Identity,
                                                            scale=sc[:, Ai, j:j + 1]),
                         reads=[tmpres, ("sc", Ai)], writes=[hres])
                else:
                    P.op("act", lambda e, j=j: e.activation(out=hT[:, j, :], in_=tmp[:, j, :], func=AF.Identity,
                                                            scale=sc[:, Ai, j:j + 1], bias=sc[:, Bi, j:j + 1]),
                         reads=[tmpres, ("sc", Ai), ("sc", Bi)], writes=[hres])

        def ffn_pass(tag, Wtag, Wts, blocks, final=None, after_block=None):
            Wg, Wu, Wd = Wts
            base = 135168
            w = work(base)
            aT = AR.bf(base + 26624, [128, NF, TB])
            sg = [AR.f32(base + 37888, [128, TB]), AR.f32(base + 38912, [128, TB])]
            tbuf = AR.f32(base + 39936, [128, 8, TB])
            nb = len(blocks)
            WR = [(Wtag, nm, pr) for nm in "gud" for pr in range(11)]

            def xres(n):
                return (tag, "xb", n % 2)

            def load(n):
                b = blocks[n]
                dma("sp", w["xb"][n % 2][:, :, :], b["src"].rearrange("(j p) t -> p j t", p=128),
                    reads=[b["srcres"]], writes=[xres(n)])

            def pre(n):
                norm_stage_pre(tag, w, w["xb"][n % 2], xres(n), n)

            def stats_and_mod(n):
                b = blocks[n]
                norm_stage_pe(tag, w, 6)
                norm_stage_post(tag, w, w["xb"][n % 2], xres(n), tbuf, (tag, "tbuf"), b["A"], b["B"], 6, (tag, "hT"))

            def gu(n, mid_hook, early_hook=None):
                for i in range(NF):
                    if i == 3 and early_hook is not None:
                        early_hook()
                    gb, ub = i % 2, 2 + i % 2
                    for j in range(8):
                        P.op("pe", lambda e, i=i, j=j, gb=gb: e.matmul(bank(gb, TB), lhsT=Wg[:, j, i * 128:(i + 1) * 128],
                                                                        rhs=w["hT"][:, j, :], start=(j == 0), stop=(j == 7)),
                             reads=[(Wtag, "g", i // 2), (tag, "hT")], writes=[("bank", gb)])
                    for j in range(8):
                        P.op("pe", lambda e, i=i, j=j, ub=ub: e.matmul(bank(ub, TB), lhsT=Wu[:, j, i * 128:(i + 1) * 128],
                                                                        rhs=w["hT"][:, j, :], start=(j == 0), stop=(j == 7)),
                             reads=[(Wtag, "u", i // 2), (tag, "hT")], writes=[("bank", ub)])
                    P.op("act", lambda e, i=i, gb=gb: e.activation(out=sg[i % 2][:], in_=bank(gb, TB), func=AF.Silu),
                         reads=[("bank", gb)], writes=[(tag, "sg", i % 2)])
                    P.op("dve", lambda e, i=i, ub=ub: e.tensor_tensor(out=aT[:, i, :], in0=sg[i % 2][:], in1=bank(ub, TB), op=ALU.mult),
                         reads=[(tag, "sg", i % 2), ("bank", ub)], writes=[(tag, "aT", i)])
                    if i == 10 and mid_hook is not None:
                        mid_hook()

            def down(n):
                b = blocks[n]
                xb = w["xb"][n % 2]
                for dc in range(8):
                    db = 4 + dc % 2
                    for i in range(NF):
                        P.op("pe", lambda e, i=i, dc=dc, db=db: e.matmul(bank(db, TB), lhsT=Wd[:, i, dc * 128:(dc + 1) * 128],
                                                                          rhs=aT[:, i, :], start=(i == 0), stop=(i == NF - 1)),
                             reads=[(Wtag, "d", i // 2), (tag, "aT", i)], writes=[("bank", db)])
                    P.op("dve", lambda e, dc=dc, db=db, xb=xb, G=b["G"]: e.scalar_tensor_tensor(
                        out=xb[:, dc, :], in0=bank(db, TB), scalar=sc[:, G, dc:dc + 1], in1=xb[:, dc, :], op0=ALU.mult, op1=ALU.add),
                        reads=[("bank", db), ("sc", b["G"]), xres(n)], writes=[xres(n)])

            def fin_pre(n):
                xb = w["xb"][n % 2]
                P.op("act", lambda e: e.activation(out=w["sq"][:], in_=xb[:], func=AF.Square), reads=[xres(n)], writes=[(tag, "sq")])

            def fin_pe(n):
                for j in range(8):
                    P.op("pe", lambda e, j=j: e.matmul(bank(7, TB), lhsT=ones[:], rhs=w["sq"][:, j, :], start=(j == 0), stop=(j == 7)),
                         reads=[(tag, "sq"), "ones"], writes=[("bank", 7)])

            def fin_post(n):
                xb = w["xb"][n % 2]
                P.op("act", lambda e: e.activation(out=w["sd"][:], in_=bank(7, TB), func=AF.Sqrt, scale=1.0 / D, bias=EPS),
                     reads=[("bank", 7)], writes=[(tag, "sd")])
                P.op("dve", lambda e: e.reciprocal(out=w["rstd"][:], in_=w["sd"][:]), reads=[(tag, "sd")], writes=[(tag, "rstd")])
                P.op("dve", lambda e: e.tensor_tensor(out=xb[:], in0=xb[:], in1=w["rstd"][:].unsqueeze(1).to_broadcast([128, 8, TB]),
                                                      op=ALU.mult), reads=[xres(n), (tag, "rstd")], writes=[xres(n)])
                for j in range(8):
                    P.op("act", lambda e, j=j: e.activation(out=xb[:, j, :], in_=xb[:, j, :], func=AF.Identity, scale=gn[:, 3, j:j + 1]),
                         reads=[xres(n), "gn"], writes=[xres(n)])

            def store(n):
                b = blocks[n]
                dma("sp", b["dst"].rearrange("(j p) t -> p j t", p=128), w["xb"][n % 2][:, :, :],
                    reads=[xres(n)], writes=[b["dstres"]])

            load(0)
            if nb > 1:
                load(1)
            pre(0)
            stats_and_mod(0)
            for n in range(nb):
                hook = (lambda n=n: pre(n + 1)) if n + 1 < nb else None
                gu(n, hook)
                if n + 1 < nb:
                    stats_and_mod(n + 1)
                down(n)
                if final:
                    fin_pre(n)
                    fin_pe(n)
                    fin_post(n)
                store(n)
                if n + 2 < nb:
                    load(n + 2)
                if after_block and n in after_block:
                    after_block[n]()
            return [(tag, "xb", 0), (tag, "xb", 1), (tag, "sq"), (tag, "hT"), (tag, "sd"), (tag, "rstd"), (tag, "tbuf"),
                    (tag, "sg", 0), (tag, "sg", 1)] + [(tag, "aT", i) for i in range(NF)]

        blocks1 = []
        for n in range(NK // TB):
            t0 = n * TB
            if t0 < NT:
                src = xT[:, t0:t0 + TB]; A, B, G = 0, 1, 2
            else:
                src = ctxT[:, t0 - NT:t0 - NT + TB]; A, B, G = 3, 4, 5
            blocks1.append(dict(src=src, srcres=("in", n), dst=scr[:, t0:t0 + TB], dstres=("scr", n), A=A, B=B, G=G))
        p1res = ffn_pass("P1", "W1", (Wg, Wu, Wd), blocks1,
                         after_block={n: (lambda n=n: ada_hook(n)) for n in range(13)})
        p1res = p1res + [("stg", 0), ("stg", 1)]
        W1R = [("W1", nm, pr) for nm in "gud" for pr in range(11)]

        def proj_pass(tag, blist, olds_work, do):
            w = work(R_PH)
            hT2 = [w["hT"], AR.bf(R_PH + 57344, [128, 8, TB])]
            cosb = [AR.f32(R_PH + 26624, [128, TB]), AR.f32(R_PH + 27648, [128, TB])]
            sinb = [AR.f32(R_PH + 28672, [128, TB]), AR.f32(R_PH + 29696, [128, TB])]
            t1 = AR.f32(R_PH + 30720, [128, TB]); t2 = AR.f32(R_PH + 31744, [128, TB])
            wres = [(tag, "xb", 0), (tag, "xb", 1),
                    (tag, "sq"), (tag, "hT", 0), (tag, "hT", 1), (tag, "sd"), (tag, "rstd"),
                    (tag, "cos", 0), (tag, "cos", 1), (tag, "sin", 0), (tag, "sin", 1), (tag, "t1"), (tag, "t2")]
            for r in wres:
                P.inherit(r, olds_work)
            nb = len(blist)

            def xres(i):
                return (tag, "xb", i % 2)

            def hres(i):
                return (tag, "hT", i % 2)

            def load(i):
                n = blist[i]
                dma("sp", w["xb"][i % 2][:, :, :], scr[:, n * TB:(n + 1) * TB].rearrange("(j p) t -> p j t", p=128),
                    reads=[("scr", n)], writes=[xres(i)])
                if do["rope"] and n < NT // TB:
                    dma("sp", cosb[i % 2][:], ropec[:, n * TB:(n + 1) * TB], writes=[(tag, "cos", i % 2)])
                    dma("sp", sinb[i % 2][:], ropes[:, n * TB:(n + 1) * TB], writes=[(tag, "sin", i % 2)])
                if do.get("f"):
                    tabs = do["tabs"]
                    dma("sp", tabs[i % 2][:, :, 0, :], dftc[n * TB:(n + 1) * TB, :].rearrange("(s p) k -> p s k", p=128),
                        writes=[(tag, "tab", i % 2, 0)])
                    dma("sp", tabs[i % 2][:, :, 1, :], dfts[n * TB:(n + 1) * TB, :].rearrange("(s p) k -> p s k", p=128),
                        writes=[(tag, "tab", i % 2, 1)])

            def prep_pre(i):
                norm_stage_pre(tag, w, w["xb"][i % 2], xres(i), i)

            def prep_pe(i):
                norm_stage_pe(tag, w, 6)

            def prep_ab(i):
                P.op("act", lambda e: e.activation(out=w["sd"][:], in_=bank(6, TB), func=AF.Ln, scale=1.0 / D, bias=EPS),
                     reads=[("bank", 6)], writes=[(tag, "sd")])
                P.op("act", lambda e: e.activation(out=w["rstd"][:], in_=w["sd"][:], func=AF.Exp, scale=-0.5),
                     reads=[(tag, "sd")], writes=[(tag, "rstd")])

            def prep_chunk(i, j):
                n = blist[i]
                ctx = n >= NT // TB
                A, B = (8, 9) if ctx else (6, 7)
                xbuf = w["xb"][i % 2]
                hT = hT2[i % 2]
                if j == 0:
                    P.op("dve", lambda e: e.tensor_tensor(out=xbuf[:], in0=xbuf[:], in1=w["rstd"][:].unsqueeze(1).to_broadcast([128, 8, TB]),
                                                          op=ALU.mult), reads=[xres(i), (tag, "rstd")], writes=[xres(i)])
                P.op("act", lambda e: e.activation(out=hT[:, j, :], in_=xbuf[:, j, :], func=AF.Identity,
                                                   scale=sc[:, A, j:j + 1], bias=sc[:, B, j:j + 1]),
                     reads=[xres(i), ("sc", A), ("sc", B)], writes=[hres(i)])

            def rope_unit(i, h, Wm, Ws, wr, wsr, dstT, dresf, tok0):
                n = blist[i]
                ctx = n >= NT // TB
                hT = hT2[i % 2]
                kb, sbk = h % 2, 2 + h % 2
                for j in range(8):
                    P.op("pe", lambda e, j=j: e.matmul(bank(kb, TB), lhsT=Wm[:, j, h * 128:(h + 1) * 128],
                                                       rhs=hT[:, j, :], start=(j == 0), stop=(j == 7)),
                         reads=[wr, hres(i)], writes=[("bank", kb)])
                if ctx:
                    P.op("act", lambda e: e.activation(out=dstT[:, h, tok0:tok0 + TB], in_=bank(kb, TB), func=AF.Copy),
                         reads=[("bank", kb)], writes=[dresf(h, tok0)])
                    return
                for j in range(8):
                    P.op("pe", lambda e, j=j: e.matmul(bank(sbk, TB), lhsT=Ws[:, j, h * 128:(h + 1) * 128],
                                                       rhs=hT[:, j, :], start=(j == 0), stop=(j == 7)),
                         reads=[wsr, hres(i)], writes=[("bank", sbk)])
                P.op("dve", lambda e: e.tensor_tensor(out=t1[:], in0=bank(kb, TB), in1=cosb[i % 2][:], op=ALU.mult),
                     reads=[("bank", kb), (tag, "cos", i % 2)], writes=[(tag, "t1")])
                P.op("dve", lambda e: e.tensor_tensor(out=t2[:], in0=bank(sbk, TB), in1=sinb[i % 2][:], op=ALU.mult),
                     reads=[("bank", sbk), (tag, "sin", i % 2)], writes=[(tag, "t2")])
                P.op("dve", lambda e: e.tensor_tensor(out=dstT[:, h, tok0:tok0 + TB], in0=t1[:], in1=t2[:], op=ALU.add),
                     reads=[(tag, "t1"), (tag, "t2")], writes=[dresf(h, tok0)])

            def v_unit(i, s):
                n = blist[i]
                hT = hT2[i % 2]
                kc = n * (TB // 128) + s
                for (c0, cn, vb) in ((0, 512, 4), (512, 256, 5)):
                    for j in range(8):
                        P.op("pe", lambda e, j=j, c0=c0, cn=cn, vb=vb: e.matmul(
                            bank(vb, cn), lhsT=hT[:, j, s * 128:(s + 1) * 128], rhs=do["Wv"][:, j, c0:c0 + cn],
                            start=(j == 0), stop=(j == 7)), reads=[(tag, "Wv"), hres(i)], writes=[("bank", vb)])
                P.op("act", lambda e: e.activation(out=do["V"][:, kc, 0:512], in_=bank(4, 512), func=AF.Copy),
                     reads=[("bank", 4)], writes=["V"])
                P.op("act", lambda e: e.activation(out=do["V"][:, kc, 512:768], in_=bank(5, 256), func=AF.Copy),
                     reads=[("bank", 5)], writes=["V"])

            def f_unit(i, cc):
                hT = hT2[i % 2]
                fT = do["fT"]
                fb = cc
                for j in range(8):
                    P.op("pe", lambda e, j=j: e.matmul(bank(fb, TB), lhsT=do["Wf"][:, j, cc * 128:(cc + 1) * 128],
                                                       rhs=hT[:, j, :], start=(j == 0), stop=(j == 7)),
                         reads=[(tag, "Wf"), hres(i)], writes=[("bank", fb)])
                P.op("act", lambda e: e.activation(out=fT[:, cc, :], in_=bank(fb, TB), func=AF.Copy),
                     reads=[("bank", fb)], writes=[(tag, "fT", cc)])

            def g_unit(i, s):
                fT = do["fT"]; Gcs = do["Gcs"]
                gb = 2 + s
                for cc in range(2):
                    P.op("pe", lambda e, cc=cc: e.matmul(bank(gb, 512), lhsT=fT[:, cc, s * 128:(s + 1) * 128],
                                                         rhs=bcs_t[:, cc, :], start=(cc == 0), stop=(cc == 1)),
                         reads=[(tag, "fT", cc), "bcs"], writes=[("bank", gb)])
                P.op("act", lambda e: e.activation(out=Gcs[:, s, :], in_=bank(gb, 512), func=AF.Copy),
                     reads=[("bank", gb)], writes=[(tag, "Gcs", s)])

            def z_unit(i, cc, kf):
                Gcs = do["Gcs"]; tabs = do["tabs"]
                zb = (4, 5, 0, 1, 2, 3)[(cc * 4 + kf) % 6]
                k = 0
                for s in range(TB // 128):
                    for cs in range(2):
                        P.op("pe", lambda e, s=s, cs=cs, k=k: e.matmul(
                            bank(zb, 512), lhsT=Gcs[:, s, cs * 256 + cc * 128: cs * 256 + (cc + 1) * 128],
                            rhs=tabs[i % 2][:, s, cs, kf * 512:(kf + 1) * 512], start=(k == 0), stop=(k == 3)),
                            reads=[(tag, "Gcs", s), (tag, "tab", i % 2, cs)], writes=[("bank", zb)])
                        k += 1
                zsl = do["ZT"][:, cc, kf * 512:(kf + 1) * 512]
                if i == 0:
                    P.op("dve", lambda e: e.tensor_copy(out=zsl, in_=bank(zb, 512)),
                         reads=[("bank", zb)], writes=[("ZT", cc, kf)])
                else:
                    P.op("dve", lambda e: e.tensor_tensor(out=zsl, in0=bank(zb, 512), in1=zsl, op=ALU.add),
                         reads=[("bank", zb), ("ZT", cc, kf)], writes=[("ZT", cc, kf)])

            def units(i):
                n = blist[i]
                ctx = n >= NT // TB
                us = []
                if do.get("k"):
                    for h in range(6):
                        us.append(lambda h=h: rope_unit(i, h, do["Wk"], do["Wks"], (tag, "Wk"), (tag, "Wks"), do["KT"],
                                                        (lambda hh, t: "KT"), n * TB))
                    for s in range(TB // 128):
                        us.append(lambda s=s: v_unit(i, s))
                if do.get("q"):
                    for h in range(6):
                        us.append(lambda h=h: rope_unit(i, h, do["Wq"], do["Wqs"], (tag, "Wq", h), (tag, "Wqs", h), do["QT"],
                                                        (lambda hh, t: ("QT", hh, t // 512)), n * TB))
                if do.get("f"):
                    for cc in range(2):
                        us.append(lambda cc=cc: f_unit(i, cc))
                    for s in range(TB // 128):
                        us.append(lambda s=s: g_unit(i, s))
                    for cc in range(2):
                        for kf in range(4):
                            us.append(lambda cc=cc, kf=kf: z_unit(i, cc, kf))
                return us

            load(0)
            if nb > 1:
                load(1)
            prep_pre(0)
            prep_pe(0)
            prep_ab(0)
            for j in range(8):
                prep_chunk(0, j)
            for i in range(nb):
                us = units(i)
                nxt = i + 1 < nb
                if nxt:
                    prep_pre(i + 1)
                us[0]()
                if nxt:
                    prep_pe(i + 1)
                    prep_ab(i + 1)
                rest = us[1:]
                chunks = list(range(8)) if nxt else []
                per = -(-8 // len(rest))
                for u in rest:
                    u()
                    for _ in range(per):
                        if chunks:
                            prep_chunk(i + 1, chunks.pop(0))
                while chunks:
                    prep_chunk(i + 1, chunks.pop(0))
                if i + 2 < nb:
                    load(i + 2)
            return wres

        w_inv = w_in.rearrange("(j p) c -> p j c", p=128)
        w_swv = w_in_sw.rearrange("(j p) c -> p j c", p=128)

        ZTf = AR.f32(0, [128, 2, NQ])
        tab0 = arena_t[:, 16384 // 4: 16384 // 4 + 4096].bitcast(BF16).rearrange("p (s c k) -> p s c k", s=2, c=2)
        tab1 = arena_t[:, 32768 // 4: 32768 // 4 + 4096].bitcast(BF16).rearrange("p (s c k) -> p s c k", s=2, c=2)
        tabs = [tab0, tab1]
        Wf = AR.bf(49152, [128, 8, 256])
        fT = AR.bf(R_PH + 61440, [128, 2, TB])
        Gcs = AR.bf(R_PH + 62464, [128, 2, 512])
        for r in [("ZT", cc, kf) for cc in range(2) for kf in range(4)] + [("P2C", "tab", a, b) for a in range(2) for b in range(2)] + [("P2C", "Wf")]:
            P.inherit(r, W1R)
        for r in [("P2C", "fT", 0), ("P2C", "fT", 1), ("P2C", "Gcs", 0), ("P2C", "Gcs", 1)]:
            P.inherit(r, p1res)
        dma("pool", Wf[:, :, :], w_inv[:, :, 2304:2560], writes=[("P2C", "Wf")])
        resC = proj_pass("P2C", list(range(NT // TB)), p1res,
                         dict(rope=False, f=True, Wf=Wf, fT=fT, Gcs=Gcs, tabs=tabs, ZT=ZTf))
        ZTb = AR.bf(R_ZB, [128, 2, NQ])
        P.inherit("ZTb", W1R)
        for cc in range(2):
            P.op("act", lambda e, cc=cc: e.activation(out=ZTb[:, cc, :], in_=ZTf[:, cc, :], func=AF.Copy),
                 reads=[("ZT", cc, kf) for kf in range(4)], writes=["ZTb"])
        resC = resC + [("P2C", "fT", 0), ("P2C", "fT", 1), ("P2C", "Gcs", 0), ("P2C", "Gcs", 1)]

        KT = AR.bf(R_KT, [128, 6, NK])
        V = AR.bf(R_V, [128, NK // 128, 768])
        QT = AR.bf(R_QT, [128, 6, NQ])
        Wk = AR.bf(R_PH + 32768, [128, 8, 768]); Wks = AR.bf(R_PH + 45056, [128, 8, 768])
        Wv = AR.bf(R_QT, [128, 8, 768])
        ZTR = [("ZT", cc, kf) for cc in range(2) for kf in range(4)] + [("P2C", "tab", a, b) for a in range(2) for b in range(2)] + [("P2C", "Wf")]
        P.inherit("KT", W1R + ZTR)
        P.inherit("V", W1R + ZTR)
        for r in [("P2A", "Wk"), ("P2A", "Wks")]:
            P.inherit(r, p1res)
        P.inherit(("P2A", "Wv"), W1R)
        dma("pool", Wk[:, :, :], w_inv[:, :, 768:1536], writes=[("P2A", "Wk")])
        dma("pool", Wks[:, :, :], w_swv[:, :, 768:1536], writes=[("P2A", "Wks")])
        dma("pool", Wv[:, :, :], w_inv[:, :, 1536:2304], writes=[("P2A", "Wv")])
        resA = proj_pass("P2A", list(range(NK // TB)), p1res + resC,
                         dict(rope=True, k=True, Wk=Wk, Wks=Wks, Wv=Wv, KT=KT, V=V))

        Wq = AR.bf(R_PH + 32768, [128, 8, 768]); Wqs = AR.bf(R_PH + 45056, [128, 8, 768])
        for hh in range(6):
            P.inherit(("P2B", "Wq", hh), [("P2A", "Wk")])
            P.inherit(("P2B", "Wqs", hh), [("P2A", "Wks")])
        QTR = [("QT", h, qb) for h in range(6) for qb in range(4)]
        for r in QTR:
            P.inherit(r, [("P2A", "Wv")])
        for hh in range(6):
            dma("pool", Wq[:, :, hh * 128:(hh + 1) * 128], w_inv[:, :, hh * 128:(hh + 1) * 128], writes=[("P2B", "Wq", hh)])
            dma("pool", Wqs[:, :, hh * 128:(hh + 1) * 128], w_swv[:, :, hh * 128:(hh + 1) * 128], writes=[("P2B", "Wqs", hh)])
        resB = proj_pass("P2B", list(range(NQ // TB)), resA, dict(rope=True, q=True, Wq=Wq, Wqs=Wqs, QT=QT))

        Wo = AR.bf(R_PH, [128, 8, D])
        NPT = 4
        PT = [AR.bf(R_PH + 16384 + 2048 * i, [128, 1024]) for i in range(NPT)]
        PS = [AR.bf(R_PH + 24576 + 2048 * i, [128, 1024]) for i in range(2)]
        ev = [AR.f32(R_PH + 28672 + 2048 * i, [128, 512]) for i in range(6)]
        fmT = AR.bf(R_PH + 40960, [128, 2, NQ])
        prevw = resB + [("P2B", "Wq", hh) for hh in range(6)] + [("P2B", "Wqs", hh) for hh in range(6)]
        P.inherit("Wo", prevw)
        for i in range(NPT):
            P.inherit(("PT", i), prevw)
        for i in range(2):
            P.inherit(("PS", i), prevw)
        for i in range(6):
            P.inherit(("ev", i), prevw)
        P.inherit("fmT", prevw)
        dma("pool", Wo[:, :, :], w_out.rearrange("(j p) c -> p j c", p=128), writes=["Wo"])

        it = 0
        pending = []
        for h in range(6):
            for qb in range(4):
                q0 = qb * 512

                def qk(kc, h=h, q0=q0):
                    b0 = 2 * (kc % 2)
                    P.op("pe", lambda e: e.matmul(bank(b0), lhsT=KT[0:64, h, kc * 128:(kc + 1) * 128], rhs=QT[0:64, h, q0:q0 + 512],
                                                  start=True, stop=True), reads=["KT", ("QT", h, q0 // 512)], writes=[("bank", b0)])
                    P.op("pe", lambda e: e.matmul(bank(b0 + 1), lhsT=KT[64:128, h, kc * 128:(kc + 1) * 128], rhs=QT[64:128, h, q0:q0 + 512],
                                                  start=True, stop=True, tile_position=(64, 0)), reads=["KT", ("QT", h, q0 // 512)], writes=[("bank", b0 + 1)])
                    P.op("act", lambda e: e.activation(out=PT[kc % NPT][:], in_=pp[b0 // 2][:, :], func=AF.Exp, scale=0.125),
                         reads=[("bank", b0), ("bank", b0 + 1)], writes=[("PT", kc % NPT)])

                nkc = NK // 128

                def av(kc, h=h):
                    pt = PT[kc % NPT]
                    st, sp_ = (kc == 0), (kc == nkc - 1)
                    P.op("pe", lambda e: e.matmul(bank(4), lhsT=V[:, kc, h * 128:(h + 1) * 128], rhs=pt[:, 0:512], start=st, stop=sp_),
                         reads=["V", ("PT", kc % NPT)], writes=[("bank", 4)])
                    P.op("pe", lambda e: e.matmul(bank(5), lhsT=V[:, kc, h * 128:(h + 1) * 128], rhs=pt[:, 512:1024], start=st, stop=sp_),
                         reads=["V", ("PT", kc % NPT)], writes=[("bank", 5)])

                def pairadd(p):
                    a_, b_ = PT[(2 * p) % NPT], PT[(2 * p + 1) % NPT]
                    P.op("dve", lambda e: e.tensor_tensor(out=PS[p % 2][:], in0=a_[:], in1=b_[:], op=ALU.add),
                         reads=[("PT", (2 * p) % NPT), ("PT", (2 * p + 1) % NPT)], writes=[("PS", p % 2)])

                def sums(p):
                    st, sp_ = (p == 0), (p == nkc // 2 - 1)
                    P.op("pe", lambda e: e.matmul(bank(6), lhsT=ones[:], rhs=PS[p % 2][:, 0:512], start=st, stop=sp_),
                         reads=["ones", ("PS", p % 2)], writes=[("bank", 6)])
                    P.op("pe", lambda e: e.matmul(bank(7), lhsT=ones[:], rhs=PS[p % 2][:, 512:1024], start=st, stop=sp_),
                         reads=["ones", ("PS", p % 2)], writes=[("bank", 7)])

                qk(0)
                qk(1)
                for kc in range(nkc):
                    if kc + 2 < nkc:
                        qk(kc + 2)
                    av(kc)
                    if kc % 2 == 1:
                        pairadd(kc // 2)
                        if pending:
                            pending.pop(0)()
                        if kc >= 3:
                            sums(kc // 2 - 1)
                sums(nkc // 2 - 1)
                while pending:
                    pending.pop(0)()
                c0, c1, c2, c3 = ev[0], ev[1], ev[2], ev[3]
                P.op("dve", lambda e, c0=c0: e.tensor_copy(out=c0[:], in_=bank(4)), reads=[("bank", 4)], writes=[("ev", 0)])
                P.op("dve", lambda e, c2=c2: e.tensor_copy(out=c2[:], in_=bank(6)), reads=[("bank", 6)], writes=[("ev", 2)])
                P.op("dve", lambda e, c1=c1: e.tensor_copy(out=c1[:], in_=bank(5)), reads=[("bank", 5)], writes=[("ev", 1)])
                P.op("dve", lambda e, c3=c3: e.tensor_copy(out=c3[:], in_=bank(7)), reads=[("bank", 7)], writes=[("ev", 3)])
                def mk_pending(h=h, qb=qb, q0=q0, c0=c0, c1=c1, c2=c2, c3=c3):
                    out = []
                    for (cs, ci) in ((c2, 2), (c3, 3)):
                        for qq in range(4):
                            out.append(lambda cs=cs, ci=ci, qq=qq: P.op(
                                "dve", lambda e: e.reciprocal(out=cs[:, qq * 128:(qq + 1) * 128], in_=cs[:, qq * 128:(qq + 1) * 128]),
                                reads=[("ev", ci)], writes=[("ev", ci)]))
                    out.append(lambda: P.op("dve", lambda e: e.tensor_tensor(out=c0[:], in0=c0[:], in1=c2[:], op=ALU.mult),
                                            reads=[("ev", 0), ("ev", 2)], writes=[("ev", 0)]))
                    out.append(lambda: P.op("dve", lambda e: e.tensor_tensor(out=c1[:], in0=c1[:], in1=c3[:], op=ALU.mult),
                                            reads=[("ev", 1), ("ev", 3)], writes=[("ev", 1)]))
                    out.append(lambda: P.op("dve", lambda e: e.scalar_tensor_tensor(
                        out=QT[:, h, q0:q0 + 512], in0=c1[:], scalar=neglam, in1=c0[:], op0=ALU.mult, op1=ALU.add),
                        reads=[("ev", 0), ("ev", 1), "neglam"], writes=[("QT", h, qb)]))
                    return out
                pending.extend(mk_pending())
                it += 1

        while pending:
            pending.pop(0)()

        lnb = [AR.f32(R_PH + 28672, [128, 1024]), AR.f32(R_PH + 28672 + 4096, [128, 1024])]
        itn = 0
        for h in range(6):
            for hq in range(2):
                q0 = hq * 1024
                sb_ = itn % 2
                qres = [("QT", h, 2 * hq), ("QT", h, 2 * hq + 1)]
                P.op("act", lambda e, h=h, q0=q0, sb_=sb_: e.activation(out=PT[sb_][:], in_=QT[:, h, q0:q0 + 1024], func=AF.Square),
                     reads=qres, writes=[("PT", sb_)])
                for t in range(2):
                    bk = 2 * sb_ + t
                    P.op("pe", lambda e, sb_=sb_, bk=bk, t=t: e.matmul(bank(bk), lhsT=ones[:], rhs=PT[sb_][:, t * 512:(t + 1) * 512], start=True, stop=True),
                         reads=[("PT", sb_), "ones"], writes=[("bank", bk)])
                P.op("act", lambda e, sb_=sb_: e.activation(out=lnb[sb_][:], in_=pp[sb_][:, :], func=AF.Ln,
                                                            scale=1.0 / 128, bias=EPS),
                     reads=[("bank", 2 * sb_), ("bank", 2 * sb_ + 1)], writes=[("ev", 2 * sb_), ("ev", 2 * sb_ + 1)])
                P.op("act", lambda e, sb_=sb_: e.activation(out=lnb[sb_][:], in_=lnb[sb_][:], func=AF.Exp, scale=-0.5),
                     reads=[("ev", 2 * sb_), ("ev", 2 * sb_ + 1)], writes=[("ev", 2 * sb_), ("ev", 2 * sb_ + 1)])
                P.op("dve", lambda e, h=h, q0=q0, sb_=sb_: e.scalar_tensor_tensor(
                    out=QT[:, h, q0:q0 + 1024], in0=QT[:, h, q0:q0 + 1024], scalar=sg08, in1=lnb[sb_][:], op0=ALU.mult, op1=ALU.mult),
                    reads=qres + ["sg08", ("ev", 2 * sb_), ("ev", 2 * sb_ + 1)], writes=qres)
                itn += 1

        for qb in range(4):
            for oc in range(2):
                fb = 4 + (qb * 2 + oc) % 2
                for cc in range(2):
                    P.op("pe", lambda e, qb=qb, oc=oc, cc=cc, fb=fb: e.matmul(
                        bank(fb), lhsT=wf_t[:, cc, oc * 128:(oc + 1) * 128], rhs=ZTb[:, cc, qb * 512:(qb + 1) * 512],
                        start=(cc == 0), stop=(cc == 1)), reads=["wf", "ZTb"], writes=[("bank", fb)])
                P.op("act", lambda e, qb=qb, oc=oc, fb=fb: e.activation(out=fmT[:, oc, qb * 512:(qb + 1) * 512], in_=bank(fb), func=AF.Copy),
                     reads=[("bank", fb)], writes=["fmT"])

        xm = [AR.f32(R_PH + 49152, [128, 8, TB]), AR.f32(R_PH + 57344, [128, 8, TB])]
        P.inherit(("xm", 0), prevw)
        P.inherit(("xm", 1), prevw)
        nbm = NQ // TB

        def mload(n):
            dma("sp", xm[n % 2][:, :, :], scr[:, n * TB:(n + 1) * TB].rearrange("(j p) t -> p j t", p=128),
                reads=[("scr", n)], writes=[("xm", n % 2)])

        mload(0)
        mload(1)
        for n in range(nbm):
            t0 = n * TB
            for dc in range(8):
                mb = dc % 4
                for ic in range(8):
                    rhs = QT[:, ic, t0:t0 + TB] if ic < 6 else fmT[:, ic - 6, t0:t0 + TB]
                    P.op("pe", lambda e, dc=dc, ic=ic, mb=mb, rhs=rhs: e.matmul(bank(mb, TB), lhsT=Wo[:, ic, dc * 128:(dc + 1) * 128], rhs=rhs,
                                                                                start=(ic == 0), stop=(ic == 7)),
                         reads=["Wo", (("QT", ic, t0 // 512) if ic < 6 else "fmT")], writes=[("bank", mb)])
                P.op("dve", lambda e, dc=dc, mb=mb, n=n: e.scalar_tensor_tensor(
                    out=xm[n % 2][:, dc, :], in0=bank(mb, TB), scalar=sc[:, 10, dc:dc + 1], in1=xm[n % 2][:, dc, :], op0=ALU.mult, op1=ALU.add),
                    reads=[("bank", mb), ("sc", 10), ("xm", n % 2)], writes=[("xm", n % 2)])
            dma("sp", scr[:, t0:t0 + TB].rearrange("(j p) t -> p j t", p=128), xm[n % 2][:, :, :],
                reads=[("xm", n % 2)], writes=[("scr", n)])
            if n + 2 < nbm:
                mload(n + 2)

        attn_res = ["KT", "V", "ZTb", "Wo", "fmT", ("xm", 0), ("xm", 1)] + [("PT", i) for i in range(NPT)] + [("PS", 0), ("PS", 1)] + [("ev", i) for i in range(6)] + prevw + QTR

        W2 = load_ffn_weights("W2", w2g, w2u, w2d,
                              lambda nm, pr: {"g": ["KT"], "u": ["KT", "V"], "d": ["V", "ZTb"] + QTR}[nm])
        for r in [("P5", "xb", 0), ("P5", "xb", 1), ("P5", "sq"), ("P5", "hT"), ("P5", "sd"), ("P5", "rstd"), ("P5", "tbuf"),
                  ("P5", "sg", 0), ("P5", "sg", 1)] + [("P5", "aT", i) for i in range(NF)]:
            P.inherit(r, attn_res)
        blocks5 = []
        for n in range(NQ // TB):
            t0 = n * TB
            blocks5.append(dict(src=scr[:, t0:t0 + TB], srcres=("scr", n), dst=outT[:, t0:t0 + TB], dstres=("out", n), A=11, B=12, G=13))
        ffn_pass("P5", "W2", W2, blocks5, final=True)
        P.op("sp", lambda e: None, reads=[("out", n) for n in range(NQ // TB)])
        P.emit()
    return nc


_NC_CACHE = {}


def _host_tables():
    bf = ml_dtypes.bfloat16
    tabs = {}
    pair = np.arange(128) % 16
    part = (np.arange(128) // 16) % 2
    axis = (np.arange(128) // 32) % 2
    inv_freq = (np.float32(10000.0) ** (-(np.arange(16, dtype=np.float32)) / np.float32(16))).astype(np.float32)
    d = np.arange(64)
    ang64 = 2.0 * np.pi * ((d[:, None] * d[None, :]) % 64) / 64.0
    BC = np.zeros((256, 256), np.float64); BS = np.zeros((256, 256), np.float64)
    for g in range(4):
        BC[g * 64:(g + 1) * 64, g * 64:(g + 1) * 64] = np.cos(ang64)
        BS[g * 64:(g + 1) * 64, g * 64:(g + 1) * 64] = np.sin(ang64)
    tabs["bcs"] = np.concatenate([BC, BS], axis=1).astype(np.float32).astype(bf)
    for half in range(2):
        nloc = np.concatenate([half * NQ + np.arange(NQ), (1 - half) * NQ + np.arange(NQ)])
        row = (nloc // 64).astype(np.float32)
        col = (nloc % 64).astype(np.float32)
        pos = np.where(axis[:, None] == 0, row[None, :], col[None, :]).astype(np.float32)
        ang = (pos * inv_freq[pair][:, None]).astype(np.float32)
        cos = np.cos(ang).astype(np.float32)
        sin = np.sin(ang).astype(np.float32)
        sgn = np.where(part == 0, -1.0, 1.0).astype(np.float32)[:, None]
        tabs[("ropec", half)] = np.ascontiguousarray(cos)
        tabs[("ropes", half)] = np.ascontiguousarray(sin * sgn)
        kk = (half * NQ + np.arange(NQ)).astype(np.int64)
        prod = (nloc.astype(np.int64)[:, None] * kk[None, :]) % NT
        a = 2.0 * np.pi * prod.astype(np.float64) / NT
        tabs[("dftc", half)] = (np.cos(a) / 512.0).astype(np.float32).astype(bf)
        tabs[("dfts", half)] = (-np.sin(a) / 512.0).astype(np.float32).astype(bf)
    return tabs


def kernel(x, c, ctx, c_ctx, w_ada, b_ada, norm1_g, ffn1_w_gate, ffn1_w_up, ffn1_w_down,
           norm_mix_g, w_in, lambda_q1, lambda_k1, lambda_q2, lambda_k2, subln_g, w_fourier,
           w_out, norm2_g, ffn2_w_gate, ffn2_w_up, ffn2_w_down, final_norm_g):
    f = lambda a: np.ascontiguousarray(np.asarray(a, dtype=np.float32))
    x = f(x); c = f(c); ctx = f(ctx); c_ctx = f(c_ctx)
    if "nc" not in _NC_CACHE:
        _NC_CACHE["nc"] = build_nc()
        _NC_CACHE["tabs"] = _host_tables()
    nc = _NC_CACHE["nc"]
    tabs = _NC_CACHE["tabs"]

    def vT(v):
        return np.ascontiguousarray(f(v).reshape(8, 128).T)

    gains = np.ascontiguousarray(np.stack([vT(norm1_g[0]), vT(norm_mix_g[0]), vT(norm2_g[0]), vT(final_norm_g)], axis=1))
    b_adaT = np.ascontiguousarray(f(b_ada[0]).reshape(72, 128).T)
    w_in0 = f(w_in[0])
    cols = np.arange(1536)
    dd = cols % 64
    partner = (cols - dd) + (dd // 32) * 32 + (1 - (dd // 16) % 2) * 16 + dd % 16
    w_in_sw = np.ascontiguousarray(w_in0[:, partner])
    lamin = np.stack([f(lambda_q1[0]), f(lambda_k1[0]), f(lambda_q2[0]), f(lambda_k2[0])], axis=0)
    lamin = np.ascontiguousarray(np.broadcast_to(lamin[None], (128, 4, 64)))
    sublnT = np.ascontiguousarray(f(subln_g[0]).reshape(128, 1))
    shared = dict(
        w_ada=f(w_ada[0]), b_adaT=b_adaT, gains=gains,
        w1g=f(ffn1_w_gate[0]), w1u=f(ffn1_w_up[0]), w1d=f(ffn1_w_down[0]),
        w2g=f(ffn2_w_gate[0]), w2u=f(ffn2_w_up[0]), w2d=f(ffn2_w_down[0]),
        w_in=w_in0, w_in_sw=w_in_sw, lamin=lamin, sublnT=sublnT, w_fo=f(w_fourier[0]),
        bcs=tabs["bcs"], w_out=f(w_out[0]),
    )
    in_maps = []
    for core in range(8):
        b, half = core // 2, core % 2
        xb = x[b]
        xloc = np.concatenate([xb[half * NQ:(half + 1) * NQ], xb[(1 - half) * NQ:(2 - half) * NQ]], axis=0)
        m = dict(shared)
        m["xT"] = np.ascontiguousarray(xloc.T)
        m["ctxT"] = np.ascontiguousarray(ctx[b].T)
        m["cc2"] = np.ascontiguousarray(np.stack([vT(c[b]), vT(c_ctx)], axis=2))
        m["ropec"] = tabs[("ropec", half)]
        m["ropes"] = tabs[("ropes", half)]
        m["dftc"] = tabs[("dftc", half)]
        m["dfts"] = tabs[("dfts", half)]
        in_maps.append(m)
    res = run_bass_kernel_spmd(nc, in_maps, core_ids=list(range(8)))
    out = np.empty((4, NT, D), np.float32)
    for core in range(8):
        b, half = core // 2, core % 2
        out[b, half * NQ:(half + 1) * NQ, :] = np.asarray(res.results[core]["outT"]).T
    return out
```

```python
from contextlib import ExitStack

import numpy as np
import ml_dtypes
import concourse.bass as bass
import concourse.mybir as mybir
from concourse.bass_utils import run_bass_kernel_spmd

F32 = mybir.dt.float32
BF16 = mybir.dt.bfloat16
AF = mybir.ActivationFunctionType
ALU = mybir.AluOpType
AX = mybir.AxisListType

D = 1024
NT = 4096
NCTX = 256
NK = NT + NCTX
NQ = 2048
DFF = 2816
NF = DFF // 128
TB = 256
EPS = 1e-6
LAMBDA_INIT = 0.2

DEBUG = False


class Prog:
    ENGS = ("pe", "act", "dve", "pool", "sp")
    GROUP = 2000
    NDMA = 20

    def __init__(self, nc):
        self.nc = nc
        self.ops = {e: [] for e in self.ENGS}
        self.last_w = {}
        self.readers = {}
        self.dma_rr = {e: 0 for e in self.ENGS}
        self.dma_last = {}
        self.dma_cnt = {}

    def _add_reader(self, res, me, is_dma):
        d = self.readers.setdefault(res, {})
        if is_dma:
            d.setdefault("dma", []).append(me)
        else:
            d[me[0]] = me

    def _reader_list(self, res):
        d = self.readers.get(res)
        if not d:
            return []
        out = []
        for k, v in d.items():
            if k == "dma":
                out.extend(v)
            else:
                out.append(v)
        return out

    def op(self, eng, fn, reads=(), writes=(), dma=False):
        idx = len(self.ops[eng])
        me = (eng, idx)
        deps = set()
        for r in reads:
            w = self.last_w.get(r)
            if w is not None:
                deps.add(w)
        for w_ in writes:
            w = self.last_w.get(w_)
            if w is not None:
                deps.add(w)
            deps.update(self._reader_list(w_))
        rec = dict(fn=fn, deps=deps, signal=False, dma=dma, sig=None)
        if dma:
            k = self.dma_rr[eng] % self.NDMA
            self.dma_rr[eng] += 1
            key = ("dma", eng, k)
            prev = self.dma_last.get(key)
            if prev is not None:
                deps.add(prev)
            self.dma_last[key] = me
            n = self.dma_cnt.get(key, 0) + 1
            self.dma_cnt[key] = n
            rec["sig"] = (key, 16 * n)
        deps.discard(me)
        if eng == "pe":
            deps = {d for d in deps if d[0] != "pe"}
        rec["deps"] = deps
        self.ops[eng].append(rec)
        for r in reads:
            self._add_reader(r, me, dma)
        for w_ in writes:
            self.last_w[w_] = me
            self.readers[w_] = {}
        return me

    def inherit(self, new, olds):
        d = self.readers.setdefault(new, {})
        for o in olds:
            w = self.last_w.get(o)
            if w is not None:
                d.setdefault("dma", []).append(w)
            for r in self._reader_list(o):
                d.setdefault("dma", []).append(r)

    def emit(self):
        nc = self.nc
        for e in self.ENGS:
            for rec in self.ops[e]:
                for (e2, i2) in rec["deps"]:
                    t = self.ops[e2][i2]
                    if not t["dma"]:
                        t["signal"] = True
        semkeys = []
        for e in self.ENGS:
            k = 0
            for rec in self.ops[e]:
                if not rec["dma"] and rec["signal"]:
                    rec["sig"] = (("eng", e, k // self.GROUP), k % self.GROUP + 1)
                    k += 1
                if rec["sig"] is not None and rec["sig"][0] not in semkeys:
                    semkeys.append(rec["sig"][0])
        with ExitStack() as es:
            sems = {}
            for k in semkeys:
                sems[k] = es.enter_context(nc.semaphore("s_" + "_".join(str(x) for x in k)))
            block = es.enter_context(nc.Block())
            handles = {"pe": block.tensor, "act": block.scalar, "dve": block.vector,
                       "pool": block.gpsimd, "sp": block.sync}
            for e in self.ENGS:
                if not self.ops[e]:
                    continue

                def body(engh, e=e):
                    waited = {}
                    for rec in self.ops[e]:
                        need = {}
                        for (e2, i2) in rec["deps"]:
                            sk, v = self.ops[e2][i2]["sig"]
                            if waited.get(sk, 0) >= v:
                                continue
                            if need.get(sk, 0) < v:
                                need[sk] = v
                        for sk, v in need.items():
                            engh.wait_ge(sems[sk], v)
                            waited[sk] = v
                        ins = rec["fn"](engh)
                        if ins is not None and rec["sig"] is not None and (rec["dma"] or rec["signal"]):
                            sk, v = rec["sig"]
                            ins.then_inc(sems[sk], 16 if rec["dma"] else 1)
                handles[e](body)


def _prod(s):
    r = 1
    for x in s:
        r *= x
    return r


class Arena:
    def __init__(self, t):
        self.t = t

    def _shape(self, v, shape):
        if len(shape) == 2:
            return v
        if len(shape) == 3:
            return v.rearrange("p (a b) -> p a b", a=shape[1])
        raise ValueError

    def f32(self, off, shape):
        n = _prod(shape[1:])
        assert off % 4 == 0
        v = self.t[:, off // 4: off // 4 + n]
        return self._shape(v, shape)

    def bf(self, off, shape):
        n = _prod(shape[1:])
        assert off % 4 == 0 and n % 2 == 0
        v = self.t[:, off // 4: off // 4 + n // 2].bitcast(BF16)
        return self._shape(v, shape)


ARENA_WORDS = 50688
R_KT = 0
R_V = 52224
R_QT = 104448
R_ZB = 129024
R_PH = 137216


def build_nc():
    nc = bass.Bass("TRN2", target_bir_lowering=False)

    def din(name, shape, dt=F32):
        return nc.dram_tensor(name, shape, dt, kind="ExternalInput").ap()

    xT = din("xT", [D, NT])
    ctxT = din("ctxT", [D, NCTX])
    cc2 = din("cc2", [128, 8, 2])
    w_ada = din("w_ada", [D, 9 * D])
    b_adaT = din("b_adaT", [128, 72])
    gains = din("gains", [128, 4, 8])
    w1g = din("w1g", [D, DFF]); w1u = din("w1u", [D, DFF]); w1d = din("w1d", [DFF, D])
    w2g = din("w2g", [D, DFF]); w2u = din("w2u", [D, DFF]); w2d = din("w2d", [DFF, D])
    w_in = din("w_in", [D, 2560])
    w_in_sw = din("w_in_sw", [D, 1536])
    lamin = din("lamin", [128, 4, 64])
    sublnT = din("sublnT", [128, 1])
    w_fo = din("w_fo", [256, 256])
    bcs = din("bcs", [256, 512], BF16)
    w_out = din("w_out", [D, D])
    ropec = din("ropec", [128, NT])
    ropes = din("ropes", [128, NT])
    dftc = din("dftc", [NT, NQ], BF16)
    dfts = din("dfts", [NT, NQ], BF16)
    outT = nc.dram_tensor("outT", [D, NQ], F32, kind="ExternalOutput").ap()
    scr = nc.dram_tensor("scr", [D, NK], F32, kind="Internal").ap()

    es = ExitStack()
    with es:
        def sb(name, shape, dt):
            return es.enter_context(nc.sbuf_tensor(name, shape, dt))

        arena_t = sb("arena", [128, ARENA_WORDS], F32)
        AR = Arena(arena_t)
        pp = [es.enter_context(nc.psum_tensor("ps%d" % i, [128, 1024], F32)) for i in range(4)]
        ones = sb("ones", [128, 128], BF16)
        s2 = sb("s2", [128, 8, 2], F32)
        s2b = sb("s2b", [128, 8, 2], BF16)
        badaT = sb("badaT", [128, 72], F32)
        gn = sb("gn", [128, 4, 8], F32)
        modT = sb("modT", [128, 72, 2], F32)
        sc = sb("sc", [128, 14, 8], F32)
        lam_t = sb("lam_t", [128, 4, 64], F32)
        lam_w = sb("lam_w", [128, 8], F32)
        sgt = sb("sgt", [128, 2], F32)
        wf_t = sb("wf_t", [128, 2, 256], BF16)
        bcs_t = sb("bcs_t", [128, 2, 512], BF16)

        P = Prog(nc)

        def bank(b, n=512):
            return pp[b // 2][:, (b % 2) * 512: (b % 2) * 512 + n]

        def dma(eng, out, in_, reads=(), writes=()):
            return P.op(eng, lambda e: e.dma_start(out=out, in_=in_), reads=reads, writes=writes, dma=True)

        P.op("dve", lambda e: e.memset(ones[:], 1.0), writes=["ones"])
        dma("sp", s2[:], cc2, writes=["s2"])
        dma("sp", badaT[:], b_adaT, writes=["badaT"])
        dma("sp", gn[:], gains, writes=["gn"])
        dma("sp", lam_t[:], lamin, writes=["lam_t"])
        dma("sp", sgt[:, 0:1], sublnT, writes=["sgt0"])
        dma("pool", wf_t[:], w_fo.rearrange("(c p) n -> p c n", p=128), writes=["wf"])
        dma("sp", bcs_t[:], bcs.rearrange("(c p) n -> p c n", p=128), writes=["bcs"])
        P.op("act", lambda e: e.activation(out=s2[:], in_=s2[:], func=AF.Silu), reads=["s2"], writes=["s2"])

        def load_ffn_weights(tag, wg, wu, wd, olds):
            Wg = AR.bf(0, [128, 8, DFF]); Wu = AR.bf(45056, [128, 8, DFF]); Wd = AR.bf(90112, [128, NF, D])
            wgv = wg.rearrange("(j p) f -> p j f", p=128)
            wuv = wu.rearrange("(j p) f -> p j f", p=128)
            wdv = wd.rearrange("(i p) d -> p i d", p=128)
            for pr in range(11):
                for nm, Wt, src in (("g", Wg, wgv), ("u", Wu, wuv)):
                    res = (tag, nm, pr)
                    P.inherit(res, olds(nm, pr))
                    dma("pool", Wt[:, :, pr * 256:(pr + 1) * 256], src[:, :, pr * 256:(pr + 1) * 256], writes=[res])
            for pr in range(11):
                res = (tag, "d", pr)
                P.inherit(res, olds("d", pr))
                dma("pool", Wd[:, 2 * pr:2 * pr + 2, :], wdv[:, 2 * pr:2 * pr + 2, :], writes=[res])
            return Wg, Wu, Wd

        ST0 = 183296
        stg = [AR.bf(ST0, [128, 8, 512]), AR.bf(ST0 + 8192, [128, 8, 512])]
        wav = w_ada.rearrange("(j p) n -> p j n", p=128)
        P.op("dve", lambda e: e.tensor_copy(out=s2b[:], in_=s2[:]), reads=["s2"], writes=["s2b"])

        def ada_load(nb):
            dma("pool", stg[nb % 2][:, :, :], wav[:, :, nb * 512:(nb + 1) * 512], writes=[("stg", nb % 2)])

        def ada_compute(nb):
            sbuf = stg[nb % 2]
            k = nb // 2
            for q in range(4):
                n = nb * 4 + q
                for j in range(8):
                    P.op("pe", lambda e, sbuf=sbuf, q=q, j=j, n=n: e.matmul(
                        pp[3][:, 512 + 2 * n:512 + 2 * n + 2], lhsT=sbuf[:, j, q * 128:(q + 1) * 128], rhs=s2b[:, j, :],
                        start=(j == 0), stop=(j == 7)), reads=[("stg", nb % 2), "s2b"], writes=["ps_ada"])
            if nb % 2 == 1:
                P.op("dve", lambda e, k=k: e.tensor_tensor(
                    out=modT[:, 8 * k:8 * k + 8, :], in0=pp[3][:, 512 + 16 * k:512 + 16 * k + 16].rearrange("p (n m) -> p n m", m=2),
                    in1=badaT[:, 8 * k:8 * k + 8].unsqueeze(2).to_broadcast([128, 8, 2]), op=ALU.add),
                    reads=["ps_ada", "badaT"], writes=[("modT", k)])

        ada_load(0)
        ada_load(1)
        for nb in range(6):
            ada_compute(nb)
            if nb + 2 < 6:
                ada_load(nb + 2)
        Wg, Wu, Wd = load_ffn_weights("W1", w1g, w1u, w1d, lambda nm, pr: [])

        def mod(k, m):
            return modT[:, k * 8:(k + 1) * 8, m]

        def mk_scale(idx, gi, k, m):
            P.op("dve", lambda e: e.scalar_tensor_tensor(out=sc[:, idx, :], in0=mod(k, m), scalar=1.0, in1=gn[:, gi, :],
                                                         op0=ALU.add, op1=ALU.mult), reads=[("modT", k), "gn"], writes=[("sc", idx)])

        def mk_copy(idx, k, m, f):
            P.op("dve", lambda e: e.tensor_scalar(out=sc[:, idx, :], in0=mod(k, m), scalar1=f, scalar2=None, op0=ALU.mult),
                 reads=[("modT", k)], writes=[("sc", idx)])

        mk_scale(0, 0, 1, 0); mk_copy(1, 0, 0, 1.0); mk_copy(2, 2, 0, 0.5)
        mk_scale(3, 0, 1, 1); mk_copy(4, 0, 1, 1.0); mk_copy(5, 2, 1, 0.5)

        def ada_hook(n):
            if n == 0:
                ada_load(6)
                ada_load(7)
                return
            nb = 5 + n
            ada_compute(nb)
            if nb + 2 < 18:
                ada_load(nb + 2)
            if nb == 17:
                mk_scale(6, 1, 4, 0); mk_copy(7, 3, 0, 1.0)
                mk_scale(8, 1, 4, 1); mk_copy(9, 3, 1, 1.0)
                mk_copy(10, 5, 0, 1.0)
                mk_scale(11, 2, 7, 0); mk_copy(12, 6, 0, 1.0); mk_copy(13, 8, 0, 0.5)
        SCR = [("sc", i) for i in range(14)]

        P.op("dve", lambda e: e.tensor_tensor(out=lam_t[:, 0, :], in0=lam_t[:, 0, :], in1=lam_t[:, 1, :], op=ALU.mult),
             reads=["lam_t"], writes=["lam_t"])
        P.op("dve", lambda e: e.tensor_tensor(out=lam_t[:, 2, :], in0=lam_t[:, 2, :], in1=lam_t[:, 3, :], op=ALU.mult),
             reads=["lam_t"], writes=["lam_t"])
        P.op("dve", lambda e: e.reduce_sum(out=lam_w[:, 0:1], in_=lam_t[:, 0, :], axis=AX.X), reads=["lam_t"], writes=["lam_w"])
        P.op("dve", lambda e: e.reduce_sum(out=lam_w[:, 1:2], in_=lam_t[:, 2, :], axis=AX.X), reads=["lam_w", "lam_t"], writes=["lam_w"])
        P.op("act", lambda e: e.activation(out=lam_w[:, 2:4], in_=lam_w[:, 0:2], func=AF.Exp), reads=["lam_w"], writes=["lam_w"])
        P.op("dve", lambda e: e.tensor_tensor(out=lam_w[:, 4:5], in0=lam_w[:, 3:4], in1=lam_w[:, 2:3], op=ALU.subtract),
             reads=["lam_w"], writes=["lam_w"])
        P.op("dve", lambda e: e.tensor_scalar(out=lam_w[:, 5:6], in0=lam_w[:, 4:5], scalar1=-LAMBDA_INIT, scalar2=None, op0=ALU.add),
             reads=["lam_w"], writes=["neglam"])
        neglam = lam_w[:, 5:6]
        P.op("dve", lambda e: e.tensor_scalar(out=sgt[:, 1:2], in0=sgt[:, 0:1], scalar1=1.0 - LAMBDA_INIT, scalar2=None, op0=ALU.mult),
             reads=["sgt0"], writes=["sg08"])
        sg08 = sgt[:, 1:2]

        def work(base):
            w = {}
            w["xb"] = [AR.f32(base, [128, 8, TB]), AR.f32(base + 8192, [128, 8, TB])]
            w["sq"] = AR.bf(base + 16384, [128, 8, TB])
            w["hT"] = AR.bf(base + 20480, [128, 8, TB])
            w["sd"] = AR.f32(base + 24576, [128, TB])
            w["rstd"] = AR.f32(base + 25600, [128, TB])
            return w

        def norm_stage_pre(tag, w, xbuf, xres, n):
            P.op("act", lambda e: e.activation(out=w["sq"][:], in_=xbuf[:], func=AF.Square), reads=[xres], writes=[(tag, "sq")])

        def norm_stage_pe(tag, w, sbank):
            for j in range(8):
                P.op("pe", lambda e, j=j: e.matmul(bank(sbank, TB), lhsT=ones[:], rhs=w["sq"][:, j, :], start=(j == 0), stop=(j == 7)),
                     reads=[(tag, "sq"), "ones"], writes=[("bank", sbank)])

        def norm_stage_post(tag, w, xbuf, xres, tmp, tmpres, Ai, Bi, sbank, hres, hT=None):
            hT = w["hT"] if hT is None else hT
            P.op("act", lambda e: e.activation(out=w["sd"][:], in_=bank(sbank, TB), func=AF.Sqrt, scale=1.0 / D, bias=EPS),
                 reads=[("bank", sbank)], writes=[(tag, "sd")])
            P.op("dve", lambda e: e.reciprocal(out=w["rstd"][:], in_=w["sd"][:]), reads=[(tag, "sd")], writes=[(tag, "rstd")])
            P.op("dve", lambda e: e.tensor_tensor(out=tmp[:], in0=xbuf[:], in1=w["rstd"][:].unsqueeze(1).to_broadcast([128, 8, TB]),
                                                  op=ALU.mult), reads=[xres, (tag, "rstd")], writes=[tmpres])
            for j in range(8):
                if Bi is None:
                    P.op("act", lambda e, j=j: e.activation(out=hT[:, j, :], in_=tmp[:, j, :], func=AF.Identity,
                                                            scale=sc[:, Ai, j:j + 1]),
                         reads=[tmpres, ("sc", Ai)], writes=[hres])
                else:
                    P.op("act", lambda e, j=j: e.activation(out=hT[:, j, :], in_=tmp[:, j, :], func=AF.Identity,
                                                            scale=sc[:, Ai, j:j + 1], bias=sc[:, Bi, j:j + 1]),
                         reads=[tmpres, ("sc", Ai), ("sc", Bi)], writes=[hres])

        def ffn_pass(tag, Wtag, Wts, blocks, final=None, after_block=None, byj=False):
            Wg, Wu, Wd = Wts
            base = 135168
            w = work(base)
            aT = AR.bf(base + 26624, [128, NF, TB])
            sg = [AR.f32(base + 37888, [128, TB]), AR.f32(base + 38912, [128, TB])]
            tbuf = AR.f32(base + 39936, [128, 8, TB])
            nb = len(blocks)
            WR = [(Wtag, nm, pr) for nm in "gud" for pr in range(11)]

            def xres(n):
                return (tag, "xb", n % 2)

            def load(n):
                b = blocks[n]
                dma("sp", w["xb"][n % 2][:, :, :], b["src"].rearrange("(j p) t -> p j t", p=128),
                    reads=[b["srcres"]], writes=[xres(n)])

            def pre(n):
                norm_stage_pre(tag, w, w["xb"][n % 2], xres(n), n)

            def stats_and_mod(n):
                b = blocks[n]
                norm_stage_pe(tag, w, 6)
                norm_stage_post(tag, w, w["xb"][n % 2], xres(n), tbuf, (tag, "tbuf"), b["A"], b["B"], 6, (tag, "hT"))

            def gu(n, mid_hook, early_hook=None):
                for i in range(NF):
                    if i == 3 and early_hook is not None:
                        early_hook()
                    gb, ub = i % 2, 2 + i % 2
                    for j in range(8):
                        P.op("pe", lambda e, i=i, j=j, gb=gb: e.matmul(bank(gb, TB), lhsT=Wg[:, j, i * 128:(i + 1) * 128],
                                                                        rhs=w["hT"][:, j, :], start=(j == 0), stop=(j == 7)),
                             reads=[(Wtag, "g", j if byj else i // 2), (tag, "hT")], writes=[("bank", gb)])
                    for j in range(8):
                        P.op("pe", lambda e, i=i, j=j, ub=ub: e.matmul(bank(ub, TB), lhsT=Wu[:, j, i * 128:(i + 1) * 128],
                                                                        rhs=w["hT"][:, j, :], start=(j == 0), stop=(j == 7)),
                             reads=[(Wtag, "u", j if byj else i // 2), (tag, "hT")], writes=[("bank", ub)])
                    P.op("act", lambda e, i=i, gb=gb: e.activation(out=sg[i % 2][:], in_=bank(gb, TB), func=AF.Silu),
                         reads=[("bank", gb)], writes=[(tag, "sg", i % 2)])
                    P.op("dve", lambda e, i=i, ub=ub: e.tensor_tensor(out=aT[:, i, :], in0=sg[i % 2][:], in1=bank(ub, TB), op=ALU.mult),
                         reads=[(tag, "sg", i % 2), ("bank", ub)], writes=[(tag, "aT", i)])
                    if i == 10 and mid_hook is not None:
                        mid_hook()

            def down(n):
                b = blocks[n]
                xb = w["xb"][n % 2]
                for dc in range(8):
                    db = 4 + dc % 2
                    for i in range(NF):
                        P.op("pe", lambda e, i=i, dc=dc, db=db: e.matmul(bank(db, TB), lhsT=Wd[:, i, dc * 128:(dc + 1) * 128],
                                                                          rhs=aT[:, i, :], start=(i == 0), stop=(i == NF - 1)),
                             reads=[(Wtag, "d", i // 2), (tag, "aT", i)], writes=[("bank", db)])
                    P.op("dve", lambda e, dc=dc, db=db, xb=xb, G=b["G"]: e.scalar_tensor_tensor(
                        out=xb[:, dc, :], in0=bank(db, TB), scalar=sc[:, G, dc:dc + 1], in1=xb[:, dc, :], op0=ALU.mult, op1=ALU.add),
                        reads=[("bank", db), ("sc", b["G"]), xres(n)], writes=[xres(n)])

            def fin_pre(n):
                xb = w["xb"][n % 2]
                P.op("act", lambda e: e.activation(out=w["sq"][:], in_=xb[:], func=AF.Square), reads=[xres(n)], writes=[(tag, "sq")])

            def fin_pe(n):
                for j in range(8):
                    P.op("pe", lambda e, j=j: e.matmul(bank(7, TB), lhsT=ones[:], rhs=w["sq"][:, j, :], start=(j == 0), stop=(j == 7)),
                         reads=[(tag, "sq"), "ones"], writes=[("bank", 7)])

            def fin_post(n):
                xb = w["xb"][n % 2]
                P.op("act", lambda e: e.activation(out=w["sd"][:], in_=bank(7, TB), func=AF.Sqrt, scale=1.0 / D, bias=EPS),
                     reads=[("bank", 7)], writes=[(tag, "sd")])
                P.op("dve", lambda e: e.reciprocal(out=w["rstd"][:], in_=w["sd"][:]), reads=[(tag, "sd")], writes=[(tag, "rstd")])
                P.op("dve", lambda e: e.tensor_tensor(out=xb[:], in0=xb[:], in1=w["rstd"][:].unsqueeze(1).to_broadcast([128, 8, TB]),
                                                      op=ALU.mult), reads=[xres(n), (tag, "rstd")], writes=[xres(n)])
                for j in range(8):
                    P.op("act", lambda e, j=j: e.activation(out=xb[:, j, :], in_=xb[:, j, :], func=AF.Identity, scale=gn[:, 3, j:j + 1]),
                         reads=[xres(n), "gn"], writes=[xres(n)])

            def store(n):
                b = blocks[n]
                dma("sp", b["dst"].rearrange("(j p) t -> p j t", p=128), w["xb"][n % 2][:, :, :],
                    reads=[xres(n)], writes=[b["dstres"]])

            load(0)
            if nb > 1:
                load(1)
            pre(0)
            stats_and_mod(0)
            for n in range(nb):
                def hook(n=n):
                    if n + 1 < nb:
                        pre(n + 1)

                def early(n=n):
                    if final and n >= 1:
                        fin_pe(n - 1)
                        fin_post(n - 1)
                        store(n - 1)
                        if n + 1 < nb:
                            load(n + 1)
                gu(n, hook, early)
                if n + 1 < nb:
                    stats_and_mod(n + 1)
                down(n)
                if final:
                    fin_pre(n)
                    if n == nb - 1:
                        fin_pe(n)
                        fin_post(n)
                        store(n)
                else:
                    store(n)
                    if n + 2 < nb:
                        load(n + 2)
                if after_block and n in after_block:
                    after_block[n]()
            return [(tag, "xb", 0), (tag, "xb", 1), (tag, "sq"), (tag, "hT"), (tag, "sd"), (tag, "rstd"), (tag, "tbuf"),
                    (tag, "sg", 0), (tag, "sg", 1)] + [(tag, "aT", i) for i in range(NF)]

        blocks1 = []
        for n in range(NK // TB):
            t0 = n * TB
            if t0 < NT:
                src = xT[:, t0:t0 + TB]; A, B, G = 0, 1, 2
            else:
                src = ctxT[:, t0 - NT:t0 - NT + TB]; A, B, G = 3, 4, 5
            blocks1.append(dict(src=src, srcres=("in", n), dst=scr[:, t0:t0 + TB], dstres=("scr", n), A=A, B=B, G=G))
        p1res = ffn_pass("P1", "W1", (Wg, Wu, Wd), blocks1,
                         after_block={n: (lambda n=n: ada_hook(n)) for n in range(13)})
        p1res = p1res + [("stg", 0), ("stg", 1)]
        W1R = [("W1", nm, pr) for nm in "gud" for pr in range(11)]

        def proj_pass(tag, blist, olds_work, do):
            w = work(R_PH)
            hT2 = [w["hT"], AR.bf(R_PH + 57344, [128, 8, TB])]
            cosb = [AR.f32(R_PH + 26624, [128, TB]), AR.f32(R_PH + 27648, [128, TB])]
            sinb = [AR.f32(R_PH + 28672, [128, TB]), AR.f32(R_PH + 29696, [128, TB])]
            t1 = AR.f32(R_PH + 30720, [128, TB]); t2 = AR.f32(R_PH + 31744, [128, TB])
            wres = [(tag, "xb", 0), (tag, "xb", 1),
                    (tag, "sq"), (tag, "hT", 0), (tag, "hT", 1), (tag, "sd"), (tag, "rstd"),
                    (tag, "cos", 0), (tag, "cos", 1), (tag, "sin", 0), (tag, "sin", 1), (tag, "t1"), (tag, "t2")]
            for r in wres:
                P.inherit(r, olds_work)
            nb = len(blist)

            def xres(i):
                return (tag, "xb", i % 2)

            def hres(i):
                return (tag, "hT", i % 2)

            def load(i):
                n = blist[i]
                dma("sp", w["xb"][i % 2][:, :, :], scr[:, n * TB:(n + 1) * TB].rearrange("(j p) t -> p j t", p=128),
                    reads=[("scr", n)], writes=[xres(i)])
                if do["rope"] and n < NT // TB:
                    dma("sp", cosb[i % 2][:], ropec[:, n * TB:(n + 1) * TB], writes=[(tag, "cos", i % 2)])
                    dma("sp", sinb[i % 2][:], ropes[:, n * TB:(n + 1) * TB], writes=[(tag, "sin", i % 2)])
                if do.get("f"):
                    tabs = do["tabs"]
                    dma("sp", tabs[i % 2][:, :, 0, :], dftc[n * TB:(n + 1) * TB, :].rearrange("(s p) k -> p s k", p=128),
                        writes=[(tag, "tab", i % 2, 0)])
                    dma("sp", tabs[i % 2][:, :, 1, :], dfts[n * TB:(n + 1) * TB, :].rearrange("(s p) k -> p s k", p=128),
                        writes=[(tag, "tab", i % 2, 1)])

            def prep_pre(i):
                norm_stage_pre(tag, w, w["xb"][i % 2], xres(i), i)

            def prep_pe(i):
                norm_stage_pe(tag, w, 6)

            def prep_ab(i):
                P.op("act", lambda e: e.activation(out=w["sd"][:], in_=bank(6, TB), func=AF.Ln, scale=1.0 / D, bias=EPS),
                     reads=[("bank", 6)], writes=[(tag, "sd")])
                P.op("act", lambda e: e.activation(out=w["rstd"][:], in_=w["sd"][:], func=AF.Exp, scale=-0.5),
                     reads=[(tag, "sd")], writes=[(tag, "rstd")])

            def prep_chunk(i, j):
                n = blist[i]
                ctx = n >= NT // TB
                A, B = (8, 9) if ctx else (6, 7)
                xbuf = w["xb"][i % 2]
                hT = hT2[i % 2]
                if j == 0:
                    P.op("dve", lambda e: e.tensor_tensor(out=xbuf[:], in0=xbuf[:], in1=w["rstd"][:].unsqueeze(1).to_broadcast([128, 8, TB]),
                                                          op=ALU.mult), reads=[xres(i), (tag, "rstd")], writes=[xres(i)])
                P.op("act", lambda e: e.activation(out=hT[:, j, :], in_=xbuf[:, j, :], func=AF.Identity,
                                                   scale=sc[:, A, j:j + 1], bias=sc[:, B, j:j + 1]),
                     reads=[xres(i), ("sc", A), ("sc", B)], writes=[hres(i)])

            def rope_unit(i, h, Wm, Ws, wr, wsr, dstT, dresf, tok0):
                n = blist[i]
                ctx = n >= NT // TB
                hT = hT2[i % 2]
                kb, sbk = h % 2, 2 + h % 2
                for j in range(8):
                    P.op("pe", lambda e, j=j: e.matmul(bank(kb, TB), lhsT=Wm[:, j, h * 128:(h + 1) * 128],
                                                       rhs=hT[:, j, :], start=(j == 0), stop=(j == 7)),
                         reads=[wr, hres(i)], writes=[("bank", kb)])
                if ctx:
                    P.op("act", lambda e: e.activation(out=dstT[:, h, tok0:tok0 + TB], in_=bank(kb, TB), func=AF.Copy),
                         reads=[("bank", kb)], writes=[dresf(h, tok0)])
                    return
                for j in range(8):
                    P.op("pe", lambda e, j=j: e.matmul(bank(sbk, TB), lhsT=Ws[:, j, h * 128:(h + 1) * 128],
                                                       rhs=hT[:, j, :], start=(j == 0), stop=(j == 7)),
                         reads=[wsr, hres(i)], writes=[("bank", sbk)])
                P.op("dve", lambda e: e.tensor_tensor(out=t1[:], in0=bank(kb, TB), in1=cosb[i % 2][:], op=ALU.mult),
                     reads=[("bank", kb), (tag, "cos", i % 2)], writes=[(tag, "t1")])
                P.op("dve", lambda e: e.tensor_tensor(out=t2[:], in0=bank(sbk, TB), in1=sinb[i % 2][:], op=ALU.mult),
                     reads=[("bank", sbk), (tag, "sin", i % 2)], writes=[(tag, "t2")])
                P.op("dve", lambda e: e.tensor_tensor(out=dstT[:, h, tok0:tok0 + TB], in0=t1[:], in1=t2[:], op=ALU.add),
                     reads=[(tag, "t1"), (tag, "t2")], writes=[dresf(h, tok0)])

            def v_unit(i, s):
                n = blist[i]
                hT = hT2[i % 2]
                kc = n * (TB // 128) + s
                for (c0, cn, vb) in ((0, 512, 4), (512, 256, 5)):
                    for j in range(8):
                        P.op("pe", lambda e, j=j, c0=c0, cn=cn, vb=vb: e.matmul(
                            bank(vb, cn), lhsT=hT[:, j, s * 128:(s + 1) * 128], rhs=do["Wv"][:, j, c0:c0 + cn],
                            start=(j == 0), stop=(j == 7)), reads=[(tag, "Wv"), hres(i)], writes=[("bank", vb)])
                P.op("act", lambda e: e.activation(out=do["V"][:, 0:4, kc, :], in_=bank(4, 512).rearrange("p (a b) -> p a b", a=4),
                                                   func=AF.Copy),
                     reads=[("bank", 4)], writes=[("V", hh) for hh in range(4)])
                P.op("act", lambda e: e.activation(out=do["V"][:, 4:6, kc, :], in_=bank(5, 256).rearrange("p (a b) -> p a b", a=2),
                                                   func=AF.Copy),
                     reads=[("bank", 5)], writes=[("V", 4), ("V", 5)])

            def f_unit(i, cc):
                hT = hT2[i % 2]
                fT = do["fT"]
                fb = cc
                for j in range(8):
                    P.op("pe", lambda e, j=j: e.matmul(bank(fb, TB), lhsT=do["Wf"][:, j, cc * 128:(cc + 1) * 128],
                                                       rhs=hT[:, j, :], start=(j == 0), stop=(j == 7)),
                         reads=[(tag, "Wf"), hres(i)], writes=[("bank", fb)])
                P.op("act", lambda e: e.activation(out=fT[:, cc, :], in_=bank(fb, TB), func=AF.Copy),
                     reads=[("bank", fb)], writes=[(tag, "fT", cc)])

            def g_unit(i, s):
                fT = do["fT"]; Gcs = do["Gcs"]
                gb = 2 + s
                for cc in range(2):
                    P.op("pe", lambda e, cc=cc: e.matmul(bank(gb, 512), lhsT=fT[:, cc, s * 128:(s + 1) * 128],
                                                         rhs=bcs_t[:, cc, :], start=(cc == 0), stop=(cc == 1)),
                         reads=[(tag, "fT", cc), "bcs"], writes=[("bank", gb)])
                P.op("act", lambda e: e.activation(out=Gcs[:, s, :], in_=bank(gb, 512), func=AF.Copy),
                     reads=[("bank", gb)], writes=[(tag, "Gcs", s)])

            def z_unit(i, cc, kf):
                Gcs = do["Gcs"]; tabs = do["tabs"]
                zb = (4, 5, 0, 1, 2, 3)[(cc * 4 + kf) % 6]
                k = 0
                for s in range(TB // 128):
                    for cs in range(2):
                        P.op("pe", lambda e, s=s, cs=cs, k=k: e.matmul(
                            bank(zb, 512), lhsT=Gcs[:, s, cs * 256 + cc * 128: cs * 256 + (cc + 1) * 128],
                            rhs=tabs[i % 2][:, s, cs, kf * 512:(kf + 1) * 512], start=(k == 0), stop=(k == 3)),
                            reads=[(tag, "Gcs", s), (tag, "tab", i % 2, cs)], writes=[("bank", zb)])
                        k += 1
                zsl = do["ZT"][:, cc, kf * 512:(kf + 1) * 512]
                if i == 0:
                    P.op("dve", lambda e: e.tensor_copy(out=zsl, in_=bank(zb, 512)),
                         reads=[("bank", zb)], writes=[("ZT", cc, kf)])
                else:
                    P.op("dve", lambda e: e.tensor_tensor(out=zsl, in0=bank(zb, 512), in1=zsl, op=ALU.add),
                         reads=[("bank", zb), ("ZT", cc, kf)], writes=[("ZT", cc, kf)])

            def units(i):
                n = blist[i]
                ctx = n >= NT // TB
                us = []
                if do.get("k"):
                    for h in range(6):
                        us.append(lambda h=h: rope_unit(i, h, do["Wk"], do["Wks"], (tag, "Wk"), (tag, "Wks"), do["KT"],
                                                        (lambda hh, t: ("KT", hh)), n * TB))
                    for s in range(TB // 128):
                        us.append(lambda s=s: v_unit(i, s))
                if do.get("q"):
                    for h in range(6):
                        us.append(lambda h=h: rope_unit(i, h, do["Wq"], do["Wqs"], (tag, "Wq", h), (tag, "Wqs", h), do["QT"],
                                                        (lambda hh, t: ("QT", hh, t // 512)), n * TB))
                if do.get("f"):
                    for cc in range(2):
                        us.append(lambda cc=cc: f_unit(i, cc))
                    for s in range(TB // 128):
                        us.append(lambda s=s: g_unit(i, s))
                    for cc in range(2):
                        for kf in range(4):
                            us.append(lambda cc=cc, kf=kf: z_unit(i, cc, kf))
                return us

            load(0)
            if nb > 1:
                load(1)
            prep_pre(0)
            prep_pe(0)
            prep_ab(0)
            for j in range(8):
                prep_chunk(0, j)
            for i in range(nb):
                us = units(i)
                nxt = i + 1 < nb
                if nxt:
                    prep_pre(i + 1)
                us[0]()
                if nxt:
                    prep_pe(i + 1)
                    prep_ab(i + 1)
                rest = us[1:]
                chunks = list(range(8)) if nxt else []
                per = -(-8 // len(rest))
                for u in rest:
                    u()
                    for _ in range(per):
                        if chunks:
                            prep_chunk(i + 1, chunks.pop(0))
                while chunks:
                    prep_chunk(i + 1, chunks.pop(0))
                if i + 2 < nb:
                    load(i + 2)
            return wres

        w_inv = w_in.rearrange("(j p) c -> p j c", p=128)
        w_swv = w_in_sw.rearrange("(j p) c -> p j c", p=128)

        ZTf = AR.f32(0, [128, 2, NQ])
        tab0 = arena_t[:, 16384 // 4: 16384 // 4 + 4096].bitcast(BF16).rearrange("p (s c k) -> p s c k", s=2, c=2)
        tab1 = arena_t[:, 32768 // 4: 32768 // 4 + 4096].bitcast(BF16).rearrange("p (s c k) -> p s c k", s=2, c=2)
        tabs = [tab0, tab1]
        Wf = AR.bf(49152, [128, 8, 256])
        fT = AR.bf(R_PH + 61440, [128, 2, TB])
        Gcs = AR.bf(R_PH + 62464, [128, 2, 512])
        for r in [("ZT", cc, kf) for cc in range(2) for kf in range(4)] + [("P2C", "tab", a, b) for a in range(2) for b in range(2)] + [("P2C", "Wf")]:
            P.inherit(r, W1R)
        for r in [("P2C", "fT", 0), ("P2C", "fT", 1), ("P2C", "Gcs", 0), ("P2C", "Gcs", 1)]:
            P.inherit(r, p1res)
        dma("pool", Wf[:, :, :], w_inv[:, :, 2304:2560], writes=[("P2C", "Wf")])
        resC = proj_pass("P2C", list(range(NT // TB)), p1res,
                         dict(rope=False, f=True, Wf=Wf, fT=fT, Gcs=Gcs, tabs=tabs, ZT=ZTf))
        ZTb = AR.bf(R_ZB, [128, 2, NQ])
        P.inherit("ZTb", W1R)
        for cc in range(2):
            P.op("act", lambda e, cc=cc: e.activation(out=ZTb[:, cc, :], in_=ZTf[:, cc, :], func=AF.Copy),
                 reads=[("ZT", cc, kf) for kf in range(4)], writes=["ZTb"])
        resC = resC + [("P2C", "fT", 0), ("P2C", "fT", 1), ("P2C", "Gcs", 0), ("P2C", "Gcs", 1)]

        KVh = arena_t[:, 0:(6 * 17408) // 4].bitcast(BF16).rearrange("p (h x) -> p h x", h=6)
        KT = KVh[:, :, 0:NK]
        V = KVh[:, :, NK:2 * NK].rearrange("p h (k v) -> p h k v", v=128)
        KVR = [("KT", hh) for hh in range(6)] + [("V", hh) for hh in range(6)]
        QT = AR.bf(R_QT, [128, 6, NQ])
        Wk = AR.bf(R_PH + 32768, [128, 8, 768]); Wks = AR.bf(R_PH + 45056, [128, 8, 768])
        Wv = AR.bf(R_QT, [128, 8, 768])
        ZTR = [("ZT", cc, kf) for cc in range(2) for kf in range(4)] + [("P2C", "tab", a, b) for a in range(2) for b in range(2)] + [("P2C", "Wf")]
        for r in KVR:
            P.inherit(r, W1R + ZTR)
        for r in [("P2A", "Wk"), ("P2A", "Wks")]:
            P.inherit(r, p1res)
        P.inherit(("P2A", "Wv"), W1R)
        dma("pool", Wk[:, :, :], w_inv[:, :, 768:1536], writes=[("P2A", "Wk")])
        dma("pool", Wks[:, :, :], w_swv[:, :, 768:1536], writes=[("P2A", "Wks")])
        dma("pool", Wv[:, :, :], w_inv[:, :, 1536:2304], writes=[("P2A", "Wv")])
        resA = proj_pass("P2A", list(range(NK // TB)), p1res + resC,
                         dict(rope=True, k=True, Wk=Wk, Wks=Wks, Wv=Wv, KT=KT, V=V))

        Wq = AR.bf(R_PH + 32768, [128, 8, 768]); Wqs = AR.bf(R_PH + 45056, [128, 8, 768])
        for hh in range(6):
            P.inherit(("P2B", "Wq", hh), [("P2A", "Wk")])
            P.inherit(("P2B", "Wqs", hh), [("P2A", "Wks")])
        QTR = [("QT", h, qb) for h in range(6) for qb in range(4)]
        for r in QTR:
            P.inherit(r, [("P2A", "Wv")])
        for hh in range(6):
            dma("pool", Wq[:, :, hh * 128:(hh + 1) * 128], w_inv[:, :, hh * 128:(hh + 1) * 128], writes=[("P2B", "Wq", hh)])
            dma("pool", Wqs[:, :, hh * 128:(hh + 1) * 128], w_swv[:, :, hh * 128:(hh + 1) * 128], writes=[("P2B", "Wqs", hh)])
        resB = proj_pass("P2B", list(range(NQ // TB)), resA, dict(rope=True, q=True, Wq=Wq, Wqs=Wqs, QT=QT))

        Wo = AR.bf(R_PH, [128, 8, D])
        NPT = 4
        PT = [AR.bf(R_PH + 16384 + 2048 * i, [128, 1024]) for i in range(NPT)]
        PS = [AR.bf(R_PH + 24576 + 2048 * i, [128, 1024]) for i in range(2)]
        ev = [AR.f32(R_PH + 28672 + 2048 * i, [128, 512]) for i in range(6)]
        fmT = AR.bf(R_PH + 40960, [128, 2, NQ])
        prevw = resB + [("P2B", "Wq", hh) for hh in range(6)] + [("P2B", "Wqs", hh) for hh in range(6)]
        P.inherit("Wo", prevw)
        for i in range(NPT):
            P.inherit(("PT", i), prevw)
        for i in range(2):
            P.inherit(("PS", i), prevw)
        for i in range(6):
            P.inherit(("ev", i), prevw)
        for qb in range(4):
            P.inherit(("fmT", qb), prevw)
        dma("pool", Wo[:, :, :], w_out.rearrange("(j p) c -> p j c", p=128), writes=["Wo"])

        it = 0
        pending = []
        for h in range(6):
            for qb in range(4):
                q0 = qb * 512

                def qk(kc, h=h, q0=q0):
                    b0 = 2 * (kc % 2)
                    P.op("pe", lambda e: e.matmul(bank(b0), lhsT=KT[0:64, h, kc * 128:(kc + 1) * 128], rhs=QT[0:64, h, q0:q0 + 512],
                                                  start=True, stop=True), reads=[("KT", h), ("QT", h, q0 // 512)], writes=[("bank", b0)])
                    P.op("pe", lambda e: e.matmul(bank(b0 + 1), lhsT=KT[64:128, h, kc * 128:(kc + 1) * 128], rhs=QT[64:128, h, q0:q0 + 512],
                                                  start=True, stop=True, tile_position=(64, 0)), reads=[("KT", h), ("QT", h, q0 // 512)], writes=[("bank", b0 + 1)])
                    P.op("act", lambda e: e.activation(out=PT[kc % NPT][:], in_=pp[b0 // 2][:, :], func=AF.Exp, scale=0.125),
                         reads=[("bank", b0), ("bank", b0 + 1)], writes=[("PT", kc % NPT)])

                nkc = NK // 128

                def av(kc, h=h):
                    pt = PT[kc % NPT]
                    st, sp_ = (kc == 0), (kc == nkc - 1)
                    P.op("pe", lambda e: e.matmul(bank(4), lhsT=V[:, h, kc, :], rhs=pt[:, 0:512], start=st, stop=sp_),
                         reads=[("V", h), ("PT", kc % NPT)], writes=[("bank", 4)])
                    P.op("pe", lambda e: e.matmul(bank(5), lhsT=V[:, h, kc, :], rhs=pt[:, 512:1024], start=st, stop=sp_),
                         reads=[("V", h), ("PT", kc % NPT)], writes=[("bank", 5)])

                def pairadd(p):
                    a_, b_ = PT[(2 * p) % NPT], PT[(2 * p + 1) % NPT]
                    P.op("dve", lambda e: e.tensor_tensor(out=PS[p % 2][:], in0=a_[:], in1=b_[:], op=ALU.add),
                         reads=[("PT", (2 * p) % NPT), ("PT", (2 * p + 1) % NPT)], writes=[("PS", p % 2)])

                def sums(p):
                    st, sp_ = (p == 0), (p == nkc // 2 - 1)
                    P.op("pe", lambda e: e.matmul(bank(6), lhsT=ones[:], rhs=PS[p % 2][:, 0:512], start=st, stop=sp_),
                         reads=["ones", ("PS", p % 2)], writes=[("bank", 6)])
                    P.op("pe", lambda e: e.matmul(bank(7), lhsT=ones[:], rhs=PS[p % 2][:, 512:1024], start=st, stop=sp_),
                         reads=["ones", ("PS", p % 2)], writes=[("bank", 7)])

                qk(0)
                qk(1)
                for kc in range(nkc):
                    if kc + 2 < nkc:
                        qk(kc + 2)
                    av(kc)
                    if kc % 2 == 1:
                        pairadd(kc // 2)
                        if pending:
                            pending.pop(0)()
                        if kc >= 3:
                            sums(kc // 2 - 1)
                sums(nkc // 2 - 1)
                while pending:
                    pending.pop(0)()
                c0, c1, c2, c3 = ev[0], ev[1], ev[2], ev[3]
                P.op("dve", lambda e, c0=c0: e.tensor_copy(out=c0[:], in_=bank(4)), reads=[("bank", 4)], writes=[("ev", 0)])
                P.op("dve", lambda e, c2=c2: e.tensor_copy(out=c2[:], in_=bank(6)), reads=[("bank", 6)], writes=[("ev", 2)])
                P.op("dve", lambda e, c1=c1: e.tensor_copy(out=c1[:], in_=bank(5)), reads=[("bank", 5)], writes=[("ev", 1)])
                P.op("dve", lambda e, c3=c3: e.tensor_copy(out=c3[:], in_=bank(7)), reads=[("bank", 7)], writes=[("ev", 3)])
                def mk_pending(h=h, qb=qb, q0=q0, c0=c0, c1=c1, c2=c2, c3=c3):
                    out = []
                    for (cs, ci) in ((c2, 2), (c3, 3)):
                        for qq in range(4):
                            out.append(lambda cs=cs, ci=ci, qq=qq: P.op(
                                "dve", lambda e: e.reciprocal(out=cs[:, qq * 128:(qq + 1) * 128], in_=cs[:, qq * 128:(qq + 1) * 128]),
                                reads=[("ev", ci)], writes=[("ev", ci)]))
                    out.append(lambda: P.op("dve", lambda e: e.tensor_tensor(out=c0[:], in0=c0[:], in1=c2[:], op=ALU.mult),
                                            reads=[("ev", 0), ("ev", 2)], writes=[("ev", 0)]))
                    out.append(lambda: P.op("dve", lambda e: e.tensor_tensor(out=c1[:], in0=c1[:], in1=c3[:], op=ALU.mult),
                                            reads=[("ev", 1), ("ev", 3)], writes=[("ev", 1)]))
                    out.append(lambda: P.op("dve", lambda e: e.scalar_tensor_tensor(
                        out=QT[:, h, q0:q0 + 512], in0=c1[:], scalar=neglam, in1=c0[:], op0=ALU.mult, op1=ALU.add),
                        reads=[("ev", 0), ("ev", 1), "neglam"], writes=[("QT", h, qb)]))
                    return out
                pending.extend(mk_pending())
                it += 1

        while pending:
            pending.pop(0)()

        lnb = [AR.f32(R_PH + 28672, [128, 1024]), AR.f32(R_PH + 28672 + 4096, [128, 1024])]
        xm = [AR.f32(R_PH + 49152, [128, 8, TB]), AR.f32(R_PH + 57344, [128, 8, TB])]
        P.inherit(("xm", 0), prevw)
        P.inherit(("xm", 1), prevw)
        nbm = NQ // TB

        def mload(n):
            dma("sp", xm[n % 2][:, :, :], scr[:, n * TB:(n + 1) * TB].rearrange("(j p) t -> p j t", p=128),
                reads=[("scr", n)], writes=[("xm", n % 2)])

        def subln(h, hq, sb_):
            q0 = hq * 1024
            qres = [("QT", h, 2 * hq), ("QT", h, 2 * hq + 1)]
            P.op("act", lambda e: e.activation(out=PT[sb_][:], in_=QT[:, h, q0:q0 + 1024], func=AF.Square),
                 reads=qres, writes=[("PT", sb_)])
            for t in range(2):
                bk = 2 * sb_ + t
                P.op("pe", lambda e, bk=bk, t=t: e.matmul(bank(bk), lhsT=ones[:], rhs=PT[sb_][:, t * 512:(t + 1) * 512], start=True, stop=True),
                     reads=[("PT", sb_), "ones"], writes=[("bank", bk)])
            P.op("act", lambda e: e.activation(out=lnb[sb_][:], in_=pp[sb_][:, :], func=AF.Ln,
                                               scale=1.0 / 128, bias=EPS),
                 reads=[("bank", 2 * sb_), ("bank", 2 * sb_ + 1)], writes=[("ev", 2 * sb_), ("ev", 2 * sb_ + 1)])
            P.op("act", lambda e: e.activation(out=lnb[sb_][:], in_=lnb[sb_][:], func=AF.Exp, scale=-0.5),
                 reads=[("ev", 2 * sb_), ("ev", 2 * sb_ + 1)], writes=[("ev", 2 * sb_), ("ev", 2 * sb_ + 1)])
            P.op("dve", lambda e: e.scalar_tensor_tensor(
                out=QT[:, h, q0:q0 + 1024], in0=QT[:, h, q0:q0 + 1024], scalar=sg08, in1=lnb[sb_][:], op0=ALU.mult, op1=ALU.mult),
                reads=qres + ["sg08", ("ev", 2 * sb_), ("ev", 2 * sb_ + 1)], writes=qres)

        def fm(qb):
            for oc in range(2):
                fb = 4 + oc
                for cc in range(2):
                    P.op("pe", lambda e, oc=oc, cc=cc, fb=fb: e.matmul(
                        bank(fb), lhsT=wf_t[:, cc, oc * 128:(oc + 1) * 128], rhs=ZTb[:, cc, qb * 512:(qb + 1) * 512],
                        start=(cc == 0), stop=(cc == 1)), reads=["wf", "ZTb"], writes=[("bank", fb)])
                P.op("act", lambda e, oc=oc, fb=fb: e.activation(out=fmT[:, oc, qb * 512:(qb + 1) * 512], in_=bank(fb), func=AF.Copy),
                     reads=[("bank", fb)], writes=[("fmT", qb)])

        def mix(n):
            t0 = n * TB
            for dc in range(8):
                mb = 6 + dc % 2
                for ic in range(8):
                    rhs = QT[:, ic, t0:t0 + TB] if ic < 6 else fmT[:, ic - 6, t0:t0 + TB]
                    P.op("pe", lambda e, dc=dc, ic=ic, mb=mb, rhs=rhs: e.matmul(bank(mb, TB), lhsT=Wo[:, ic, dc * 128:(dc + 1) * 128], rhs=rhs,
                                                                                start=(ic == 0), stop=(ic == 7)),
                         reads=["Wo", (("QT", ic, t0 // 512) if ic < 6 else ("fmT", t0 // 512))], writes=[("bank", mb)])
                P.op("dve", lambda e, dc=dc, mb=mb: e.scalar_tensor_tensor(
                    out=xm[n % 2][:, dc, :], in0=bank(mb, TB), scalar=sc[:, 10, dc:dc + 1], in1=xm[n % 2][:, dc, :], op0=ALU.mult, op1=ALU.add),
                    reads=[("bank", mb), ("sc", 10), ("xm", n % 2)], writes=[("xm", n % 2)])
            dma("sp", scr[:, t0:t0 + TB].rearrange("(j p) t -> p j t", p=128), xm[n % 2][:, :, :],
                reads=[("xm", n % 2)], writes=[("scr", n)])
            if n + 2 < nbm:
                mload(n + 2)

        mload(0)
        mload(1)
        itn = 0
        for h in range(6):
            subln(h, 0, itn % 2)
            itn += 1
        fm(0)
        fm(1)
        mix_after = {0: 0, 1: 1, 3: 2, 4: 3}
        for h in range(6):
            subln(h, 1, itn % 2)
            itn += 1
            if h in mix_after:
                mix(mix_after[h])
        fm(2)
        fm(3)
        for n in range(4, 8):
            mix(n)

        attn_res = KVR + ["ZTb", "Wo"] + [("fmT", qb) for qb in range(4)] + [("xm", 0), ("xm", 1)] + [("PT", i) for i in range(NPT)] + [("PS", 0), ("PS", 1)] + [("ev", i) for i in range(6)] + prevw + QTR

        Wg2 = AR.bf(0, [128, 8, DFF]); Wu2 = AR.bf(45056, [128, 8, DFF]); Wd2 = AR.bf(90112, [128, NF, D])
        w2gv = w2g.rearrange("(j p) f -> p j f", p=128)
        w2uv = w2u.rearrange("(j p) f -> p j f", p=128)
        w2dv = w2d.rearrange("(i p) d -> p i d", p=128)
        for (nm, Wt, src, base) in (("g", Wg2, w2gv, 0), ("u", Wu2, w2uv, 45056)):
            for j in range(8):
                lo, hi = base + j * 5632, base + (j + 1) * 5632 - 1
                heads = list(range(lo // 17408, min(5, hi // 17408) + 1))
                olds = [("KT", hh) for hh in heads] + [("V", hh) for hh in heads]
                if hi >= 104448:
                    olds += QTR
                P.inherit(("W2", nm, j), olds)
                dma("pool", Wt[:, j, :], src[:, j, :], writes=[("W2", nm, j)])
        for pr in range(11):
            P.inherit(("W2", "d", pr), [("KT", 5), ("V", 5), "ZTb"] + QTR)
            dma("pool", Wd2[:, 2 * pr:2 * pr + 2, :], w2dv[:, 2 * pr:2 * pr + 2, :], writes=[("W2", "d", pr)])
        W2 = (Wg2, Wu2, Wd2)
        for r in [("P5", "xb", 0), ("P5", "xb", 1), ("P5", "sq"), ("P5", "hT"), ("P5", "sd"), ("P5", "rstd"), ("P5", "tbuf"),
                  ("P5", "sg", 0), ("P5", "sg", 1)] + [("P5", "aT", i) for i in range(NF)]:
            P.inherit(r, attn_res)
        blocks5 = []
        for n in range(NQ // TB):
            t0 = n * TB
            blocks5.append(dict(src=scr[:, t0:t0 + TB], srcres=("scr", n), dst=outT[:, t0:t0 + TB], dstres=("out", n), A=11, B=12, G=13))
        ffn_pass("P5", "W2", W2, blocks5, final=True, byj=True)
        P.op("sp", lambda e: None, reads=[("out", n) for n in range(NQ // TB)])
        P.emit()
    return nc


_NC_CACHE = {}


def _host_tables():
    bf = ml_dtypes.bfloat16
    tabs = {}
    pair = np.arange(128) % 16
    part = (np.arange(128) // 16) % 2
    axis = (np.arange(128) // 32) % 2
    inv_freq = (np.float32(10000.0) ** (-(np.arange(16, dtype=np.float32)) / np.float32(16))).astype(np.float32)
    d = np.arange(64)
    ang64 = 2.0 * np.pi * ((d[:, None] * d[None, :]) % 64) / 64.0
    BC = np.zeros((256, 256), np.float64); BS = np.zeros((256, 256), np.float64)
    for g in range(4):
        BC[g * 64:(g + 1) * 64, g * 64:(g + 1) * 64] = np.cos(ang64)
        BS[g * 64:(g + 1) * 64, g * 64:(g + 1) * 64] = np.sin(ang64)
    tabs["bcs"] = np.concatenate([BC, BS], axis=1).astype(np.float32).astype(bf)
    for half in range(2):
        nloc = np.concatenate([half * NQ + np.arange(NQ), (1 - half) * NQ + np.arange(NQ)])
        row = (nloc // 64).astype(np.float32)
        col = (nloc % 64).astype(np.float32)
        pos = np.where(axis[:, None] == 0, row[None, :], col[None, :]).astype(np.float32)
        ang = (pos * inv_freq[pair][:, None]).astype(np.float32)
        cos = np.cos(ang).astype(np.float32)
        sin = np.sin(ang).astype(np.float32)
        sgn = np.where(part == 0, -1.0, 1.0).astype(np.float32)[:, None]
        tabs[("ropec", half)] = np.ascontiguousarray(cos)
        tabs[("ropes", half)] = np.ascontiguousarray(sin * sgn)
        kk = (half * NQ + np.arange(NQ)).astype(np.int64)
        prod = (nloc.astype(np.int64)[:, None] * kk[None, :]) % NT
        a = 2.0 * np.pi * prod.astype(np.float64) / NT
        tabs[("dftc", half)] = (np.cos(a) / 512.0).astype(np.float32).astype(bf)
        tabs[("dfts", half)] = (-np.sin(a) / 512.0).astype(np.float32).astype(bf)
    return tabs


def kernel(x, c, ctx, c_ctx, w_ada, b_ada, norm1_g, ffn1_w_gate, ffn1_w_up, ffn1_w_down,
           norm_mix_g, w_in, lambda_q1, lambda_k1, lambda_q2, lambda_k2, subln_g, w_fourier,
           w_out, norm2_g, ffn2_w_gate, ffn2_w_up, ffn2_w_down, final_norm_g):
    f = lambda a: np.ascontiguousarray(np.asarray(a, dtype=np.float32))
    x = f(x); c = f(c); ctx = f(ctx); c_ctx = f(c_ctx)
    if "nc" not in _NC_CACHE:
        _NC_CACHE["nc"] = build_nc()
        _NC_CACHE["tabs"] = _host_tables()
    nc = _NC_CACHE["nc"]
    tabs = _NC_CACHE["tabs"]

    def vT(v):
        return np.ascontiguousarray(f(v).reshape(8, 128).T)

    gains = np.ascontiguousarray(np.stack([vT(norm1_g[0]), vT(norm_mix_g[0]), vT(norm2_g[0]), vT(final_norm_g)], axis=1))
    b_adaT = np.ascontiguousarray(f(b_ada[0]).reshape(72, 128).T)
    w_in0 = f(w_in[0])
    cols = np.arange(1536)
    dd = cols % 64
    partner = (cols - dd) + (dd // 32) * 32 + (1 - (dd // 16) % 2) * 16 + dd % 16
    w_in_sw = np.ascontiguousarray(w_in0[:, partner])
    lamin = np.stack([f(lambda_q1[0]), f(lambda_k1[0]), f(lambda_q2[0]), f(lambda_k2[0])], axis=0)
    lamin = np.ascontiguousarray(np.broadcast_to(lamin[None], (128, 4, 64)))
    sublnT = np.ascontiguousarray(f(subln_g[0]).reshape(128, 1))
    shared = dict(
        w_ada=f(w_ada[0]), b_adaT=b_adaT, gains=gains,
        w1g=f(ffn1_w_gate[0]), w1u=f(ffn1_w_up[0]), w1d=f(ffn1_w_down[0]),
        w2g=f(ffn2_w_gate[0]), w2u=f(ffn2_w_up[0]), w2d=f(ffn2_w_down[0]),
        w_in=w_in0, w_in_sw=w_in_sw, lamin=lamin, sublnT=sublnT, w_fo=f(w_fourier[0]),
        bcs=tabs["bcs"], w_out=f(w_out[0]),
    )
    in_maps = []
    for core in range(8):
        b, half = core // 2, core % 2
        xb = x[b]
        xloc = np.concatenate([xb[half * NQ:(half + 1) * NQ], xb[(1 - half) * NQ:(2 - half) * NQ]], axis=0)
        m = dict(shared)
        m["xT"] = np.ascontiguousarray(xloc.T)
        m["ctxT"] = np.ascontiguousarray(ctx[b].T)
        m["cc2"] = np.ascontiguousarray(np.stack([vT(c[b]), vT(c_ctx)], axis=2))
        m["ropec"] = tabs[("ropec", half)]
        m["ropes"] = tabs[("ropes", half)]
        m["dftc"] = tabs[("dftc", half)]
        m["dfts"] = tabs[("dfts", half)]
        in_maps.append(m)
    res = run_bass_kernel_spmd(nc, in_maps, core_ids=list(range(8)))
    out = np.empty((4, NT, D), np.float32)
    for core in range(8):
        b, half = core // 2, core % 2
        out[b, half * NQ:(half + 1) * NQ, :] = np.asarray(res.results[core]["outT"]).T
    return out
```

```python
from contextlib import ExitStack

import numpy as np
import ml_dtypes
import concourse.bass as bass
import concourse.mybir as mybir
from concourse.bass_utils import run_bass_kernel_spmd

F32 = mybir.dt.float32
BF16 = mybir.dt.bfloat16
AF = mybir.ActivationFunctionType
ALU = mybir.AluOpType
AX = mybir.AxisListType

D = 1024
NT = 4096
NCTX = 256
NK = NT + NCTX
NQ = 2048
DFF = 2816
NF = DFF // 128
TB = 256
EPS = 1e-6
LAMBDA_INIT = 0.2

DEBUG = False


class Prog:
    ENGS = ("pe", "act", "dve", "pool", "sp")
    GROUP = 2000
    NDMA = 20

    def __init__(self, nc):
        self.nc = nc
        self.ops = {e: [] for e in self.ENGS}
        self.last_w = {}
        self.readers = {}
        self.dma_rr = {e: 0 for e in self.ENGS}
        self.dma_last = {}
        self.dma_cnt = {}

    def _add_reader(self, res, me, is_dma):
        d = self.readers.setdefault(res, {})
        if is_dma:
            d.setdefault("dma", []).append(me)
        else:
            d[me[0]] = me

    def _reader_list(self, res):
        d = self.readers.get(res)
        if not d:
            return []
        out = []
        for k, v in d.items():
            if k == "dma":
                out.extend(v)
            else:
                out.append(v)
        return out

    def op(self, eng, fn, reads=(), writes=(), dma=False):
        idx = len(self.ops[eng])
        me = (eng, idx)
        deps = set()
        for r in reads:
            w = self.last_w.get(r)
            if w is not None:
                deps.add(w)
        for w_ in writes:
            w = self.last_w.get(w_)
            if w is not None:
                deps.add(w)
            deps.update(self._reader_list(w_))
        rec = dict(fn=fn, deps=deps, signal=False, dma=dma, sig=None)
        if dma:
            k = self.dma_rr[eng] % self.NDMA
            self.dma_rr[eng] += 1
            key = ("dma", eng, k)
            prev = self.dma_last.get(key)
            if prev is not None:
                deps.add(prev)
            self.dma_last[key] = me
            n = self.dma_cnt.get(key, 0) + 1
            self.dma_cnt[key] = n
            rec["sig"] = (key, 16 * n)
        deps.discard(me)
        if eng == "pe":
            deps = {d for d in deps if d[0] != "pe"}
        rec["deps"] = deps
        self.ops[eng].append(rec)
        for r in reads:
            self._add_reader(r, me, dma)
        for w_ in writes:
            self.last_w[w_] = me
            self.readers[w_] = {}
        return me

    def inherit(self, new, olds):
        d = self.readers.setdefault(new, {})
        for o in olds:
            w = self.last_w.get(o)
            if w is not None:
                d.setdefault("dma", []).append(w)
            for r in self._reader_list(o):
                d.setdefault("dma", []).append(r)

    def emit(self):
        nc = self.nc
        for e in self.ENGS:
            for rec in self.ops[e]:
                for (e2, i2) in rec["deps"]:
                    t = self.ops[e2][i2]
                    if not t["dma"]:
                        t["signal"] = True
        semkeys = []
        for e in self.ENGS:
            k = 0
            for rec in self.ops[e]:
                if not rec["dma"] and rec["signal"]:
                    rec["sig"] = (("eng", e, k // self.GROUP), k % self.GROUP + 1)
                    k += 1
                if rec["sig"] is not None and rec["sig"][0] not in semkeys:
                    semkeys.append(rec["sig"][0])
        with ExitStack() as es:
            sems = {}
            for k in semkeys:
                sems[k] = es.enter_context(nc.semaphore("s_" + "_".join(str(x) for x in k)))
            block = es.enter_context(nc.Block())
            handles = {"pe": block.tensor, "act": block.scalar, "dve": block.vector,
                       "pool": block.gpsimd, "sp": block.sync}
            for e in self.ENGS:
                if not self.ops[e]:
                    continue

                def body(engh, e=e):
                    waited = {}
                    for rec in self.ops[e]:
                        need = {}
                        for (e2, i2) in rec["deps"]:
                            sk, v = self.ops[e2][i2]["sig"]
                            if waited.get(sk, 0) >= v:
                                continue
                            if need.get(sk, 0) < v:
                                need[sk] = v
                        for sk, v in need.items():
                            engh.wait_ge(sems[sk], v)
                            waited[sk] = v
                        ins = rec["fn"](engh)
                        if ins is not None and rec["sig"] is not None and (rec["dma"] or rec["signal"]):
                            sk, v = rec["sig"]
                            ins.then_inc(sems[sk], 16 if rec["dma"] else 1)
                handles[e](body)


def _prod(s):
    r = 1
    for x in s:
        r *= x
    return r


class Arena:
    def __init__(self, t):
        self.t = t

    def _shape(self, v, shape):
        if len(shape) == 2:
            return v
        if len(shape) == 3:
            return v.rearrange("p (a b) -> p a b", a=shape[1])
        raise ValueError

    def f32(self, off, shape):
        n = _prod(shape[1:])
        assert off % 4 == 0
        v = self.t[:, off // 4: off // 4 + n]
        return self._shape(v, shape)

    def bf(self, off, shape):
        n = _prod(shape[1:])
        assert off % 4 == 0 and n % 2 == 0
        v = self.t[:, off // 4: off // 4 + n // 2].bitcast(BF16)
        return self._shape(v, shape)


ARENA_WORDS = 50688
R_KT = 0
R_V = 52224
R_QT = 104448
R_ZB = 129024
R_PH = 137216


def build_nc():
    nc = bass.Bass("TRN2", target_bir_lowering=False)

    def din(name, shape, dt=F32):
        return nc.dram_tensor(name, shape, dt, kind="ExternalInput").ap()

    xT = din("xT", [D, NT])
    ctxT = din("ctxT", [D, NCTX])
    cc2 = din("cc2", [128, 8, 2])
    w_ada = din("w_ada", [D, 9 * D])
    b_adaT = din("b_adaT", [128, 72])
    gains = din("gains", [128, 4, 8])
    w1g = din("w1g", [D, DFF]); w1u = din("w1u", [D, DFF]); w1d = din("w1d", [DFF, D])
    w2g = din("w2g", [D, DFF]); w2u = din("w2u", [D, DFF]); w2d = din("w2d", [DFF, D])
    w_in = din("w_in", [D, 2560])
    w_in_sw = din("w_in_sw", [D, 1536])
    lamin = din("lamin", [128, 4, 64])
    sublnT = din("sublnT", [128, 1])
    w_fo = din("w_fo", [256, 256])
    bcs = din("bcs", [256, 512], BF16)
    w_out = din("w_out", [D, D])
    ropec = din("ropec", [128, NT])
    ropes = din("ropes", [128, NT])
    dftc = din("dftc", [NT, NQ], BF16)
    dfts = din("dfts", [NT, NQ], BF16)
    outT = nc.dram_tensor("outT", [D, NQ], F32, kind="ExternalOutput").ap()
    scr = nc.dram_tensor("scr", [D, NK], F32, kind="Internal").ap()

    es = ExitStack()
    with es:
        def sb(name, shape, dt):
            return es.enter_context(nc.sbuf_tensor(name, shape, dt))

        arena_t = sb("arena", [128, ARENA_WORDS], F32)
        AR = Arena(arena_t)
        pp = [es.enter_context(nc.psum_tensor("ps%d" % i, [128, 1024], F32)) for i in range(4)]
        ones = sb("ones", [128, 128], BF16)
        s2 = sb("s2", [128, 8, 2], F32)
        s2b = sb("s2b", [128, 8, 2], BF16)
        badaT = sb("badaT", [128, 72], F32)
        gn = sb("gn", [128, 4, 8], F32)
        modT = sb("modT", [128, 72, 2], F32)
        sc = sb("sc", [128, 14, 8], F32)
        lam_t = sb("lam_t", [128, 4, 64], F32)
        lam_w = sb("lam_w", [128, 8], F32)
        sgt = sb("sgt", [128, 2], F32)
        wf_t = sb("wf_t", [128, 2, 256], BF16)
        bcs_t = sb("bcs_t", [128, 2, 512], BF16)

        P = Prog(nc)

        def bank(b, n=512):
            return pp[b // 2][:, (b % 2) * 512: (b % 2) * 512 + n]

        def dma(eng, out, in_, reads=(), writes=()):
            return P.op(eng, lambda e: e.dma_start(out=out, in_=in_), reads=reads, writes=writes, dma=True)

        P.op("dve", lambda e: e.memset(ones[:], 1.0), writes=["ones"])
        dma("sp", s2[:], cc2, writes=["s2"])
        dma("sp", badaT[:], b_adaT, writes=["badaT"])
        dma("sp", gn[:], gains, writes=["gn"])
        dma("sp", lam_t[:], lamin, writes=["lam_t"])
        dma("sp", sgt[:, 0:1], sublnT, writes=["sgt0"])
        dma("pool", wf_t[:], w_fo.rearrange("(c p) n -> p c n", p=128), writes=["wf"])
        dma("sp", bcs_t[:], bcs.rearrange("(c p) n -> p c n", p=128), writes=["bcs"])
        P.op("act", lambda e: e.activation(out=s2[:], in_=s2[:], func=AF.Silu), reads=["s2"], writes=["s2"])

        def load_ffn_weights(tag, wg, wu, wd, olds):
            Wg = AR.bf(0, [128, 8, DFF]); Wu = AR.bf(45056, [128, 8, DFF]); Wd = AR.bf(90112, [128, NF, D])
            wgv = wg.rearrange("(j p) f -> p j f", p=128)
            wuv = wu.rearrange("(j p) f -> p j f", p=128)
            wdv = wd.rearrange("(i p) d -> p i d", p=128)
            for pr in range(11):
                for nm, Wt, src in (("g", Wg, wgv), ("u", Wu, wuv)):
                    res = (tag, nm, pr)
                    P.inherit(res, olds(nm, pr))
                    dma("pool", Wt[:, :, pr * 256:(pr + 1) * 256], src[:, :, pr * 256:(pr + 1) * 256], writes=[res])
            for pr in range(11):
                res = (tag, "d", pr)
                P.inherit(res, olds("d", pr))
                dma("pool", Wd[:, 2 * pr:2 * pr + 2, :], wdv[:, 2 * pr:2 * pr + 2, :], writes=[res])
            return Wg, Wu, Wd

        ST0 = 183296
        stg = [AR.bf(ST0, [128, 8, 512]), AR.bf(ST0 + 8192, [128, 8, 512])]
        wav = w_ada.rearrange("(j p) n -> p j n", p=128)
        P.op("dve", lambda e: e.tensor_copy(out=s2b[:], in_=s2[:]), reads=["s2"], writes=["s2b"])

        def ada_load(nb):
            dma("pool", stg[nb % 2][:, :, :], wav[:, :, nb * 512:(nb + 1) * 512], writes=[("stg", nb % 2)])

        def ada_compute(nb):
            sbuf = stg[nb % 2]
            k = nb // 2
            for q in range(4):
                n = nb * 4 + q
                for j in range(8):
                    P.op("pe", lambda e, sbuf=sbuf, q=q, j=j, n=n: e.matmul(
                        pp[3][:, 512 + 2 * n:512 + 2 * n + 2], lhsT=sbuf[:, j, q * 128:(q + 1) * 128], rhs=s2b[:, j, :],
                        start=(j == 0), stop=(j == 7)), reads=[("stg", nb % 2), "s2b"], writes=["ps_ada"])
            if nb % 2 == 1:
                P.op("dve", lambda e, k=k: e.tensor_tensor(
                    out=modT[:, 8 * k:8 * k + 8, :], in0=pp[3][:, 512 + 16 * k:512 + 16 * k + 16].rearrange("p (n m) -> p n m", m=2),
                    in1=badaT[:, 8 * k:8 * k + 8].unsqueeze(2).to_broadcast([128, 8, 2]), op=ALU.add),
                    reads=["ps_ada", "badaT"], writes=[("modT", k)])

        ada_load(0)
        ada_load(1)
        for nb in range(6):
            ada_compute(nb)
            if nb + 2 < 6:
                ada_load(nb + 2)
        Wg, Wu, Wd = load_ffn_weights("W1", w1g, w1u, w1d, lambda nm, pr: [])

        def mod(k, m):
            return modT[:, k * 8:(k + 1) * 8, m]

        def mk_scale(idx, gi, k, m):
            P.op("dve", lambda e: e.scalar_tensor_tensor(out=sc[:, idx, :], in0=mod(k, m), scalar=1.0, in1=gn[:, gi, :],
                                                         op0=ALU.add, op1=ALU.mult), reads=[("modT", k), "gn"], writes=[("sc", idx)])

        def mk_copy(idx, k, m, f):
            P.op("dve", lambda e: e.tensor_scalar(out=sc[:, idx, :], in0=mod(k, m), scalar1=f, scalar2=None, op0=ALU.mult),
                 reads=[("modT", k)], writes=[("sc", idx)])

        mk_scale(0, 0, 1, 0); mk_copy(1, 0, 0, 1.0); mk_copy(2, 2, 0, 0.5)
        mk_scale(3, 0, 1, 1); mk_copy(4, 0, 1, 1.0); mk_copy(5, 2, 1, 0.5)

        def ada_hook(n):
            if n == 0:
                ada_load(6)
                ada_load(7)
                return
            nb = 5 + n
            ada_compute(nb)
            if nb + 2 < 18:
                ada_load(nb + 2)
            if nb == 17:
                mk_scale(6, 1, 4, 0); mk_copy(7, 3, 0, 1.0)
                mk_scale(8, 1, 4, 1); mk_copy(9, 3, 1, 1.0)
                mk_copy(10, 5, 0, 1.0)
                mk_scale(11, 2, 7, 0); mk_copy(12, 6, 0, 1.0); mk_copy(13, 8, 0, 0.5)
        SCR = [("sc", i) for i in range(14)]

        P.op("dve", lambda e: e.tensor_tensor(out=lam_t[:, 0, :], in0=lam_t[:, 0, :], in1=lam_t[:, 1, :], op=ALU.mult),
             reads=["lam_t"], writes=["lam_t"])
        P.op("dve", lambda e: e.tensor_tensor(out=lam_t[:, 2, :], in0=lam_t[:, 2, :], in1=lam_t[:, 3, :], op=ALU.mult),
             reads=["lam_t"], writes=["lam_t"])
        P.op("dve", lambda e: e.reduce_sum(out=lam_w[:, 0:1], in_=lam_t[:, 0, :], axis=AX.X), reads=["lam_t"], writes=["lam_w"])
        P.op("dve", lambda e: e.reduce_sum(out=lam_w[:, 1:2], in_=lam_t[:, 2, :], axis=AX.X), reads=["lam_w", "lam_t"], writes=["lam_w"])
        P.op("act", lambda e: e.activation(out=lam_w[:, 2:4], in_=lam_w[:, 0:2], func=AF.Exp), reads=["lam_w"], writes=["lam_w"])
        P.op("dve", lambda e: e.tensor_tensor(out=lam_w[:, 4:5], in0=lam_w[:, 3:4], in1=lam_w[:, 2:3], op=ALU.subtract),
             reads=["lam_w"], writes=["lam_w"])
        P.op("dve", lambda e: e.tensor_scalar(out=lam_w[:, 5:6], in0=lam_w[:, 4:5], scalar1=-LAMBDA_INIT, scalar2=None, op0=ALU.add),
             reads=["lam_w"], writes=["neglam"])
        neglam = lam_w[:, 5:6]
        P.op("dve", lambda e: e.tensor_scalar(out=sgt[:, 1:2], in0=sgt[:, 0:1], scalar1=1.0 - LAMBDA_INIT, scalar2=None, op0=ALU.mult),
             reads=["sgt0"], writes=["sg08"])
        sg08 = sgt[:, 1:2]

        def work(base):
            w = {}
            w["xb"] = [AR.f32(base, [128, 8, TB]), AR.f32(base + 8192, [128, 8, TB])]
            w["sq"] = AR.bf(base + 16384, [128, 8, TB])
            w["hT"] = AR.bf(base + 20480, [128, 8, TB])
            w["sd"] = AR.f32(base + 24576, [128, TB])
            w["rstd"] = AR.f32(base + 25600, [128, TB])
            return w

        def norm_stage_pre(tag, w, xbuf, xres, n):
            P.op("act", lambda e: e.activation(out=w["sq"][:], in_=xbuf[:], func=AF.Square), reads=[xres], writes=[(tag, "sq")])

        def norm_stage_pe(tag, w, sbank):
            for j in range(8):
                P.op("pe", lambda e, j=j: e.matmul(bank(sbank, TB), lhsT=ones[:], rhs=w["sq"][:, j, :], start=(j == 0), stop=(j == 7)),
                     reads=[(tag, "sq"), "ones"], writes=[("bank", sbank)])

        def norm_stage_post(tag, w, xbuf, xres, tmp, tmpres, Ai, Bi, sbank, hres, hT=None):
            hT = w["hT"] if hT is None else hT
            P.op("act", lambda e: e.activation(out=w["sd"][:], in_=bank(sbank, TB), func=AF.Ln, scale=1.0 / D, bias=EPS),
                 reads=[("bank", sbank)], writes=[(tag, "sd")])
            P.op("act", lambda e: e.activation(out=w["rstd"][:], in_=w["sd"][:], func=AF.Exp, scale=-0.5),
                 reads=[(tag, "sd")], writes=[(tag, "rstd")])
            P.op("dve", lambda e: e.tensor_tensor(out=tmp[:], in0=xbuf[:], in1=w["rstd"][:].unsqueeze(1).to_broadcast([128, 8, TB]),
                                                  op=ALU.mult), reads=[xres, (tag, "rstd")], writes=[tmpres])
            for j in range(8):
                if Bi is None:
                    P.op("act", lambda e, j=j: e.activation(out=hT[:, j, :], in_=tmp[:, j, :], func=AF.Identity,
                                                            scale=sc[:, Ai, j:j + 1]),
                         reads=[tmpres, ("sc", Ai)], writes=[hres])
                else:
                    P.op("act", lambda e, j=j: e.activation(out=hT[:, j, :], in_=tmp[:, j, :], func=AF.Identity,
                                                            scale=sc[:, Ai, j:j + 1], bias=sc[:, Bi, j:j + 1]),
                         reads=[tmpres, ("sc", Ai), ("sc", Bi)], writes=[hres])

        def ffn_pass(tag, Wtag, Wts, blocks, final=None, after_block=None, byj=False):
            Wg, Wu, Wd = Wts
            base = 135168
            w = work(base)
            aT = AR.bf(base + 26624, [128, NF, TB])
            sg = [AR.f32(base + 37888, [128, TB]), AR.f32(base + 38912, [128, TB])]
            tbuf = AR.f32(base + 39936, [128, 8, TB])
            nb = len(blocks)
            NX = 3 if final else 2
            xbs = list(w["xb"]) + ([AR.f32(base + 48128, [128, 8, TB])] if final else [])
            sq2 = AR.bf(base + 56320, [128, 8, TB]) if final else None
            sd2 = AR.f32(base + 60416, [128, TB]) if final else None
            rstd2 = AR.f32(base + 61440, [128, TB]) if final else None
            WR = [(Wtag, nm, pr) for nm in "gud" for pr in range(11)]

            def xres(n):
                return (tag, "xb", n % NX)

            def load(n):
                b = blocks[n]
                dma("sp", xbs[n % NX][:, :, :], b["src"].rearrange("(j p) t -> p j t", p=128),
                    reads=[b["srcres"]], writes=[xres(n)])

            def pre(n):
                norm_stage_pre(tag, w, xbs[n % NX], xres(n), n)

            def stats_and_mod(n):
                b = blocks[n]
                norm_stage_pe(tag, w, 6)
                norm_stage_post(tag, w, xbs[n % NX], xres(n), tbuf, (tag, "tbuf"), b["A"], b["B"], 6, (tag, "hT"))

            def gu(n, mid_hook, chunk_hooks=None):
                for i in range(NF):
                    if chunk_hooks and i in chunk_hooks:
                        chunk_hooks[i]()
                    gb, ub = i % 2, 2 + i % 2
                    for j in range(8):
                        P.op("pe", lambda e, i=i, j=j, gb=gb: e.matmul(bank(gb, TB), lhsT=Wg[:, j, i * 128:(i + 1) * 128],
                                                                        rhs=w["hT"][:, j, :], start=(j == 0), stop=(j == 7)),
                             reads=[(Wtag, "g", j if byj else i // 2), (tag, "hT")], writes=[("bank", gb)])
                    for j in range(8):
                        P.op("pe", lambda e, i=i, j=j, ub=ub: e.matmul(bank(ub, TB), lhsT=Wu[:, j, i * 128:(i + 1) * 128],
                                                                        rhs=w["hT"][:, j, :], start=(j == 0), stop=(j == 7)),
                             reads=[(Wtag, "u", j if byj else i // 2), (tag, "hT")], writes=[("bank", ub)])
                    P.op("act", lambda e, i=i, gb=gb: e.activation(out=sg[i % 2][:], in_=bank(gb, TB), func=AF.Silu),
                         reads=[("bank", gb)], writes=[(tag, "sg", i % 2)])
                    P.op("dve", lambda e, i=i, ub=ub: e.tensor_tensor(out=aT[:, i, :], in0=sg[i % 2][:], in1=bank(ub, TB), op=ALU.mult),
                         reads=[(tag, "sg", i % 2), ("bank", ub)], writes=[(tag, "aT", i)])
                    if i == 10 and mid_hook is not None:
                        mid_hook()

            def down(n):
                b = blocks[n]
                xb = xbs[n % NX]
                for dc in range(8):
                    db = 4 + dc % 2
                    for i in range(NF):
                        P.op("pe", lambda e, i=i, dc=dc, db=db: e.matmul(bank(db, TB), lhsT=Wd[:, i, dc * 128:(dc + 1) * 128],
                                                                          rhs=aT[:, i, :], start=(i == 0), stop=(i == NF - 1)),
                             reads=[(Wtag, "d", i // 2), (tag, "aT", i)], writes=[("bank", db)])
                    P.op("dve", lambda e, dc=dc, db=db, xb=xb, G=b["G"]: e.scalar_tensor_tensor(
                        out=xb[:, dc, :], in0=bank(db, TB), scalar=sc[:, G, dc:dc + 1], in1=xb[:, dc, :], op0=ALU.mult, op1=ALU.add),
                        reads=[("bank", db), ("sc", b["G"]), xres(n)], writes=[xres(n)])

            def fin_pre(n):
                xb = xbs[n % NX]
                P.op("act", lambda e: e.activation(out=sq2[:], in_=xb[:], func=AF.Square), reads=[xres(n)], writes=[(tag, "sq2")])

            def fin_pe(n):
                for j in range(8):
                    P.op("pe", lambda e, j=j: e.matmul(bank(7, TB), lhsT=ones[:], rhs=sq2[:, j, :], start=(j == 0), stop=(j == 7)),
                         reads=[(tag, "sq2"), "ones"], writes=[("bank", 7)])

            def fin_rstd(n):
                P.op("act", lambda e: e.activation(out=sd2[:], in_=bank(7, TB), func=AF.Ln, scale=1.0 / D, bias=EPS),
                     reads=[("bank", 7)], writes=[(tag, "sd2")])
                P.op("act", lambda e: e.activation(out=rstd2[:], in_=sd2[:], func=AF.Exp, scale=-0.5),
                     reads=[(tag, "sd2")], writes=[(tag, "rstd2")])

            def fin_scale(n, j):
                xb = xbs[n % NX]
                P.op("dve", lambda e: e.scalar_tensor_tensor(out=xb[:, j, :], in0=xb[:, j, :], scalar=gn[:, 3, j:j + 1],
                                                             in1=rstd2[:], op0=ALU.mult, op1=ALU.mult),
                     reads=[xres(n), (tag, "rstd2"), "gn"], writes=[xres(n)])

            def fin_hooks(n):
                hk = {0: (lambda: fin_pre(n)), 2: (lambda: fin_pe(n)), 3: (lambda: fin_rstd(n))}
                for j in range(8):
                    hk[4 + j] = (lambda j=j: fin_scale(n, j))

                def fin_out():
                    store(n)
                    if n + 3 < nb:
                        load(n + 3)
                hk[12] = fin_out
                return hk

            def store(n):
                b = blocks[n]
                dma("sp", b["dst"].rearrange("(j p) t -> p j t", p=128), xbs[n % NX][:, :, :],
                    reads=[xres(n)], writes=[b["dstres"]])

            for n0 in range(min(NX, nb)):
                load(n0)
            pre(0)
            stats_and_mod(0)
            for n in range(nb):
                hook = (lambda n=n: pre(n + 1)) if n + 1 < nb else None
                gu(n, hook, fin_hooks(n - 1) if (final and n >= 1) else None)
                if n + 1 < nb:
                    stats_and_mod(n + 1)
                down(n)
                if final:
                    if n == nb - 1:
                        hk = fin_hooks(n)
                        for i in sorted(hk):
                            hk[i]()
                else:
                    store(n)
                    if n + 2 < nb:
                        load(n + 2)
                if after_block and n in after_block:
                    after_block[n]()
            return [(tag, "xb", 0), (tag, "xb", 1), (tag, "xb", 2), (tag, "sq2"), (tag, "sd2"), (tag, "rstd2"), (tag, "sq"), (tag, "hT"), (tag, "sd"), (tag, "rstd"), (tag, "tbuf"),
                    (tag, "sg", 0), (tag, "sg", 1)] + [(tag, "aT", i) for i in range(NF)]

        blocks1 = []
        for n in range(NK // TB):
            t0 = n * TB
            if t0 < NT:
                src = xT[:, t0:t0 + TB]; A, B, G = 0, 1, 2
            else:
                src = ctxT[:, t0 - NT:t0 - NT + TB]; A, B, G = 3, 4, 5
            blocks1.append(dict(src=src, srcres=("in", n), dst=scr[:, t0:t0 + TB], dstres=("scr", n), A=A, B=B, G=G))
        p1res = ffn_pass("P1", "W1", (Wg, Wu, Wd), blocks1,
                         after_block={n: (lambda n=n: ada_hook(n)) for n in range(13)})
        p1res = p1res + [("stg", 0), ("stg", 1)]
        W1R = [("W1", nm, pr) for nm in "gud" for pr in range(11)]

        def proj_pass(tag, blist, olds_work, do):
            w = work(R_PH)
            hT2 = [w["hT"], AR.bf(R_PH + 57344, [128, 8, TB])]
            cosb = [AR.f32(R_PH + 26624, [128, TB]), AR.f32(R_PH + 27648, [128, TB])]
            sinb = [AR.f32(R_PH + 28672, [128, TB]), AR.f32(R_PH + 29696, [128, TB])]
            t1 = AR.f32(R_PH + 30720, [128, TB]); t2 = AR.f32(R_PH + 31744, [128, TB])
            wres = [(tag, "xb", 0), (tag, "xb", 1),
                    (tag, "sq"), (tag, "hT", 0), (tag, "hT", 1), (tag, "sd"), (tag, "rstd"),
                    (tag, "cos", 0), (tag, "cos", 1), (tag, "sin", 0), (tag, "sin", 1), (tag, "t1"), (tag, "t2")]
            for r in wres:
                P.inherit(r, olds_work)
            nb = len(blist)

            def xres(i):
                return (tag, "xb", i % 2)

            def hres(i):
                return (tag, "hT", i % 2)

            def load(i):
                n = blist[i]
                dma("sp", w["xb"][i % 2][:, :, :], scr[:, n * TB:(n + 1) * TB].rearrange("(j p) t -> p j t", p=128),
                    reads=[("scr", n)], writes=[xres(i)])
                if do["rope"] and n < NT // TB:
                    dma("sp", cosb[i % 2][:], ropec[:, n * TB:(n + 1) * TB], writes=[(tag, "cos", i % 2)])
                    dma("sp", sinb[i % 2][:], ropes[:, n * TB:(n + 1) * TB], writes=[(tag, "sin", i % 2)])
                if do.get("f"):
                    tabs = do["tabs"]
                    dma("sp", tabs[i % 2][:, :, 0, :], dftc[n * TB:(n + 1) * TB, :].rearrange("(s p) k -> p s k", p=128),
                        writes=[(tag, "tab", i % 2, 0)])
                    dma("sp", tabs[i % 2][:, :, 1, :], dfts[n * TB:(n + 1) * TB, :].rearrange("(s p) k -> p s k", p=128),
                        writes=[(tag, "tab", i % 2, 1)])

            def prep_pre(i):
                norm_stage_pre(tag, w, w["xb"][i % 2], xres(i), i)

            def prep_pe(i):
                norm_stage_pe(tag, w, 6)

            def prep_ab(i):
                P.op("act", lambda e: e.activation(out=w["sd"][:], in_=bank(6, TB), func=AF.Ln, scale=1.0 / D, bias=EPS),
                     reads=[("bank", 6)], writes=[(tag, "sd")])
                P.op("act", lambda e: e.activation(out=w["rstd"][:], in_=w["sd"][:], func=AF.Exp, scale=-0.5),
                     reads=[(tag, "sd")], writes=[(tag, "rstd")])

            def prep_chunk(i, j):
                n = blist[i]
                ctx = n >= NT // TB
                A, B = (8, 9) if ctx else (6, 7)
                xbuf = w["xb"][i % 2]
                hT = hT2[i % 2]
                if j == 0:
                    P.op("dve", lambda e: e.tensor_tensor(out=xbuf[:], in0=xbuf[:], in1=w["rstd"][:].unsqueeze(1).to_broadcast([128, 8, TB]),
                                                          op=ALU.mult), reads=[xres(i), (tag, "rstd")], writes=[xres(i)])
                P.op("act", lambda e: e.activation(out=hT[:, j, :], in_=xbuf[:, j, :], func=AF.Identity,
                                                   scale=sc[:, A, j:j + 1], bias=sc[:, B, j:j + 1]),
                     reads=[xres(i), ("sc", A), ("sc", B)], writes=[hres(i)])

            def rope_unit(i, h, Wm, Ws, wr, wsr, dstT, dresf, tok0):
                n = blist[i]
                ctx = n >= NT // TB
                hT = hT2[i % 2]
                kb, sbk = h % 2, 2 + h % 2
                for j in range(8):
                    P.op("pe", lambda e, j=j: e.matmul(bank(kb, TB), lhsT=Wm[:, j, h * 128:(h + 1) * 128],
                                                       rhs=hT[:, j, :], start=(j == 0), stop=(j == 7)),
                         reads=[wr, hres(i)], writes=[("bank", kb)])
                if ctx:
                    P.op("act", lambda e: e.activation(out=dstT[:, h, tok0:tok0 + TB], in_=bank(kb, TB), func=AF.Copy),
                         reads=[("bank", kb)], writes=[dresf(h, tok0)])
                    return
                for j in range(8):
                    P.op("pe", lambda e, j=j: e.matmul(bank(sbk, TB), lhsT=Ws[:, j, h * 128:(h + 1) * 128],
                                                       rhs=hT[:, j, :], start=(j == 0), stop=(j == 7)),
                         reads=[wsr, hres(i)], writes=[("bank", sbk)])
                P.op("dve", lambda e: e.tensor_tensor(out=t1[:], in0=bank(kb, TB), in1=cosb[i % 2][:], op=ALU.mult),
                     reads=[("bank", kb), (tag, "cos", i % 2)], writes=[(tag, "t1")])
                P.op("dve", lambda e: e.tensor_tensor(out=t2[:], in0=bank(sbk, TB), in1=sinb[i % 2][:], op=ALU.mult),
                     reads=[("bank", sbk), (tag, "sin", i % 2)], writes=[(tag, "t2")])
                P.op("dve", lambda e: e.tensor_tensor(out=dstT[:, h, tok0:tok0 + TB], in0=t1[:], in1=t2[:], op=ALU.add),
                     reads=[(tag, "t1"), (tag, "t2")], writes=[dresf(h, tok0)])

            def v_unit(i, s):
                n = blist[i]
                hT = hT2[i % 2]
                kc = n * (TB // 128) + s
                for (c0, cn, vb) in ((0, 512, 4), (512, 256, 5)):
                    for j in range(8):
                        P.op("pe", lambda e, j=j, c0=c0, cn=cn, vb=vb: e.matmul(
                            bank(vb, cn), lhsT=hT[:, j, s * 128:(s + 1) * 128], rhs=do["Wv"][:, j, c0:c0 + cn],
                            start=(j == 0), stop=(j == 7)), reads=[(tag, "Wv"), hres(i)], writes=[("bank", vb)])
                P.op("act", lambda e: e.activation(out=do["V"][:, 0:4, kc, :], in_=bank(4, 512).rearrange("p (a b) -> p a b", a=4),
                                                   func=AF.Copy),
                     reads=[("bank", 4)], writes=[("V", hh) for hh in range(4)])
                P.op("act", lambda e: e.activation(out=do["V"][:, 4:6, kc, :], in_=bank(5, 256).rearrange("p (a b) -> p a b", a=2),
                                                   func=AF.Copy),
                     reads=[("bank", 5)], writes=[("V", 4), ("V", 5)])

            def f_unit(i, cc):
                hT = hT2[i % 2]
                fT = do["fT"]
                fb = cc
                for j in range(8):
                    P.op("pe", lambda e, j=j: e.matmul(bank(fb, TB), lhsT=do["Wf"][:, j, cc * 128:(cc + 1) * 128],
                                                       rhs=hT[:, j, :], start=(j == 0), stop=(j == 7)),
                         reads=[(tag, "Wf"), hres(i)], writes=[("bank", fb)])
                P.op("act", lambda e: e.activation(out=fT[:, cc, :], in_=bank(fb, TB), func=AF.Copy),
                     reads=[("bank", fb)], writes=[(tag, "fT", cc)])

            def g_unit(i, s):
                fT = do["fT"]; Gcs = do["Gcs"]
                gb = 2 + s
                for cc in range(2):
                    P.op("pe", lambda e, cc=cc: e.matmul(bank(gb, 512), lhsT=fT[:, cc, s * 128:(s + 1) * 128],
                                                         rhs=bcs_t[:, cc, :], start=(cc == 0), stop=(cc == 1)),
                         reads=[(tag, "fT", cc), "bcs"], writes=[("bank", gb)])
                P.op("act", lambda e: e.activation(out=Gcs[:, s, :], in_=bank(gb, 512), func=AF.Copy),
                     reads=[("bank", gb)], writes=[(tag, "Gcs", s)])

            def z_unit(i, cc, kf):
                Gcs = do["Gcs"]; tabs = do["tabs"]
                zb = (4, 5, 0, 1, 2, 3)[(cc * 4 + kf) % 6]
                k = 0
                for s in range(TB // 128):
                    for cs in range(2):
                        P.op("pe", lambda e, s=s, cs=cs, k=k: e.matmul(
                            bank(zb, 512), lhsT=Gcs[:, s, cs * 256 + cc * 128: cs * 256 + (cc + 1) * 128],
                            rhs=tabs[i % 2][:, s, cs, kf * 512:(kf + 1) * 512], start=(k == 0), stop=(k == 3)),
                            reads=[(tag, "Gcs", s), (tag, "tab", i % 2, cs)], writes=[("bank", zb)])
                        k += 1
                zsl = do["ZT"][:, cc, kf * 512:(kf + 1) * 512]
                if i == 0:
                    P.op("dve", lambda e: e.tensor_copy(out=zsl, in_=bank(zb, 512)),
                         reads=[("bank", zb)], writes=[("ZT", cc, kf)])
                else:
                    P.op("dve", lambda e: e.tensor_tensor(out=zsl, in0=bank(zb, 512), in1=zsl, op=ALU.add),
                         reads=[("bank", zb), ("ZT", cc, kf)], writes=[("ZT", cc, kf)])

            def units(i):
                n = blist[i]
                ctx = n >= NT // TB
                us = []
                if do.get("k"):
                    for h in range(6):
                        us.append(lambda h=h: rope_unit(i, h, do["Wk"], do["Wks"], (tag, "Wk"), (tag, "Wks"), do["KT"],
                                                        (lambda hh, t: ("KT", hh)), n * TB))
                    for s in range(TB // 128):
                        us.append(lambda s=s: v_unit(i, s))
                if do.get("q"):
                    for h in range(6):
                        us.append(lambda h=h: rope_unit(i, h, do["Wq"], do["Wqs"], (tag, "Wq", h), (tag, "Wqs", h), do["QT"],
                                                        (lambda hh, t: ("QT", hh, t // 512)), n * TB))
                if do.get("f"):
                    for cc in range(2):
                        us.append(lambda cc=cc: f_unit(i, cc))
                    for s in range(TB // 128):
                        us.append(lambda s=s: g_unit(i, s))
                    for cc in range(2):
                        for kf in range(4):
                            us.append(lambda cc=cc, kf=kf: z_unit(i, cc, kf))
                return us

            load(0)
            if nb > 1:
                load(1)
            prep_pre(0)
            prep_pe(0)
            prep_ab(0)
            for j in range(8):
                prep_chunk(0, j)
            for i in range(nb):
                us = units(i)
                nxt = i + 1 < nb
                if nxt:
                    prep_pre(i + 1)
                us[0]()
                if nxt:
                    prep_pe(i + 1)
                    prep_ab(i + 1)
                rest = us[1:]
                chunks = list(range(8)) if nxt else []
                per = -(-8 // len(rest))
                for u in rest:
                    u()
                    for _ in range(per):
                        if chunks:
                            prep_chunk(i + 1, chunks.pop(0))
                while chunks:
                    prep_chunk(i + 1, chunks.pop(0))
                if i + 2 < nb:
                    load(i + 2)
            return wres

        w_inv = w_in.rearrange("(j p) c -> p j c", p=128)
        w_swv = w_in_sw.rearrange("(j p) c -> p j c", p=128)

        ZTf = AR.f32(0, [128, 2, NQ])
        tab0 = arena_t[:, 16384 // 4: 16384 // 4 + 4096].bitcast(BF16).rearrange("p (s c k) -> p s c k", s=2, c=2)
        tab1 = arena_t[:, 32768 // 4: 32768 // 4 + 4096].bitcast(BF16).rearrange("p (s c k) -> p s c k", s=2, c=2)
        tabs = [tab0, tab1]
        Wf = AR.bf(49152, [128, 8, 256])
        fT = AR.bf(R_PH + 61440, [128, 2, TB])
        Gcs = AR.bf(R_PH + 62464, [128, 2, 512])
        for r in [("ZT", cc, kf) for cc in range(2) for kf in range(4)] + [("P2C", "tab", a, b) for a in range(2) for b in range(2)] + [("P2C", "Wf")]:
            P.inherit(r, W1R)
        for r in [("P2C", "fT", 0), ("P2C", "fT", 1), ("P2C", "Gcs", 0), ("P2C", "Gcs", 1)]:
            P.inherit(r, p1res)
        dma("pool", Wf[:, :, :], w_inv[:, :, 2304:2560], writes=[("P2C", "Wf")])
        resC = proj_pass("P2C", list(range(NT // TB)), p1res,
                         dict(rope=False, f=True, Wf=Wf, fT=fT, Gcs=Gcs, tabs=tabs, ZT=ZTf))
        ZTb = AR.bf(R_ZB, [128, 2, NQ])
        P.inherit("ZTb", W1R)
        for cc in range(2):
            P.op("act", lambda e, cc=cc: e.activation(out=ZTb[:, cc, :], in_=ZTf[:, cc, :], func=AF.Copy),
                 reads=[("ZT", cc, kf) for kf in range(4)], writes=["ZTb"])
        resC = resC + [("P2C", "fT", 0), ("P2C", "fT", 1), ("P2C", "Gcs", 0), ("P2C", "Gcs", 1)]

        KVh = arena_t[:, 0:(6 * 17408) // 4].bitcast(BF16).rearrange("p (h x) -> p h x", h=6)
        KT = KVh[:, :, 0:NK]
        V = KVh[:, :, NK:2 * NK].rearrange("p h (k v) -> p h k v", v=128)
        KVR = [("KT", hh) for hh in range(6)] + [("V", hh) for hh in range(6)]
        QT = AR.bf(R_QT, [128, 6, NQ])
        Wk = AR.bf(R_PH + 32768, [128, 8, 768]); Wks = AR.bf(R_PH + 45056, [128, 8, 768])
        Wv = AR.bf(R_QT, [128, 8, 768])
        ZTR = [("ZT", cc, kf) for cc in range(2) for kf in range(4)] + [("P2C", "tab", a, b) for a in range(2) for b in range(2)] + [("P2C", "Wf")]
        for r in KVR:
            P.inherit(r, W1R + ZTR)
        for r in [("P2A", "Wk"), ("P2A", "Wks")]:
            P.inherit(r, p1res)
        P.inherit(("P2A", "Wv"), W1R)
        dma("pool", Wk[:, :, :], w_inv[:, :, 768:1536], writes=[("P2A", "Wk")])
        dma("pool", Wks[:, :, :], w_swv[:, :, 768:1536], writes=[("P2A", "Wks")])
        dma("pool", Wv[:, :, :], w_inv[:, :, 1536:2304], writes=[("P2A", "Wv")])
        resA = proj_pass("P2A", list(range(NK // TB)), p1res + resC,
                         dict(rope=True, k=True, Wk=Wk, Wks=Wks, Wv=Wv, KT=KT, V=V))

        Wq = AR.bf(R_PH + 32768, [128, 8, 768]); Wqs = AR.bf(R_PH + 45056, [128, 8, 768])
        for hh in range(6):
            P.inherit(("P2B", "Wq", hh), [("P2A", "Wk")])
            P.inherit(("P2B", "Wqs", hh), [("P2A", "Wks")])
        QTR = [("QT", h, qb) for h in range(6) for qb in range(4)]
        for r in QTR:
            P.inherit(r, [("P2A", "Wv")])
        for hh in range(6):
            dma("pool", Wq[:, :, hh * 128:(hh + 1) * 128], w_inv[:, :, hh * 128:(hh + 1) * 128], writes=[("P2B", "Wq", hh)])
            dma("pool", Wqs[:, :, hh * 128:(hh + 1) * 128], w_swv[:, :, hh * 128:(hh + 1) * 128], writes=[("P2B", "Wqs", hh)])
        resB = proj_pass("P2B", list(range(NQ // TB)), resA, dict(rope=True, q=True, Wq=Wq, Wqs=Wqs, QT=QT))

        Wo = AR.bf(R_PH, [128, 8, D])
        NPT = 4
        PT = [AR.bf(R_PH + 16384 + 2048 * i, [128, 1024]) for i in range(NPT)]
        PS = [AR.bf(R_PH + 24576 + 2048 * i, [128, 1024]) for i in range(2)]
        ev = [AR.f32(R_PH + 28672 + 2048 * i, [128, 512]) for i in range(6)]
        fmT = AR.bf(R_PH + 40960, [128, 2, NQ])
        prevw = resB + [("P2B", "Wq", hh) for hh in range(6)] + [("P2B", "Wqs", hh) for hh in range(6)]
        P.inherit("Wo", prevw)
        for i in range(NPT):
            P.inherit(("PT", i), prevw)
        for i in range(2):
            P.inherit(("PS", i), prevw)
        for i in range(6):
            P.inherit(("ev", i), prevw)
        for qb in range(4):
            P.inherit(("fmT", qb), prevw)
        dma("pool", Wo[:, :, :], w_out.rearrange("(j p) c -> p j c", p=128), writes=["Wo"])

        it = 0
        pending = []
        for h in range(6):
            for qb in range(4):
                q0 = qb * 512

                def qk(kc, h=h, q0=q0):
                    b0 = 2 * (kc % 2)
                    P.op("pe", lambda e: e.matmul(bank(b0), lhsT=KT[0:64, h, kc * 128:(kc + 1) * 128], rhs=QT[0:64, h, q0:q0 + 512],
                                                  start=True, stop=True), reads=[("KT", h), ("QT", h, q0 // 512)], writes=[("bank", b0)])
                    P.op("pe", lambda e: e.matmul(bank(b0 + 1), lhsT=KT[64:128, h, kc * 128:(kc + 1) * 128], rhs=QT[64:128, h, q0:q0 + 512],
                                                  start=True, stop=True, tile_position=(64, 0)), reads=[("KT", h), ("QT", h, q0 // 512)], writes=[("bank", b0 + 1)])
                    P.op("act", lambda e: e.activation(out=PT[kc % NPT][:], in_=pp[b0 // 2][:, :], func=AF.Exp, scale=0.125),
                         reads=[("bank", b0), ("bank", b0 + 1)], writes=[("PT", kc % NPT)])

                nkc = NK // 128

                def av(kc, h=h):
                    pt = PT[kc % NPT]
                    st, sp_ = (kc == 0), (kc == nkc - 1)
                    P.op("pe", lambda e: e.matmul(bank(4), lhsT=V[:, h, kc, :], rhs=pt[:, 0:512], start=st, stop=sp_),
                         reads=[("V", h), ("PT", kc % NPT)], writes=[("bank", 4)])
                    P.op("pe", lambda e: e.matmul(bank(5), lhsT=V[:, h, kc, :], rhs=pt[:, 512:1024], start=st, stop=sp_),
                         reads=[("V", h), ("PT", kc % NPT)], writes=[("bank", 5)])

                def pairadd(p):
                    a_, b_ = PT[(2 * p) % NPT], PT[(2 * p + 1) % NPT]
                    P.op("dve", lambda e: e.tensor_tensor(out=PS[p % 2][:], in0=a_[:], in1=b_[:], op=ALU.add),
                         reads=[("PT", (2 * p) % NPT), ("PT", (2 * p + 1) % NPT)], writes=[("PS", p % 2)])

                def sums(p):
                    st, sp_ = (p == 0), (p == nkc // 2 - 1)
                    P.op("pe", lambda e: e.matmul(bank(6), lhsT=ones[:], rhs=PS[p % 2][:, 0:512], start=st, stop=sp_),
                         reads=["ones", ("PS", p % 2)], writes=[("bank", 6)])
                    P.op("pe", lambda e: e.matmul(bank(7), lhsT=ones[:], rhs=PS[p % 2][:, 512:1024], start=st, stop=sp_),
                         reads=["ones", ("PS", p % 2)], writes=[("bank", 7)])

                qk(0)
                qk(1)
                for kc in range(nkc):
                    if kc + 2 < nkc:
                        qk(kc + 2)
                    av(kc)
                    if kc % 2 == 1:
                        pairadd(kc // 2)
                        if pending:
                            pending.pop(0)()
                        if kc >= 3:
                            sums(kc // 2 - 1)
                sums(nkc // 2 - 1)
                while pending:
                    pending.pop(0)()
                c0, c1, c2, c3 = ev[0], ev[1], ev[2], ev[3]
                P.op("dve", lambda e, c0=c0: e.tensor_copy(out=c0[:], in_=bank(4)), reads=[("bank", 4)], writes=[("ev", 0)])
                P.op("dve", lambda e, c2=c2: e.tensor_copy(out=c2[:], in_=bank(6)), reads=[("bank", 6)], writes=[("ev", 2)])
                P.op("dve", lambda e, c1=c1: e.tensor_copy(out=c1[:], in_=bank(5)), reads=[("bank", 5)], writes=[("ev", 1)])
                P.op("dve", lambda e, c3=c3: e.tensor_copy(out=c3[:], in_=bank(7)), reads=[("bank", 7)], writes=[("ev", 3)])
                def mk_pending(h=h, qb=qb, q0=q0, c0=c0, c1=c1, c2=c2, c3=c3):
                    out = []
                    for (cs, ci) in ((c2, 2), (c3, 3)):
                        for qq in range(4):
                            out.append(lambda cs=cs, ci=ci, qq=qq: P.op(
                                "dve", lambda e: e.reciprocal(out=cs[:, qq * 128:(qq + 1) * 128], in_=cs[:, qq * 128:(qq + 1) * 128]),
                                reads=[("ev", ci)], writes=[("ev", ci)]))
                    out.append(lambda: P.op("dve", lambda e: e.tensor_tensor(out=c0[:], in0=c0[:], in1=c2[:], op=ALU.mult),
                                            reads=[("ev", 0), ("ev", 2)], writes=[("ev", 0)]))
                    out.append(lambda: P.op("dve", lambda e: e.tensor_tensor(out=c1[:], in0=c1[:], in1=c3[:], op=ALU.mult),
                                            reads=[("ev", 1), ("ev", 3)], writes=[("ev", 1)]))
                    out.append(lambda: P.op("dve", lambda e: e.scalar_tensor_tensor(
                        out=QT[:, h, q0:q0 + 512], in0=c1[:], scalar=neglam, in1=c0[:], op0=ALU.mult, op1=ALU.add),
                        reads=[("ev", 0), ("ev", 1), "neglam"], writes=[("QT", h, qb)]))
                    return out
                pending.extend(mk_pending())
                it += 1

        while pending:
            pending.pop(0)()

        lnb = [AR.f32(R_PH + 28672, [128, 1024]), AR.f32(R_PH + 28672 + 4096, [128, 1024])]
        xm = [AR.f32(R_PH + 49152, [128, 8, TB]), AR.f32(R_PH + 57344, [128, 8, TB])]
        P.inherit(("xm", 0), prevw)
        P.inherit(("xm", 1), prevw)
        nbm = NQ // TB

        def mload(n):
            dma("sp", xm[n % 2][:, :, :], scr[:, n * TB:(n + 1) * TB].rearrange("(j p) t -> p j t", p=128),
                reads=[("scr", n)], writes=[("xm", n % 2)])

        def subln(h, hq, sb_):
            q0 = hq * 1024
            qres = [("QT", h, 2 * hq), ("QT", h, 2 * hq + 1)]
            P.op("act", lambda e: e.activation(out=PT[sb_][:], in_=QT[:, h, q0:q0 + 1024], func=AF.Square),
                 reads=qres, writes=[("PT", sb_)])
            for t in range(2):
                bk = 2 * sb_ + t
                P.op("pe", lambda e, bk=bk, t=t: e.matmul(bank(bk), lhsT=ones[:], rhs=PT[sb_][:, t * 512:(t + 1) * 512], start=True, stop=True),
                     reads=[("PT", sb_), "ones"], writes=[("bank", bk)])
            P.op("act", lambda e: e.activation(out=lnb[sb_][:], in_=pp[sb_][:, :], func=AF.Ln,
                                               scale=1.0 / 128, bias=EPS),
                 reads=[("bank", 2 * sb_), ("bank", 2 * sb_ + 1)], writes=[("ev", 2 * sb_), ("ev", 2 * sb_ + 1)])
            P.op("act", lambda e: e.activation(out=lnb[sb_][:], in_=lnb[sb_][:], func=AF.Exp, scale=-0.5),
                 reads=[("ev", 2 * sb_), ("ev", 2 * sb_ + 1)], writes=[("ev", 2 * sb_), ("ev", 2 * sb_ + 1)])
            P.op("dve", lambda e: e.scalar_tensor_tensor(
                out=QT[:, h, q0:q0 + 1024], in0=QT[:, h, q0:q0 + 1024], scalar=sg08, in1=lnb[sb_][:], op0=ALU.mult, op1=ALU.mult),
                reads=qres + ["sg08", ("ev", 2 * sb_), ("ev", 2 * sb_ + 1)], writes=qres)

        def fm(qb):
            for oc in range(2):
                fb = 4 + oc
                for cc in range(2):
                    P.op("pe", lambda e, oc=oc, cc=cc, fb=fb: e.matmul(
                        bank(fb), lhsT=wf_t[:, cc, oc * 128:(oc + 1) * 128], rhs=ZTb[:, cc, qb * 512:(qb + 1) * 512],
                        start=(cc == 0), stop=(cc == 1)), reads=["wf", "ZTb"], writes=[("bank", fb)])
                P.op("act", lambda e, oc=oc, fb=fb: e.activation(out=fmT[:, oc, qb * 512:(qb + 1) * 512], in_=bank(fb), func=AF.Copy),
                     reads=[("bank", fb)], writes=[("fmT", qb)])

        def mix(n):
            t0 = n * TB
            for dc in range(8):
                mb = 6 + dc % 2
                for ic in range(8):
                    rhs = QT[:, ic, t0:t0 + TB] if ic < 6 else fmT[:, ic - 6, t0:t0 + TB]
                    P.op("pe", lambda e, dc=dc, ic=ic, mb=mb, rhs=rhs: e.matmul(bank(mb, TB), lhsT=Wo[:, ic, dc * 128:(dc + 1) * 128], rhs=rhs,
                                                                                start=(ic == 0), stop=(ic == 7)),
                         reads=["Wo", (("QT", ic, t0 // 512) if ic < 6 else ("fmT", t0 // 512))], writes=[("bank", mb)])
                P.op("dve", lambda e, dc=dc, mb=mb: e.scalar_tensor_tensor(
                    out=xm[n % 2][:, dc, :], in0=bank(mb, TB), scalar=sc[:, 10, dc:dc + 1], in1=xm[n % 2][:, dc, :], op0=ALU.mult, op1=ALU.add),
                    reads=[("bank", mb), ("sc", 10), ("xm", n % 2)], writes=[("xm", n % 2)])
            dma("sp", scr[:, t0:t0 + TB].rearrange("(j p) t -> p j t", p=128), xm[n % 2][:, :, :],
                reads=[("xm", n % 2)], writes=[("scr", n)])
            if n + 2 < nbm:
                mload(n + 2)

        mload(0)
        mload(1)
        itn = 0
        for h in range(6):
            subln(h, 0, itn % 2)
            itn += 1
        fm(0)
        fm(1)
        mix_after = {0: 0, 1: 1, 3: 2, 4: 3}
        for h in range(6):
            subln(h, 1, itn % 2)
            itn += 1
            if h in mix_after:
                mix(mix_after[h])
        fm(2)
        fm(3)
        for n in range(4, 8):
            mix(n)

        attn_res = KVR + ["ZTb", "Wo"] + [("fmT", qb) for qb in range(4)] + [("xm", 0), ("xm", 1)] + [("PT", i) for i in range(NPT)] + [("PS", 0), ("PS", 1)] + [("ev", i) for i in range(6)] + prevw + QTR

        Wg2 = AR.bf(0, [128, 8, DFF]); Wu2 = AR.bf(45056, [128, 8, DFF]); Wd2 = AR.bf(90112, [128, NF, D])
        w2gv = w2g.rearrange("(j p) f -> p j f", p=128)
        w2uv = w2u.rearrange("(j p) f -> p j f", p=128)
        w2dv = w2d.rearrange("(i p) d -> p i d", p=128)
        for (nm, Wt, src, base) in (("g", Wg2, w2gv, 0), ("u", Wu2, w2uv, 45056)):
            for j in range(8):
                lo, hi = base + j * 5632, base + (j + 1) * 5632 - 1
                heads = list(range(lo // 17408, min(5, hi // 17408) + 1))
                olds = [("KT", hh) for hh in heads] + [("V", hh) for hh in heads]
                if hi >= 104448:
                    olds += QTR
                P.inherit(("W2", nm, j), olds)
                dma("pool", Wt[:, j, :], src[:, j, :], writes=[("W2", nm, j)])
        for pr in range(11):
            P.inherit(("W2", "d", pr), [("KT", 5), ("V", 5), "ZTb"] + QTR)
            dma("pool", Wd2[:, 2 * pr:2 * pr + 2, :], w2dv[:, 2 * pr:2 * pr + 2, :], writes=[("W2", "d", pr)])
        W2 = (Wg2, Wu2, Wd2)
        for r in [("P5", "xb", 0), ("P5", "xb", 1), ("P5", "xb", 2), ("P5", "sq2"), ("P5", "sd2"), ("P5", "rstd2"), ("P5", "sq"), ("P5", "hT"), ("P5", "sd"), ("P5", "rstd"), ("P5", "tbuf"),
                  ("P5", "sg", 0), ("P5", "sg", 1)] + [("P5", "aT", i) for i in range(NF)]:
            P.inherit(r, attn_res)
        blocks5 = []
        for n in range(NQ // TB):
            t0 = n * TB
            blocks5.append(dict(src=scr[:, t0:t0 + TB], srcres=("scr", n), dst=outT[:, t0:t0 + TB], dstres=("out", n), A=11, B=12, G=13))
        ffn_pass("P5", "W2", W2, blocks5, final=True, byj=True)
        P.op("sp", lambda e: None, reads=[("out", n) for n in range(NQ // TB)])
        P.emit()
    return nc


_NC_CACHE = {}


def _host_tables():
    bf = ml_dtypes.bfloat16
    tabs = {}
    pair = np.arange(128) % 16
    part = (np.arange(128) // 16) % 2
    axis = (np.arange(128) // 32) % 2
    inv_freq = (np.float32(10000.0) ** (-(np.arange(16, dtype=np.float32)) / np.float32(16))).astype(np.float32)
    d = np.arange(64)
    ang64 = 2.0 * np.pi * ((d[:, None] * d[None, :]) % 64) / 64.0
    BC = np.zeros((256, 256), np.float64); BS = np.zeros((256, 256), np.float64)
    for g in range(4):
        BC[g * 64:(g + 1) * 64, g * 64:(g + 1) * 64] = np.cos(ang64)
        BS[g * 64:(g + 1) * 64, g * 64:(g + 1) * 64] = np.sin(ang64)
    tabs["bcs"] = np.concatenate([BC, BS], axis=1).astype(np.float32).astype(bf)
    for half in range(2):
        nloc = np.concatenate([half * NQ + np.arange(NQ), (1 - half) * NQ + np.arange(NQ)])
        row = (nloc // 64).astype(np.float32)
        col = (nloc % 64).astype(np.float32)
        pos = np.where(axis[:, None] == 0, row[None, :], col[None, :]).astype(np.float32)
        ang = (pos * inv_freq[pair][:, None]).astype(np.float32)
        cos = np.cos(ang).astype(np.float32)
        sin = np.sin(ang).astype(np.float32)
        sgn = np.where(part == 0, -1.0, 1.0).astype(np.float32)[:, None]
        tabs[("ropec", half)] = np.ascontiguousarray(cos)
        tabs[("ropes", half)] = np.ascontiguousarray(sin * sgn)
        kk = (half * NQ + np.arange(NQ)).astype(np.int64)
        prod = (nloc.astype(np.int64)[:, None] * kk[None, :]) % NT
        a = 2.0 * np.pi * prod.astype(np.float64) / NT
        tabs[("dftc", half)] = (np.cos(a) / 512.0).astype(np.float32).astype(bf)
        tabs[("dfts", half)] = (-np.sin(a) / 512.0).astype(np.float32).astype(bf)
    return tabs


def kernel(x, c, ctx, c_ctx, w_ada, b_ada, norm1_g, ffn1_w_gate, ffn1_w_up, ffn1_w_down,
           norm_mix_g, w_in, lambda_q1, lambda_k1, lambda_q2, lambda_k2, subln_g, w_fourier,
           w_out, norm2_g, ffn2_w_gate, ffn2_w_up, ffn2_w_down, final_norm_g):
    f = lambda a: np.ascontiguousarray(np.asarray(a, dtype=np.float32))
    x = f(x); c = f(c); ctx = f(ctx); c_ctx = f(c_ctx)
    if "nc" not in _NC_CACHE:
        _NC_CACHE["nc"] = build_nc()
        _NC_CACHE["tabs"] = _host_tables()
    nc = _NC_CACHE["nc"]
    tabs = _NC_CACHE["tabs"]

    def vT(v):
        return np.ascontiguousarray(f(v).reshape(8, 128).T)

    gains = np.ascontiguousarray(np.stack([vT(norm1_g[0]), vT(norm_mix_g[0]), vT(norm2_g[0]), vT(final_norm_g)], axis=1))
    b_adaT = np.ascontiguousarray(f(b_ada[0]).reshape(72, 128).T)
    w_in0 = f(w_in[0])
    cols = np.arange(1536)
    dd = cols % 64
    partner = (cols - dd) + (dd // 32) * 32 + (1 - (dd // 16) % 2) * 16 + dd % 16
    w_in_sw = np.ascontiguousarray(w_in0[:, partner])
    lamin = np.stack([f(lambda_q1[0]), f(lambda_k1[0]), f(lambda_q2[0]), f(lambda_k2[0])], axis=0)
    lamin = np.ascontiguousarray(np.broadcast_to(lamin[None], (128, 4, 64)))
    sublnT = np.ascontiguousarray(f(subln_g[0]).reshape(128, 1))
    shared = dict(
        w_ada=f(w_ada[0]), b_adaT=b_adaT, gains=gains,
        w1g=f(ffn1_w_gate[0]), w1u=f(ffn1_w_up[0]), w1d=f(ffn1_w_down[0]),
        w2g=f(ffn2_w_gate[0]), w2u=f(ffn2_w_up[0]), w2d=f(ffn2_w_down[0]),
        w_in=w_in0, w_in_sw=w_in_sw, lamin=lamin, sublnT=sublnT, w_fo=f(w_fourier[0]),
        bcs=tabs["bcs"], w_out=f(w_out[0]),
    )
    in_maps = []
    for core in range(8):
        b, half = core // 2, core % 2
        xb = x[b]
        xloc = np.concatenate([xb[half * NQ:(half + 1) * NQ], xb[(1 - half) * NQ:(2 - half) * NQ]], axis=0)
        m = dict(shared)
        m["xT"] = np.ascontiguousarray(xloc.T)
        m["ctxT"] = np.ascontiguousarray(ctx[b].T)
        m["cc2"] = np.ascontiguousarray(np.stack([vT(c[b]), vT(c_ctx)], axis=2))
        m["ropec"] = tabs[("ropec", half)]
        m["ropes"] = tabs[("ropes", half)]
        m["dftc"] = tabs[("dftc", half)]
        m["dfts"] = tabs[("dfts", half)]
        in_maps.append(m)
    res = run_bass_kernel_spmd(nc, in_maps, core_ids=list(range(8)))
    out = np.empty((4, NT, D), np.float32)
    for core in range(8):
        b, half = core // 2, core % 2
        out[b, half * NQ:(half + 1) * NQ, :] = np.asarray(res.results[core]["outT"]).T
    return out
```

```python
from contextlib import ExitStack

import numpy as np
import ml_dtypes
import concourse.bass as bass
import concourse.mybir as mybir
from concourse.bass_utils import run_bass_kernel_spmd

F32 = mybir.dt.float32
BF16 = mybir.dt.bfloat16
AF = mybir.ActivationFunctionType
ALU = mybir.AluOpType
AX = mybir.AxisListType

D = 1024
NT = 4096
NCTX = 256
NK = NT + NCTX
NQ = 2048
DFF = 2816
NF = DFF // 128
TB = 256
EPS = 1e-6
LAMBDA_INIT = 0.2

DEBUG = False


class Prog:
    ENGS = ("pe", "act", "dve", "pool", "sp")
    GROUP = 2000
    NDMA = 20

    def __init__(self, nc):
        self.nc = nc
        self.ops = {e: [] for e in self.ENGS}
        self.last_w = {}
        self.readers = {}
        self.dma_rr = {e: 0 for e in self.ENGS}
        self.dma_last = {}
        self.dma_cnt = {}

    def _add_reader(self, res, me, is_dma):
        d = self.readers.setdefault(res, {})
        if is_dma:
            d.setdefault("dma", []).append(me)
        else:
            d[me[0]] = me

    def _reader_list(self, res):
        d = self.readers.get(res)
        if not d:
            return []
        out = []
        for k, v in d.items():
            if k == "dma":
                out.extend(v)
            else:
                out.append(v)
        return out

    def op(self, eng, fn, reads=(), writes=(), dma=False):
        idx = len(self.ops[eng])
        me = (eng, idx)
        deps = set()
        for r in reads:
            w = self.last_w.get(r)
            if w is not None:
                deps.add(w)
        for w_ in writes:
            w = self.last_w.get(w_)
            if w is not None:
                deps.add(w)
            deps.update(self._reader_list(w_))
        rec = dict(fn=fn, deps=deps, signal=False, dma=dma, sig=None)
        if dma:
            k = self.dma_rr[eng] % self.NDMA
            self.dma_rr[eng] += 1
            key = ("dma", eng, k)
            prev = self.dma_last.get(key)
            if prev is not None:
                deps.add(prev)
            self.dma_last[key] = me
            n = self.dma_cnt.get(key, 0) + 1
            self.dma_cnt[key] = n
            rec["sig"] = (key, 16 * n)
        deps.discard(me)
        if eng == "pe":
            deps = {d for d in deps if d[0] != "pe"}
        rec["deps"] = deps
        self.ops[eng].append(rec)
        for r in reads:
            self._add_reader(r, me, dma)
        for w_ in writes:
            self.last_w[w_] = me
            self.readers[w_] = {}
        return me

    def inherit(self, new, olds):
        d = self.readers.setdefault(new, {})
        for o in olds:
            w = self.last_w.get(o)
            if w is not None:
                d.setdefault("dma", []).append(w)
            for r in self._reader_list(o):
                d.setdefault("dma", []).append(r)

    def emit(self):
        nc = self.nc
        for e in self.ENGS:
            for rec in self.ops[e]:
                for (e2, i2) in rec["deps"]:
                    t = self.ops[e2][i2]
                    if not t["dma"]:
                        t["signal"] = True
        semkeys = []
        for e in self.ENGS:
            k = 0
            for rec in self.ops[e]:
                if not rec["dma"] and rec["signal"]:
                    rec["sig"] = (("eng", e, k // self.GROUP), k % self.GROUP + 1)
                    k += 1
                if rec["sig"] is not None and rec["sig"][0] not in semkeys:
                    semkeys.append(rec["sig"][0])
        with ExitStack() as es:
            sems = {}
            for k in semkeys:
                sems[k] = es.enter_context(nc.semaphore("s_" + "_".join(str(x) for x in k)))
            block = es.enter_context(nc.Block())
            handles = {"pe": block.tensor, "act": block.scalar, "dve": block.vector,
                       "pool": block.gpsimd, "sp": block.sync}
            for e in self.ENGS:
                if not self.ops[e]:
                    continue

                def body(engh, e=e):
                    waited = {}
                    for rec in self.ops[e]:
                        need = {}
                        for (e2, i2) in rec["deps"]:
                            sk, v = self.ops[e2][i2]["sig"]
                            if waited.get(sk, 0) >= v:
                                continue
                            if need.get(sk, 0) < v:
                                need[sk] = v
                        for sk, v in need.items():
                            engh.wait_ge(sems[sk], v)
                            waited[sk] = v
                        ins = rec["fn"](engh)
                        if ins is not None and rec["sig"] is not None and (rec["dma"] or rec["signal"]):
                            sk, v = rec["sig"]
                            ins.then_inc(sems[sk], 16 if rec["dma"] else 1)
                handles[e](body)


def _prod(s):
    r = 1
    for x in s:
        r *= x
    return r


class Arena:
    def __init__(self, t):
        self.t = t

    def _shape(self, v, shape):
        if len(shape) == 2:
            return v
        if len(shape) == 3:
            return v.rearrange("p (a b) -> p a b", a=shape[1])
        raise ValueError

    def f32(self, off, shape):
        n = _prod(shape[1:])
        assert off % 4 == 0
        v = self.t[:, off // 4: off // 4 + n]
        return self._shape(v, shape)

    def bf(self, off, shape):
        n = _prod(shape[1:])
        assert off % 4 == 0 and n % 2 == 0
        v = self.t[:, off // 4: off // 4 + n // 2].bitcast(BF16)
        return self._shape(v, shape)


ARENA_WORDS = 50688
R_KT = 0
R_V = 52224
R_QT = 104448
R_ZB = 129024
R_PH = 137216


def build_nc():
    nc = bass.Bass("TRN2", target_bir_lowering=False)

    def din(name, shape, dt=F32):
        return nc.dram_tensor(name, shape, dt, kind="ExternalInput").ap()

    xT = din("xT", [D, NT])
    ctxT = din("ctxT", [D, NCTX])
    cc2 = din("cc2", [128, 8, 2])
    w_ada = din("w_ada", [D, 9 * D])
    b_adaT = din("b_adaT", [128, 72])
    gains = din("gains", [128, 4, 8])
    w1g = din("w1g", [D, DFF]); w1u = din("w1u", [D, DFF]); w1d = din("w1d", [DFF, D])
    w2g = din("w2g", [D, DFF]); w2u = din("w2u", [D, DFF]); w2d = din("w2d", [DFF, D])
    w_in = din("w_in", [D, 2560])
    w_in_sw = din("w_in_sw", [D, 1536])
    lamin = din("lamin", [128, 4, 64])
    sublnT = din("sublnT", [128, 1])
    w_fo = din("w_fo", [256, 256])
    bcs = din("bcs", [256, 512], BF16)
    w_out = din("w_out", [D, D])
    ropec = din("ropec", [128, NT])
    ropes = din("ropes", [128, NT])
    dftc = din("dftc", [NT, NQ], BF16)
    dfts = din("dfts", [NT, NQ], BF16)
    outT = nc.dram_tensor("outT", [D, NQ], F32, kind="ExternalOutput").ap()
    scr = nc.dram_tensor("scr", [D, NK], F32, kind="Internal").ap()

    es = ExitStack()
    with es:
        def sb(name, shape, dt):
            return es.enter_context(nc.sbuf_tensor(name, shape, dt))

        arena_t = sb("arena", [128, ARENA_WORDS], F32)
        AR = Arena(arena_t)
        pp = [es.enter_context(nc.psum_tensor("ps%d" % i, [128, 1024], F32)) for i in range(4)]
        ones = sb("ones", [128, 128], BF16)
        s2 = sb("s2", [128, 8, 2], F32)
        s2b = sb("s2b", [128, 8, 2], BF16)
        badaT = sb("badaT", [128, 72], F32)
        gn = sb("gn", [128, 4, 8], F32)
        modT = sb("modT", [128, 72, 2], F32)
        sc = sb("sc", [128, 14, 8], F32)
        lam_t = sb("lam_t", [128, 4, 64], F32)
        lam_w = sb("lam_w", [128, 8], F32)
        sgt = sb("sgt", [128, 2], F32)
        wf_t = sb("wf_t", [128, 2, 256], BF16)
        bcs_t = sb("bcs_t", [128, 2, 512], BF16)

        P = Prog(nc)

        def bank(b, n=512):
            return pp[b // 2][:, (b % 2) * 512: (b % 2) * 512 + n]

        def dma(eng, out, in_, reads=(), writes=()):
            return P.op(eng, lambda e: e.dma_start(out=out, in_=in_), reads=reads, writes=writes, dma=True)

        P.op("dve", lambda e: e.memset(ones[:], 1.0), writes=["ones"])
        dma("sp", s2[:], cc2, writes=["s2"])
        dma("sp", badaT[:], b_adaT, writes=["badaT"])
        dma("sp", gn[:], gains, writes=["gn"])
        dma("sp", lam_t[:], lamin, writes=["lam_t"])
        dma("sp", sgt[:, 0:1], sublnT, writes=["sgt0"])
        dma("pool", wf_t[:], w_fo.rearrange("(c p) n -> p c n", p=128), writes=["wf"])
        dma("sp", bcs_t[:], bcs.rearrange("(c p) n -> p c n", p=128), writes=["bcs"])
        P.op("act", lambda e: e.activation(out=s2[:], in_=s2[:], func=AF.Silu), reads=["s2"], writes=["s2"])

        def load_ffn_weights(tag, wg, wu, wd, olds):
            Wg = AR.bf(0, [128, 8, DFF]); Wu = AR.bf(45056, [128, 8, DFF]); Wd = AR.bf(90112, [128, NF, D])
            wgv = wg.rearrange("(j p) f -> p j f", p=128)
            wuv = wu.rearrange("(j p) f -> p j f", p=128)
            wdv = wd.rearrange("(i p) d -> p i d", p=128)
            for pr in range(11):
                for nm, Wt, src in (("g", Wg, wgv), ("u", Wu, wuv)):
                    res = (tag, nm, pr)
                    P.inherit(res, olds(nm, pr))
                    dma("pool", Wt[:, :, pr * 256:(pr + 1) * 256], src[:, :, pr * 256:(pr + 1) * 256], writes=[res])
            for pr in range(11):
                res = (tag, "d", pr)
                P.inherit(res, olds("d", pr))
                dma("pool", Wd[:, 2 * pr:2 * pr + 2, :], wdv[:, 2 * pr:2 * pr + 2, :], writes=[res])
            return Wg, Wu, Wd

        ST0 = 183296
        stg = [AR.bf(ST0, [128, 8, 512]), AR.bf(ST0 + 8192, [128, 8, 512])]
        wav = w_ada.rearrange("(j p) n -> p j n", p=128)
        P.op("dve", lambda e: e.tensor_copy(out=s2b[:], in_=s2[:]), reads=["s2"], writes=["s2b"])

        def ada_load(nb):
            dma("pool", stg[nb % 2][:, :, :], wav[:, :, nb * 512:(nb + 1) * 512], writes=[("stg", nb % 2)])

        def ada_compute(nb):
            sbuf = stg[nb % 2]
            k = nb // 2
            for q in range(4):
                n = nb * 4 + q
                for j in range(8):
                    P.op("pe", lambda e, sbuf=sbuf, q=q, j=j, n=n: e.matmul(
                        pp[3][:, 512 + 2 * n:512 + 2 * n + 2], lhsT=sbuf[:, j, q * 128:(q + 1) * 128], rhs=s2b[:, j, :],
                        start=(j == 0), stop=(j == 7)), reads=[("stg", nb % 2), "s2b"], writes=["ps_ada"])
            if nb % 2 == 1:
                P.op("dve", lambda e, k=k: e.tensor_tensor(
                    out=modT[:, 8 * k:8 * k + 8, :], in0=pp[3][:, 512 + 16 * k:512 + 16 * k + 16].rearrange("p (n m) -> p n m", m=2),
                    in1=badaT[:, 8 * k:8 * k + 8].unsqueeze(2).to_broadcast([128, 8, 2]), op=ALU.add),
                    reads=["ps_ada", "badaT"], writes=[("modT", k)])

        ada_load(0)
        ada_load(1)
        for nb in range(6):
            ada_compute(nb)
            if nb + 2 < 6:
                ada_load(nb + 2)
        Wg, Wu, Wd = load_ffn_weights("W1", w1g, w1u, w1d, lambda nm, pr: [])

        def mod(k, m):
            return modT[:, k * 8:(k + 1) * 8, m]

        def mk_scale(idx, gi, k, m):
            P.op("dve", lambda e: e.scalar_tensor_tensor(out=sc[:, idx, :], in0=mod(k, m), scalar=1.0, in1=gn[:, gi, :],
                                                         op0=ALU.add, op1=ALU.mult), reads=[("modT", k), "gn"], writes=[("sc", idx)])

        def mk_copy(idx, k, m, f):
            P.op("dve", lambda e: e.tensor_scalar(out=sc[:, idx, :], in0=mod(k, m), scalar1=f, scalar2=None, op0=ALU.mult),
                 reads=[("modT", k)], writes=[("sc", idx)])

        mk_scale(0, 0, 1, 0); mk_copy(1, 0, 0, 1.0); mk_copy(2, 2, 0, 0.5)
        mk_scale(3, 0, 1, 1); mk_copy(4, 0, 1, 1.0); mk_copy(5, 2, 1, 0.5)

        def ada_hook(n):
            if n == 0:
                ada_load(6)
                ada_load(7)
                return
            nb = 5 + n
            ada_compute(nb)
            if nb + 2 < 18:
                ada_load(nb + 2)
            if nb == 17:
                mk_scale(6, 1, 4, 0); mk_copy(7, 3, 0, 1.0)
                mk_scale(8, 1, 4, 1); mk_copy(9, 3, 1, 1.0)
                mk_copy(10, 5, 0, 1.0)
                mk_scale(11, 2, 7, 0); mk_copy(12, 6, 0, 1.0); mk_copy(13, 8, 0, 0.5)
        SCR = [("sc", i) for i in range(14)]

        P.op("dve", lambda e: e.tensor_tensor(out=lam_t[:, 0, :], in0=lam_t[:, 0, :], in1=lam_t[:, 1, :], op=ALU.mult),
             reads=["lam_t"], writes=["lam_t"])
        P.op("dve", lambda e: e.tensor_tensor(out=lam_t[:, 2, :], in0=lam_t[:, 2, :], in1=lam_t[:, 3, :], op=ALU.mult),
             reads=["lam_t"], writes=["lam_t"])
        P.op("dve", lambda e: e.reduce_sum(out=lam_w[:, 0:1], in_=lam_t[:, 0, :], axis=AX.X), reads=["lam_t"], writes=["lam_w"])
        P.op("dve", lambda e: e.reduce_sum(out=lam_w[:, 1:2], in_=lam_t[:, 2, :], axis=AX.X), reads=["lam_w", "lam_t"], writes=["lam_w"])
        P.op("act", lambda e: e.activation(out=lam_w[:, 2:4], in_=lam_w[:, 0:2], func=AF.Exp), reads=["lam_w"], writes=["lam_w"])
        P.op("dve", lambda e: e.tensor_tensor(out=lam_w[:, 4:5], in0=lam_w[:, 3:4], in1=lam_w[:, 2:3], op=ALU.subtract),
             reads=["lam_w"], writes=["lam_w"])
        P.op("dve", lambda e: e.tensor_scalar(out=lam_w[:, 5:6], in0=lam_w[:, 4:5], scalar1=-LAMBDA_INIT, scalar2=None, op0=ALU.add),
             reads=["lam_w"], writes=["neglam"])
        neglam = lam_w[:, 5:6]
        P.op("dve", lambda e: e.tensor_scalar(out=sgt[:, 1:2], in0=sgt[:, 0:1], scalar1=1.0 - LAMBDA_INIT, scalar2=None, op0=ALU.mult),
             reads=["sgt0"], writes=["sg08"])
        sg08 = sgt[:, 1:2]

        def work(base):
            w = {}
            w["xb"] = [AR.f32(base, [128, 8, TB]), AR.f32(base + 8192, [128, 8, TB])]
            w["sq"] = AR.bf(base + 16384, [128, 8, TB])
            w["hT"] = AR.bf(base + 20480, [128, 8, TB])
            w["sd"] = AR.f32(base + 24576, [128, TB])
            w["rstd"] = AR.f32(base + 25600, [128, TB])
            return w

        def norm_stage_pre(tag, w, xbuf, xres, n):
            P.op("act", lambda e: e.activation(out=w["sq"][:], in_=xbuf[:], func=AF.Square), reads=[xres], writes=[(tag, "sq")])

        def norm_stage_pe(tag, w, sbank):
            for j in range(8):
                P.op("pe", lambda e, j=j: e.matmul(bank(sbank, TB), lhsT=ones[:], rhs=w["sq"][:, j, :], start=(j == 0), stop=(j == 7)),
                     reads=[(tag, "sq"), "ones"], writes=[("bank", sbank)])

        def norm_stage_post(tag, w, xbuf, xres, tmp, tmpres, Ai, Bi, sbank, hres, hT=None):
            hT = w["hT"] if hT is None else hT
            P.op("act", lambda e: e.activation(out=w["sd"][:], in_=bank(sbank, TB), func=AF.Ln, scale=1.0 / D, bias=EPS),
                 reads=[("bank", sbank)], writes=[(tag, "sd")])
            P.op("act", lambda e: e.activation(out=w["rstd"][:], in_=w["sd"][:], func=AF.Exp, scale=-0.5),
                 reads=[(tag, "sd")], writes=[(tag, "rstd")])
            P.op("dve", lambda e: e.tensor_tensor(out=tmp[:], in0=xbuf[:], in1=w["rstd"][:].unsqueeze(1).to_broadcast([128, 8, TB]),
                                                  op=ALU.mult), reads=[xres, (tag, "rstd")], writes=[tmpres])
            for j in range(8):
                if Bi is None:
                    P.op("act", lambda e, j=j: e.activation(out=hT[:, j, :], in_=tmp[:, j, :], func=AF.Identity,
                                                            scale=sc[:, Ai, j:j + 1]),
                         reads=[tmpres, ("sc", Ai)], writes=[hres])
                else:
                    P.op("act", lambda e, j=j: e.activation(out=hT[:, j, :], in_=tmp[:, j, :], func=AF.Identity,
                                                            scale=sc[:, Ai, j:j + 1], bias=sc[:, Bi, j:j + 1]),
                         reads=[tmpres, ("sc", Ai), ("sc", Bi)], writes=[hres])

        def ffn_pass(tag, Wtag, Wts, blocks, final=None, after_block=None, byj=False, after_loads=None):
            Wg, Wu, Wd = Wts
            base = 135168
            w = work(base)
            aT = AR.bf(base + 26624, [128, NF, TB])
            sg = [AR.f32(base + 37888, [128, TB]), AR.f32(base + 38912, [128, TB])]
            tbuf = AR.f32(base + 39936, [128, 8, TB])
            nb = len(blocks)
            NX = 3 if final else 2
            xbs = list(w["xb"]) + ([AR.f32(base + 48128, [128, 8, TB])] if final else [])
            sq2 = AR.bf(base + 56320, [128, 8, TB]) if final else None
            sd2 = AR.f32(base + 60416, [128, TB]) if final else None
            rstd2 = AR.f32(base + 61440, [128, TB]) if final else None
            WR = [(Wtag, nm, pr) for nm in "gud" for pr in range(11)]

            def xres(n):
                return (tag, "xb", n % NX)

            def load(n):
                b = blocks[n]
                dma("sp", xbs[n % NX][:, :, :], b["src"].rearrange("(j p) t -> p j t", p=128),
                    reads=[b["srcres"]], writes=[xres(n)])

            def pre(n):
                norm_stage_pre(tag, w, xbs[n % NX], xres(n), n)

            def stats_and_mod(n):
                b = blocks[n]
                norm_stage_pe(tag, w, 6)
                norm_stage_post(tag, w, xbs[n % NX], xres(n), tbuf, (tag, "tbuf"), b["A"], b["B"], 6, (tag, "hT"))

            def gu(n, mid_hook, chunk_hooks=None):
                for i in range(NF):
                    if chunk_hooks and i in chunk_hooks:
                        chunk_hooks[i]()
                    gb, ub = i % 2, 2 + i % 2
                    for j in range(8):
                        P.op("pe", lambda e, i=i, j=j, gb=gb: e.matmul(bank(gb, TB), lhsT=Wg[:, j, i * 128:(i + 1) * 128],
                                                                        rhs=w["hT"][:, j, :], start=(j == 0), stop=(j == 7)),
                             reads=[(Wtag, "g", j if byj else i // 2), (tag, "hT")], writes=[("bank", gb)])
                    for j in range(8):
                        P.op("pe", lambda e, i=i, j=j, ub=ub: e.matmul(bank(ub, TB), lhsT=Wu[:, j, i * 128:(i + 1) * 128],
                                                                        rhs=w["hT"][:, j, :], start=(j == 0), stop=(j == 7)),
                             reads=[(Wtag, "u", j if byj else i // 2), (tag, "hT")], writes=[("bank", ub)])
                    P.op("act", lambda e, i=i, gb=gb: e.activation(out=sg[i % 2][:], in_=bank(gb, TB), func=AF.Silu),
                         reads=[("bank", gb)], writes=[(tag, "sg", i % 2)])
                    P.op("dve", lambda e, i=i, ub=ub: e.tensor_tensor(out=aT[:, i, :], in0=sg[i % 2][:], in1=bank(ub, TB), op=ALU.mult),
                         reads=[(tag, "sg", i % 2), ("bank", ub)], writes=[(tag, "aT", i)])
                    if i == 10 and mid_hook is not None:
                        mid_hook()

            def down(n):
                b = blocks[n]
                xb = xbs[n % NX]
                for dc in range(8):
                    db = 4 + dc % 2
                    for i in range(NF):
                        P.op("pe", lambda e, i=i, dc=dc, db=db: e.matmul(bank(db, TB), lhsT=Wd[:, i, dc * 128:(dc + 1) * 128],
                                                                          rhs=aT[:, i, :], start=(i == 0), stop=(i == NF - 1)),
                             reads=[(Wtag, "d", i // 2), (tag, "aT", i)], writes=[("bank", db)])
                    P.op("dve", lambda e, dc=dc, db=db, xb=xb, G=b["G"]: e.scalar_tensor_tensor(
                        out=xb[:, dc, :], in0=bank(db, TB), scalar=sc[:, G, dc:dc + 1], in1=xb[:, dc, :], op0=ALU.mult, op1=ALU.add),
                        reads=[("bank", db), ("sc", b["G"]), xres(n)], writes=[xres(n)])

            def fin_pre(n):
                xb = xbs[n % NX]
                P.op("act", lambda e: e.activation(out=sq2[:], in_=xb[:], func=AF.Square), reads=[xres(n)], writes=[(tag, "sq2")])

            def fin_pe(n):
                for j in range(8):
                    P.op("pe", lambda e, j=j: e.matmul(bank(7, TB), lhsT=ones[:], rhs=sq2[:, j, :], start=(j == 0), stop=(j == 7)),
                         reads=[(tag, "sq2"), "ones"], writes=[("bank", 7)])

            def fin_rstd(n):
                P.op("act", lambda e: e.activation(out=sd2[:], in_=bank(7, TB), func=AF.Ln, scale=1.0 / D, bias=EPS),
                     reads=[("bank", 7)], writes=[(tag, "sd2")])
                P.op("act", lambda e: e.activation(out=rstd2[:], in_=sd2[:], func=AF.Exp, scale=-0.5),
                     reads=[(tag, "sd2")], writes=[(tag, "rstd2")])

            def fin_scale(n, j):
                xb = xbs[n % NX]
                P.op("dve", lambda e: e.scalar_tensor_tensor(out=xb[:, j, :], in0=xb[:, j, :], scalar=gn[:, 3, j:j + 1],
                                                             in1=rstd2[:], op0=ALU.mult, op1=ALU.mult),
                     reads=[xres(n), (tag, "rstd2"), "gn"], writes=[xres(n)])

            def fin_hooks(n):
                hk = {0: (lambda: fin_pre(n)), 2: (lambda: fin_pe(n)), 3: (lambda: fin_rstd(n))}
                for j in range(8):
                    hk[4 + j] = (lambda j=j: fin_scale(n, j))

                def fin_out():
                    store(n)
                    if n + 3 < nb:
                        load(n + 3)
                hk[12] = fin_out
                return hk

            def store(n):
                b = blocks[n]
                dma("sp", b["dst"].rearrange("(j p) t -> p j t", p=128), xbs[n % NX][:, :, :],
                    reads=[xres(n)], writes=[b["dstres"]])

            for n0 in range(min(NX, nb)):
                load(n0)
            if after_loads is not None:
                after_loads()
            pre(0)
            stats_and_mod(0)
            for n in range(nb):
                hook = (lambda n=n: pre(n + 1)) if n + 1 < nb else None
                gu(n, hook, fin_hooks(n - 1) if (final and n >= 1) else None)
                if n + 1 < nb:
                    stats_and_mod(n + 1)
                down(n)
                if final:
                    if n == nb - 1:
                        hk = fin_hooks(n)
                        for i in sorted(hk):
                            hk[i]()
                else:
                    store(n)
                    if n + 2 < nb:
                        load(n + 2)
                if after_block and n in after_block:
                    after_block[n]()
            return [(tag, "xb", 0), (tag, "xb", 1), (tag, "xb", 2), (tag, "sq2"), (tag, "sd2"), (tag, "rstd2"), (tag, "sq"), (tag, "hT"), (tag, "sd"), (tag, "rstd"), (tag, "tbuf"),
                    (tag, "sg", 0), (tag, "sg", 1)] + [(tag, "aT", i) for i in range(NF)]

        blocks1 = []
        for n in range(NK // TB):
            t0 = n * TB
            if t0 < NT:
                src = xT[:, t0:t0 + TB]; A, B, G = 0, 1, 2
            else:
                src = ctxT[:, t0 - NT:t0 - NT + TB]; A, B, G = 3, 4, 5
            blocks1.append(dict(src=src, srcres=("in", n), dst=scr[:, t0:t0 + TB], dstres=("scr", n), A=A, B=B, G=G))
        p1res = ffn_pass("P1", "W1", (Wg, Wu, Wd), blocks1,
                         after_block={n: (lambda n=n: ada_hook(n)) for n in range(13)})
        p1res = p1res + [("stg", 0), ("stg", 1)]
        W1R = [("W1", nm, pr) for nm in "gud" for pr in range(11)]

        def proj_pass(tag, blist, olds_work, do):
            w = work(R_PH)
            hT2 = [w["hT"], AR.bf(R_PH + 57344, [128, 8, TB])]
            cosb = [AR.f32(R_PH + 26624, [128, TB]), AR.f32(R_PH + 27648, [128, TB])]
            sinb = [AR.f32(R_PH + 28672, [128, TB]), AR.f32(R_PH + 29696, [128, TB])]
            t1 = AR.f32(R_PH + 30720, [128, TB]); t2 = AR.f32(R_PH + 31744, [128, TB])
            wres = [(tag, "xb", 0), (tag, "xb", 1),
                    (tag, "sq"), (tag, "hT", 0), (tag, "hT", 1), (tag, "sd"), (tag, "rstd"),
                    (tag, "cos", 0), (tag, "cos", 1), (tag, "sin", 0), (tag, "sin", 1), (tag, "t1"), (tag, "t2")]
            for r in wres:
                P.inherit(r, olds_work)
            nb = len(blist)

            def xres(i):
                return (tag, "xb", i % 2)

            def hres(i):
                return (tag, "hT", i % 2)

            def load(i):
                n = blist[i]
                dma("sp", w["xb"][i % 2][:, :, :], scr[:, n * TB:(n + 1) * TB].rearrange("(j p) t -> p j t", p=128),
                    reads=[("scr", n)], writes=[xres(i)])
                if do["rope"] and n < NT // TB:
                    dma("sp", cosb[i % 2][:], ropec[:, n * TB:(n + 1) * TB], writes=[(tag, "cos", i % 2)])
                    dma("sp", sinb[i % 2][:], ropes[:, n * TB:(n + 1) * TB], writes=[(tag, "sin", i % 2)])
                if do.get("f"):
                    tabs = do["tabs"]
                    dma("sp", tabs[i % 2][:, :, 0, :], dftc[n * TB:(n + 1) * TB, :].rearrange("(s p) k -> p s k", p=128),
                        writes=[(tag, "tab", i % 2, 0)])
                    dma("sp", tabs[i % 2][:, :, 1, :], dfts[n * TB:(n + 1) * TB, :].rearrange("(s p) k -> p s k", p=128),
                        writes=[(tag, "tab", i % 2, 1)])

            def prep_pre(i):
                norm_stage_pre(tag, w, w["xb"][i % 2], xres(i), i)

            def prep_pe(i):
                norm_stage_pe(tag, w, 6)

            def prep_ab(i):
                P.op("act", lambda e: e.activation(out=w["sd"][:], in_=bank(6, TB), func=AF.Ln, scale=1.0 / D, bias=EPS),
                     reads=[("bank", 6)], writes=[(tag, "sd")])
                P.op("act", lambda e: e.activation(out=w["rstd"][:], in_=w["sd"][:], func=AF.Exp, scale=-0.5),
                     reads=[(tag, "sd")], writes=[(tag, "rstd")])

            def prep_chunk(i, j):
                n = blist[i]
                ctx = n >= NT // TB
                A, B = (8, 9) if ctx else (6, 7)
                xbuf = w["xb"][i % 2]
                hT = hT2[i % 2]
                if j == 0:
                    P.op("dve", lambda e: e.tensor_tensor(out=xbuf[:], in0=xbuf[:], in1=w["rstd"][:].unsqueeze(1).to_broadcast([128, 8, TB]),
                                                          op=ALU.mult), reads=[xres(i), (tag, "rstd")], writes=[xres(i)])
                P.op("act", lambda e: e.activation(out=hT[:, j, :], in_=xbuf[:, j, :], func=AF.Identity,
                                                   scale=sc[:, A, j:j + 1], bias=sc[:, B, j:j + 1]),
                     reads=[xres(i), ("sc", A), ("sc", B)], writes=[hres(i)])

            def rope_unit(i, h, Wm, Ws, wr, wsr, dstT, dresf, tok0):
                n = blist[i]
                ctx = n >= NT // TB
                hT = hT2[i % 2]
                kb, sbk = h % 2, 2 + h % 2
                for j in range(8):
                    P.op("pe", lambda e, j=j: e.matmul(bank(kb, TB), lhsT=Wm[:, j, h * 128:(h + 1) * 128],
                                                       rhs=hT[:, j, :], start=(j == 0), stop=(j == 7)),
                         reads=[wr, hres(i)], writes=[("bank", kb)])
                if ctx:
                    P.op("act", lambda e: e.activation(out=dstT[:, h, tok0:tok0 + TB], in_=bank(kb, TB), func=AF.Copy),
                         reads=[("bank", kb)], writes=[dresf(h, tok0)])
                    return
                for j in range(8):
                    P.op("pe", lambda e, j=j: e.matmul(bank(sbk, TB), lhsT=Ws[:, j, h * 128:(h + 1) * 128],
                                                       rhs=hT[:, j, :], start=(j == 0), stop=(j == 7)),
                         reads=[wsr, hres(i)], writes=[("bank", sbk)])
                P.op("dve", lambda e: e.tensor_tensor(out=t1[:], in0=bank(kb, TB), in1=cosb[i % 2][:], op=ALU.mult),
                     reads=[("bank", kb), (tag, "cos", i % 2)], writes=[(tag, "t1")])
                P.op("dve", lambda e: e.tensor_tensor(out=t2[:], in0=bank(sbk, TB), in1=sinb[i % 2][:], op=ALU.mult),
                     reads=[("bank", sbk), (tag, "sin", i % 2)], writes=[(tag, "t2")])
                P.op("dve", lambda e: e.tensor_tensor(out=dstT[:, h, tok0:tok0 + TB], in0=t1[:], in1=t2[:], op=ALU.add),
                     reads=[(tag, "t1"), (tag, "t2")], writes=[dresf(h, tok0)])

            def v_unit(i, s):
                n = blist[i]
                hT = hT2[i % 2]
                kc = n * (TB // 128) + s
                for (c0, cn, vb) in ((0, 512, 4), (512, 256, 5)):
                    for j in range(8):
                        P.op("pe", lambda e, j=j, c0=c0, cn=cn, vb=vb: e.matmul(
                            bank(vb, cn), lhsT=hT[:, j, s * 128:(s + 1) * 128], rhs=do["Wv"][:, j, c0:c0 + cn],
                            start=(j == 0), stop=(j == 7)), reads=[(tag, "Wv"), hres(i)], writes=[("bank", vb)])
                P.op("act", lambda e: e.activation(out=do["V"][:, 0:4, kc, :], in_=bank(4, 512).rearrange("p (a b) -> p a b", a=4),
                                                   func=AF.Copy),
                     reads=[("bank", 4)], writes=[("V", hh) for hh in range(4)])
                P.op("act", lambda e: e.activation(out=do["V"][:, 4:6, kc, :], in_=bank(5, 256).rearrange("p (a b) -> p a b", a=2),
                                                   func=AF.Copy),
                     reads=[("bank", 5)], writes=[("V", 4), ("V", 5)])

            def f_unit(i, cc):
                hT = hT2[i % 2]
                fT = do["fT"]
                fb = cc
                for j in range(8):
                    P.op("pe", lambda e, j=j: e.matmul(bank(fb, TB), lhsT=do["Wf"][:, j, cc * 128:(cc + 1) * 128],
                                                       rhs=hT[:, j, :], start=(j == 0), stop=(j == 7)),
                         reads=[(tag, "Wf"), hres(i)], writes=[("bank", fb)])
                P.op("act", lambda e: e.activation(out=fT[:, cc, :], in_=bank(fb, TB), func=AF.Copy),
                     reads=[("bank", fb)], writes=[(tag, "fT", cc)])

            def g_unit(i, s):
                fT = do["fT"]; Gcs = do["Gcs"]
                gb = 2 + s
                for cc in range(2):
                    P.op("pe", lambda e, cc=cc: e.matmul(bank(gb, 512), lhsT=fT[:, cc, s * 128:(s + 1) * 128],
                                                         rhs=bcs_t[:, cc, :], start=(cc == 0), stop=(cc == 1)),
                         reads=[(tag, "fT", cc), "bcs"], writes=[("bank", gb)])
                P.op("act", lambda e: e.activation(out=Gcs[:, s, :], in_=bank(gb, 512), func=AF.Copy),
                     reads=[("bank", gb)], writes=[(tag, "Gcs", s)])

            def z_unit(i, cc, kf):
                Gcs = do["Gcs"]; tabs = do["tabs"]
                zb = (4, 5, 0, 1, 2, 3)[(cc * 4 + kf) % 6]
                k = 0
                for s in range(TB // 128):
                    for cs in range(2):
                        P.op("pe", lambda e, s=s, cs=cs, k=k: e.matmul(
                            bank(zb, 512), lhsT=Gcs[:, s, cs * 256 + cc * 128: cs * 256 + (cc + 1) * 128],
                            rhs=tabs[i % 2][:, s, cs, kf * 512:(kf + 1) * 512], start=(k == 0), stop=(k == 3)),
                            reads=[(tag, "Gcs", s), (tag, "tab", i % 2, cs)], writes=[("bank", zb)])
                        k += 1
                zsl = do["ZT"][:, cc, kf * 512:(kf + 1) * 512]
                if i == 0:
                    P.op("dve", lambda e: e.tensor_copy(out=zsl, in_=bank(zb, 512)),
                         reads=[("bank", zb)], writes=[("ZT", cc, kf)])
                else:
                    P.op("dve", lambda e: e.tensor_tensor(out=zsl, in0=bank(zb, 512), in1=zsl, op=ALU.add),
                         reads=[("bank", zb), ("ZT", cc, kf)], writes=[("ZT", cc, kf)])

            def units(i):
                n = blist[i]
                ctx = n >= NT // TB
                us = []
                if do.get("k"):
                    for h in range(6):
                        us.append(lambda h=h: rope_unit(i, h, do["Wk"], do["Wks"], (tag, "Wk"), (tag, "Wks"), do["KT"],
                                                        (lambda hh, t: ("KT", hh)), n * TB))
                    for s in range(TB // 128):
                        us.append(lambda s=s: v_unit(i, s))
                if do.get("q"):
                    for h in range(6):
                        us.append(lambda h=h: rope_unit(i, h, do["Wq"], do["Wqs"], (tag, "Wq", h), (tag, "Wqs", h), do["QT"],
                                                        (lambda hh, t: ("QT", hh, t // 512)), n * TB))
                if do.get("f"):
                    for cc in range(2):
                        us.append(lambda cc=cc: f_unit(i, cc))
                    for s in range(TB // 128):
                        us.append(lambda s=s: g_unit(i, s))
                    for cc in range(2):
                        for kf in range(4):
                            us.append(lambda cc=cc, kf=kf: z_unit(i, cc, kf))
                return us

            load(0)
            if nb > 1:
                load(1)
            prep_pre(0)
            prep_pe(0)
            prep_ab(0)
            for j in range(8):
                prep_chunk(0, j)
            for i in range(nb):
                us = units(i)
                nxt = i + 1 < nb
                if nxt:
                    prep_pre(i + 1)
                us[0]()
                if nxt:
                    prep_pe(i + 1)
                    prep_ab(i + 1)
                rest = us[1:]
                chunks = list(range(8)) if nxt else []
                per = -(-8 // len(rest))
                for u in rest:
                    u()
                    for _ in range(per):
                        if chunks:
                            prep_chunk(i + 1, chunks.pop(0))
                while chunks:
                    prep_chunk(i + 1, chunks.pop(0))
                if i + 2 < nb:
                    load(i + 2)
            return wres

        w_inv = w_in.rearrange("(j p) c -> p j c", p=128)
        w_swv = w_in_sw.rearrange("(j p) c -> p j c", p=128)

        ZTf = AR.f32(0, [128, 2, NQ])
        tab0 = arena_t[:, 16384 // 4: 16384 // 4 + 4096].bitcast(BF16).rearrange("p (s c k) -> p s c k", s=2, c=2)
        tab1 = arena_t[:, 32768 // 4: 32768 // 4 + 4096].bitcast(BF16).rearrange("p (s c k) -> p s c k", s=2, c=2)
        tabs = [tab0, tab1]
        Wf = AR.bf(49152, [128, 8, 256])
        fT = AR.bf(R_PH + 61440, [128, 2, TB])
        Gcs = AR.bf(R_PH + 62464, [128, 2, 512])
        for r in [("ZT", cc, kf) for cc in range(2) for kf in range(4)] + [("P2C", "tab", a, b) for a in range(2) for b in range(2)] + [("P2C", "Wf")]:
            P.inherit(r, W1R)
        for r in [("P2C", "fT", 0), ("P2C", "fT", 1), ("P2C", "Gcs", 0), ("P2C", "Gcs", 1)]:
            P.inherit(r, p1res)
        dma("pool", Wf[:, :, :], w_inv[:, :, 2304:2560], writes=[("P2C", "Wf")])
        resC = proj_pass("P2C", list(range(NT // TB)), p1res,
                         dict(rope=False, f=True, Wf=Wf, fT=fT, Gcs=Gcs, tabs=tabs, ZT=ZTf))
        ZTb = AR.bf(R_ZB, [128, 2, NQ])
        P.inherit("ZTb", W1R)
        for cc in range(2):
            P.op("act", lambda e, cc=cc: e.activation(out=ZTb[:, cc, :], in_=ZTf[:, cc, :], func=AF.Copy),
                 reads=[("ZT", cc, kf) for kf in range(4)], writes=["ZTb"])
        resC = resC + [("P2C", "fT", 0), ("P2C", "fT", 1), ("P2C", "Gcs", 0), ("P2C", "Gcs", 1)]

        KVh = arena_t[:, 0:(6 * 17408) // 4].bitcast(BF16).rearrange("p (h x) -> p h x", h=6)
        KT = KVh[:, :, 0:NK]
        V = KVh[:, :, NK:2 * NK].rearrange("p h (k v) -> p h k v", v=128)
        KVR = [("KT", hh) for hh in range(6)] + [("V", hh) for hh in range(6)]
        QT = AR.bf(R_QT, [128, 6, NQ])
        Wk = AR.bf(R_PH + 32768, [128, 8, 768]); Wks = AR.bf(R_PH + 45056, [128, 8, 768])
        Wv = AR.bf(R_QT, [128, 8, 768])
        ZTR = [("ZT", cc, kf) for cc in range(2) for kf in range(4)] + [("P2C", "tab", a, b) for a in range(2) for b in range(2)] + [("P2C", "Wf")]
        for r in KVR:
            P.inherit(r, W1R + ZTR)
        for r in [("P2A", "Wk"), ("P2A", "Wks")]:
            P.inherit(r, p1res)
        P.inherit(("P2A", "Wv"), W1R)
        dma("pool", Wk[:, :, :], w_inv[:, :, 768:1536], writes=[("P2A", "Wk")])
        dma("pool", Wks[:, :, :], w_swv[:, :, 768:1536], writes=[("P2A", "Wks")])
        dma("pool", Wv[:, :, :], w_inv[:, :, 1536:2304], writes=[("P2A", "Wv")])
        resA = proj_pass("P2A", list(range(NK // TB)), p1res + resC,
                         dict(rope=True, k=True, Wk=Wk, Wks=Wks, Wv=Wv, KT=KT, V=V))

        Wq = AR.bf(R_PH + 32768, [128, 8, 768]); Wqs = AR.bf(R_PH + 45056, [128, 8, 768])
        for hh in range(6):
            P.inherit(("P2B", "Wq", hh), [("P2A", "Wk")])
            P.inherit(("P2B", "Wqs", hh), [("P2A", "Wks")])
        QTR = [("QT", h, qb) for h in range(6) for qb in range(4)]
        for r in QTR:
            P.inherit(r, [("P2A", "Wv")])
        for hh in range(6):
            dma("pool", Wq[:, :, hh * 128:(hh + 1) * 128], w_inv[:, :, hh * 128:(hh + 1) * 128], writes=[("P2B", "Wq", hh)])
            dma("pool", Wqs[:, :, hh * 128:(hh + 1) * 128], w_swv[:, :, hh * 128:(hh + 1) * 128], writes=[("P2B", "Wqs", hh)])
        resB = proj_pass("P2B", list(range(NQ // TB)), resA, dict(rope=True, q=True, Wq=Wq, Wqs=Wqs, QT=QT))

        Wo = AR.bf(R_PH, [128, 8, D])
        NPT = 4
        PT = [AR.bf(R_PH + 16384 + 2048 * i, [128, 1024]) for i in range(NPT)]
        PS = [AR.bf(R_PH + 24576 + 2048 * i, [128, 1024]) for i in range(2)]
        ev = [AR.f32(R_PH + 28672 + 2048 * i, [128, 512]) for i in range(6)]
        fmT = AR.bf(R_PH + 40960, [128, 2, NQ])
        prevw = resB + [("P2B", "Wq", hh) for hh in range(6)] + [("P2B", "Wqs", hh) for hh in range(6)]
        P.inherit("Wo", prevw)
        for i in range(NPT):
            P.inherit(("PT", i), prevw)
        for i in range(2):
            P.inherit(("PS", i), prevw)
        for i in range(6):
            P.inherit(("ev", i), prevw)
        for qb in range(4):
            P.inherit(("fmT", qb), prevw)
        dma("pool", Wo[:, :, :], w_out.rearrange("(j p) c -> p j c", p=128), writes=["Wo"])

        it = 0
        pending = []
        for h in range(6):
            for qb in range(4):
                q0 = qb * 512

                def qk(kc, h=h, q0=q0):
                    b0 = 2 * (kc % 2)
                    P.op("pe", lambda e: e.matmul(bank(b0), lhsT=KT[0:64, h, kc * 128:(kc + 1) * 128], rhs=QT[0:64, h, q0:q0 + 512],
                                                  start=True, stop=True), reads=[("KT", h), ("QT", h, q0 // 512)], writes=[("bank", b0)])
                    P.op("pe", lambda e: e.matmul(bank(b0 + 1), lhsT=KT[64:128, h, kc * 128:(kc + 1) * 128], rhs=QT[64:128, h, q0:q0 + 512],
                                                  start=True, stop=True, tile_position=(64, 0)), reads=[("KT", h), ("QT", h, q0 // 512)], writes=[("bank", b0 + 1)])
                    P.op("act", lambda e: e.activation(out=PT[kc % NPT][:], in_=pp[b0 // 2][:, :], func=AF.Exp, scale=0.125),
                         reads=[("bank", b0), ("bank", b0 + 1)], writes=[("PT", kc % NPT)])

                nkc = NK // 128

                def av(kc, h=h):
                    pt = PT[kc % NPT]
                    st, sp_ = (kc == 0), (kc == nkc - 1)
                    P.op("pe", lambda e: e.matmul(bank(4), lhsT=V[:, h, kc, :], rhs=pt[:, 0:512], start=st, stop=sp_),
                         reads=[("V", h), ("PT", kc % NPT)], writes=[("bank", 4)])
                    P.op("pe", lambda e: e.matmul(bank(5), lhsT=V[:, h, kc, :], rhs=pt[:, 512:1024], start=st, stop=sp_),
                         reads=[("V", h), ("PT", kc % NPT)], writes=[("bank", 5)])

                def pairadd(p):
                    a_, b_ = PT[(2 * p) % NPT], PT[(2 * p + 1) % NPT]
                    P.op("dve", lambda e: e.tensor_tensor(out=PS[p % 2][:], in0=a_[:], in1=b_[:], op=ALU.add),
                         reads=[("PT", (2 * p) % NPT), ("PT", (2 * p + 1) % NPT)], writes=[("PS", p % 2)])

                def sums(p):
                    st, sp_ = (p == 0), (p == nkc // 2 - 1)
                    P.op("pe", lambda e: e.matmul(bank(6), lhsT=ones[:], rhs=PS[p % 2][:, 0:512], start=st, stop=sp_),
                         reads=["ones", ("PS", p % 2)], writes=[("bank", 6)])
                    P.op("pe", lambda e: e.matmul(bank(7), lhsT=ones[:], rhs=PS[p % 2][:, 512:1024], start=st, stop=sp_),
                         reads=["ones", ("PS", p % 2)], writes=[("bank", 7)])

                qk(0)
                qk(1)
                for kc in range(nkc):
                    if kc + 2 < nkc:
                        qk(kc + 2)
                    av(kc)
                    if kc % 2 == 1:
                        pairadd(kc // 2)
                        if pending:
                            pending.pop(0)()
                        if kc >= 3:
                            sums(kc // 2 - 1)
                sums(nkc // 2 - 1)
                while pending:
                    pending.pop(0)()
                c0, c1, c2, c3 = ev[0], ev[1], ev[2], ev[3]
                P.op("dve", lambda e, c0=c0: e.tensor_copy(out=c0[:], in_=bank(4)), reads=[("bank", 4)], writes=[("ev", 0)])
                P.op("dve", lambda e, c2=c2: e.tensor_copy(out=c2[:], in_=bank(6)), reads=[("bank", 6)], writes=[("ev", 2)])
                P.op("dve", lambda e, c1=c1: e.tensor_copy(out=c1[:], in_=bank(5)), reads=[("bank", 5)], writes=[("ev", 1)])
                P.op("dve", lambda e, c3=c3: e.tensor_copy(out=c3[:], in_=bank(7)), reads=[("bank", 7)], writes=[("ev", 3)])
                def mk_pending(h=h, qb=qb, q0=q0, c0=c0, c1=c1, c2=c2, c3=c3):
                    out = []
                    for (cs, ci) in ((c2, 2), (c3, 3)):
                        for qq in range(4):
                            out.append(lambda cs=cs, ci=ci, qq=qq: P.op(
                                "dve", lambda e: e.reciprocal(out=cs[:, qq * 128:(qq + 1) * 128], in_=cs[:, qq * 128:(qq + 1) * 128]),
                                reads=[("ev", ci)], writes=[("ev", ci)]))
                    out.append(lambda: P.op("dve", lambda e: e.tensor_tensor(out=c0[:], in0=c0[:], in1=c2[:], op=ALU.mult),
                                            reads=[("ev", 0), ("ev", 2)], writes=[("ev", 0)]))
                    out.append(lambda: P.op("dve", lambda e: e.tensor_tensor(out=c1[:], in0=c1[:], in1=c3[:], op=ALU.mult),
                                            reads=[("ev", 1), ("ev", 3)], writes=[("ev", 1)]))
                    out.append(lambda: P.op("dve", lambda e: e.scalar_tensor_tensor(
                        out=QT[:, h, q0:q0 + 512], in0=c1[:], scalar=neglam, in1=c0[:], op0=ALU.mult, op1=ALU.add),
                        reads=[("ev", 0), ("ev", 1), "neglam"], writes=[("QT", h, qb)]))
                    return out
                pending.extend(mk_pending())
                it += 1

        while pending:
            pending.pop(0)()

        lnb = [AR.f32(R_PH + 28672, [128, 1024]), AR.f32(R_PH + 28672 + 4096, [128, 1024])]
        xm = [AR.f32(R_PH + 49152, [128, 8, TB]), AR.f32(R_PH + 57344, [128, 8, TB])]
        P.inherit(("xm", 0), prevw)
        P.inherit(("xm", 1), prevw)
        nbm = NQ // TB

        def mload(n):
            dma("sp", xm[n % 2][:, :, :], scr[:, n * TB:(n + 1) * TB].rearrange("(j p) t -> p j t", p=128),
                reads=[("scr", n)], writes=[("xm", n % 2)])

        def subln(h, hq, sb_):
            q0 = hq * 1024
            qres = [("QT", h, 2 * hq), ("QT", h, 2 * hq + 1)]
            P.op("act", lambda e: e.activation(out=PT[sb_][:], in_=QT[:, h, q0:q0 + 1024], func=AF.Square),
                 reads=qres, writes=[("PT", sb_)])
            for t in range(2):
                bk = 2 * sb_ + t
                P.op("pe", lambda e, bk=bk, t=t: e.matmul(bank(bk), lhsT=ones[:], rhs=PT[sb_][:, t * 512:(t + 1) * 512], start=True, stop=True),
                     reads=[("PT", sb_), "ones"], writes=[("bank", bk)])
            P.op("act", lambda e: e.activation(out=lnb[sb_][:], in_=pp[sb_][:, :], func=AF.Ln,
                                               scale=1.0 / 128, bias=EPS),
                 reads=[("bank", 2 * sb_), ("bank", 2 * sb_ + 1)], writes=[("ev", 2 * sb_), ("ev", 2 * sb_ + 1)])
            P.op("act", lambda e: e.activation(out=lnb[sb_][:], in_=lnb[sb_][:], func=AF.Exp, scale=-0.5),
                 reads=[("ev", 2 * sb_), ("ev", 2 * sb_ + 1)], writes=[("ev", 2 * sb_), ("ev", 2 * sb_ + 1)])
            P.op("dve", lambda e: e.scalar_tensor_tensor(
                out=QT[:, h, q0:q0 + 1024], in0=QT[:, h, q0:q0 + 1024], scalar=sg08, in1=lnb[sb_][:], op0=ALU.mult, op1=ALU.mult),
                reads=qres + ["sg08", ("ev", 2 * sb_), ("ev", 2 * sb_ + 1)], writes=qres)

        def fm(qb):
            for oc in range(2):
                fb = 4 + oc
                for cc in range(2):
                    P.op("pe", lambda e, oc=oc, cc=cc, fb=fb: e.matmul(
                        bank(fb), lhsT=wf_t[:, cc, oc * 128:(oc + 1) * 128], rhs=ZTb[:, cc, qb * 512:(qb + 1) * 512],
                        start=(cc == 0), stop=(cc == 1)), reads=["wf", "ZTb"], writes=[("bank", fb)])
                P.op("act", lambda e, oc=oc, fb=fb: e.activation(out=fmT[:, oc, qb * 512:(qb + 1) * 512], in_=bank(fb), func=AF.Copy),
                     reads=[("bank", fb)], writes=[("fmT", qb)])

        def mix(n):
            t0 = n * TB
            for dc in range(8):
                mb = 6 + dc % 2
                for ic in range(8):
                    rhs = QT[:, ic, t0:t0 + TB] if ic < 6 else fmT[:, ic - 6, t0:t0 + TB]
                    P.op("pe", lambda e, dc=dc, ic=ic, mb=mb, rhs=rhs: e.matmul(bank(mb, TB), lhsT=Wo[:, ic, dc * 128:(dc + 1) * 128], rhs=rhs,
                                                                                start=(ic == 0), stop=(ic == 7)),
                         reads=["Wo", (("QT", ic, t0 // 512) if ic < 6 else ("fmT", t0 // 512))], writes=[("bank", mb)])
                P.op("dve", lambda e, dc=dc, mb=mb: e.scalar_tensor_tensor(
                    out=xm[n % 2][:, dc, :], in0=bank(mb, TB), scalar=sc[:, 10, dc:dc + 1], in1=xm[n % 2][:, dc, :], op0=ALU.mult, op1=ALU.add),
                    reads=[("bank", mb), ("sc", 10), ("xm", n % 2)], writes=[("xm", n % 2)])
            dma("sp", scr[:, t0:t0 + TB].rearrange("(j p) t -> p j t", p=128), xm[n % 2][:, :, :],
                reads=[("xm", n % 2)], writes=[("scr", n)])
            if n + 2 < nbm:
                mload(n + 2)

        mload(0)
        mload(1)
        itn = 0
        for h in range(6):
            subln(h, 0, itn % 2)
            itn += 1
        fm(0)
        fm(1)
        mix_after = {0: 0, 1: 1, 3: 2, 4: 3}
        for h in range(6):
            subln(h, 1, itn % 2)
            itn += 1
            if h in mix_after:
                mix(mix_after[h])
        fm(2)
        fm(3)
        for n in range(4, 8):
            mix(n)

        attn_res = KVR + ["ZTb", "Wo"] + [("fmT", qb) for qb in range(4)] + [("xm", 0), ("xm", 1)] + [("PT", i) for i in range(NPT)] + [("PS", 0), ("PS", 1)] + [("ev", i) for i in range(6)] + prevw + QTR

        Wg2 = AR.bf(0, [128, 8, DFF]); Wu2 = AR.bf(45056, [128, 8, DFF]); Wd2 = AR.bf(90112, [128, NF, D])
        w2gv = w2g.rearrange("(j p) f -> p j f", p=128)
        w2uv = w2u.rearrange("(j p) f -> p j f", p=128)
        w2dv = w2d.rearrange("(i p) d -> p i d", p=128)
        for (nm, Wt, src, base) in (("g", Wg2, w2gv, 0), ("u", Wu2, w2uv, 45056)):
            for j in range(8):
                lo, hi = base + j * 5632, base + (j + 1) * 5632 - 1
                heads = list(range(lo // 17408, min(5, hi // 17408) + 1))
                olds = [("KT", hh) for hh in heads] + [("V", hh) for hh in heads]
                if hi >= 104448:
                    olds += QTR
                P.inherit(("W2", nm, j), olds)
                dma("pool", Wt[:, j, :], src[:, j, :], writes=[("W2", nm, j)])
        def load_wd2(prs, extra):
            for pr in prs:
                olds = [("KT", 5), ("V", 5)] + (QTR if pr >= 3 else []) + (["ZTb"] if pr >= 9 else []) + extra
                P.inherit(("W2", "d", pr), olds)
                dma("pool", Wd2[:, 2 * pr:2 * pr + 2, :], w2dv[:, 2 * pr:2 * pr + 2, :], writes=[("W2", "d", pr)])
        load_wd2(range(0, 3), [])
        W2 = (Wg2, Wu2, Wd2)
        for r in [("P5", "xb", 0), ("P5", "xb", 1), ("P5", "xb", 2), ("P5", "sq2"), ("P5", "sd2"), ("P5", "rstd2"), ("P5", "sq"), ("P5", "hT"), ("P5", "sd"), ("P5", "rstd"), ("P5", "tbuf"),
                  ("P5", "sg", 0), ("P5", "sg", 1)] + [("P5", "aT", i) for i in range(NF)]:
            P.inherit(r, attn_res)
        blocks5 = []
        for n in range(NQ // TB):
            t0 = n * TB
            blocks5.append(dict(src=scr[:, t0:t0 + TB], srcres=("scr", n), dst=outT[:, t0:t0 + TB], dstres=("out", n), A=11, B=12, G=13))
        ffn_pass("P5", "W2", W2, blocks5, final=True, byj=True,
                 after_loads=lambda: load_wd2(range(3, 11), [("P5", "xb", 0), ("P5", "xb", 1), ("P5", "xb", 2)]))
        P.op("sp", lambda e: None, reads=[("out", n) for n in range(NQ // TB)])
        P.emit()
    return nc


_NC_CACHE = {}


def _host_tables():
    bf = ml_dtypes.bfloat16
    tabs = {}
    pair = np.arange(128) % 16
    part = (np.arange(128) // 16) % 2
    axis = (np.arange(128) // 32) % 2
    inv_freq = (np.float32(10000.0) ** (-(np.arange(16, dtype=np.float32)) / np.float32(16))).astype(np.float32)
    d = np.arange(64)
    ang64 = 2.0 * np.pi * ((d[:, None] * d[None, :]) % 64) / 64.0
    BC = np.zeros((256, 256), np.float64); BS = np.zeros((256, 256), np.float64)
    for g in range(4):
        BC[g * 64:(g + 1) * 64, g * 64:(g + 1) * 64] = np.cos(ang64)
        BS[g * 64:(g + 1) * 64, g * 64:(g + 1) * 64] = np.sin(ang64)
    tabs["bcs"] = np.concatenate([BC, BS], axis=1).astype(np.float32).astype(bf)
    for half in range(2):
        nloc = np.concatenate([half * NQ + np.arange(NQ), (1 - half) * NQ + np.arange(NQ)])
        row = (nloc // 64).astype(np.float32)
        col = (nloc % 64).astype(np.float32)
        pos = np.where(axis[:, None] == 0, row[None, :], col[None, :]).astype(np.float32)
        ang = (pos * inv_freq[pair][:, None]).astype(np.float32)
        cos = np.cos(ang).astype(np.float32)
        sin = np.sin(ang).astype(np.float32)
        sgn = np.where(part == 0, -1.0, 1.0).astype(np.float32)[:, None]
        tabs[("ropec", half)] = np.ascontiguousarray(cos)
        tabs[("ropes", half)] = np.ascontiguousarray(sin * sgn)
        kk = (half * NQ + np.arange(NQ)).astype(np.int64)
        prod = (nloc.astype(np.int64)[:, None] * kk[None, :]) % NT
        a = 2.0 * np.pi * prod.astype(np.float64) / NT
        tabs[("dftc", half)] = (np.cos(a) / 512.0).astype(np.float32).astype(bf)
        tabs[("dfts", half)] = (-np.sin(a) / 512.0).astype(np.float32).astype(bf)
    return tabs


def kernel(x, c, ctx, c_ctx, w_ada, b_ada, norm1_g, ffn1_w_gate, ffn1_w_up, ffn1_w_down,
           norm_mix_g, w_in, lambda_q1, lambda_k1, lambda_q2, lambda_k2, subln_g, w_fourier,
           w_out, norm2_g, ffn2_w_gate, ffn2_w_up, ffn2_w_down, final_norm_g):
    f = lambda a: np.ascontiguousarray(np.asarray(a, dtype=np.float32))
    x = f(x); c = f(c); ctx = f(ctx); c_ctx = f(c_ctx)
    if "nc" not in _NC_CACHE:
        _NC_CACHE["nc"] = build_nc()
        _NC_CACHE["tabs"] = _host_tables()
    nc = _NC_CACHE["nc"]
    tabs = _NC_CACHE["tabs"]

    def vT(v):
        return np.ascontiguousarray(f(v).reshape(8, 128).T)

    gains = np.ascontiguousarray(np.stack([vT(norm1_g[0]), vT(norm_mix_g[0]), vT(norm2_g[0]), vT(final_norm_g)], axis=1))
    b_adaT = np.ascontiguousarray(f(b_ada[0]).reshape(72, 128).T)
    w_in0 = f(w_in[0])
    cols = np.arange(1536)
    dd = cols % 64
    partner = (cols - dd) + (dd // 32) * 32 + (1 - (dd // 16) % 2) * 16 + dd % 16
    w_in_sw = np.ascontiguousarray(w_in0[:, partner])
    lamin = np.stack([f(lambda_q1[0]), f(lambda_k1[0]), f(lambda_q2[0]), f(lambda_k2[0])], axis=0)
    lamin = np.ascontiguousarray(np.broadcast_to(lamin[None], (128, 4, 64)))
    sublnT = np.ascontiguousarray(f(subln_g[0]).reshape(128, 1))
    shared = dict(
        w_ada=f(w_ada[0]), b_adaT=b_adaT, gains=gains,
        w1g=f(ffn1_w_gate[0]), w1u=f(ffn1_w_up[0]), w1d=f(ffn1_w_down[0]),
        w2g=f(ffn2_w_gate[0]), w2u=f(ffn2_w_up[0]), w2d=f(ffn2_w_down[0]),
        w_in=w_in0, w_in_sw=w_in_sw, lamin=lamin, sublnT=sublnT, w_fo=f(w_fourier[0]),
        bcs=tabs["bcs"], w_out=f(w_out[0]),
    )
    in_maps = []
    for core in range(8):
        b, half = core // 2, core % 2
        xb = x[b]
        xloc = np.concatenate([xb[half * NQ:(half + 1) * NQ], xb[(1 - half) * NQ:(2 - half) * NQ]], axis=0)
        m = dict(shared)
        m["xT"] = np.ascontiguousarray(xloc.T)
        m["ctxT"] = np.ascontiguousarray(ctx[b].T)
        m["cc2"] = np.ascontiguousarray(np.stack([vT(c[b]), vT(c_ctx)], axis=2))
        m["ropec"] = tabs[("ropec", half)]
        m["ropes"] = tabs[("ropes", half)]
        m["dftc"] = tabs[("dftc", half)]
        m["dfts"] = tabs[("dfts", half)]
        in_maps.append(m)
    res = run_bass_kernel_spmd(nc, in_maps, core_ids=list(range(8)))
    out = np.empty((4, NT, D), np.float32)
    for core in range(8):
        b, half = core // 2, core % 2
        out[b, half * NQ:(half + 1) * NQ, :] = np.asarray(res.results[core]["outT"]).T
    return out
```
